# Optimizing a Trainium2 kernel written in Bass

```python
import math
import jax, jax.numpy as jnp
from jax import lax
import numpy as np

D_MODEL = 1024
BATCH = 8
SEQ = 2048
DEPTH = 4

RMS_EPS = 1e-6
D_FF = 2816

SSD_WIDTH = D_MODEL
SSD_HEADDIM = 64
SSD_HEADS = SSD_WIDTH // SSD_HEADDIM
SSD_GROUPS = 2
SSD_HPG = SSD_HEADS // SSD_GROUPS
SSD_STATE = 128
SSD_CONV = 5
SSD_CHUNK = 128
SSD_XBC = SSD_WIDTH + 2 * SSD_GROUPS * SSD_STATE

MLA_HEADS = 8
MLA_Q_LORA = D_MODEL // 4
MLA_KV_LORA = D_MODEL // 8
MLA_NOPE = 64
MLA_ROPE = 32
MLA_QK = MLA_NOPE + MLA_ROPE
MLA_V = 64
MLA_WIDTH = MLA_HEADS * MLA_V
ROPE_BASE = 10000.0
ATTN_BLOCK = 128

CONV_WIDTH = D_MODEL // 2
CONV_GROUPS = 8
CONV_K = 3

D_MIX = SSD_WIDTH + MLA_WIDTH + CONV_WIDTH
IN_SIZES = (SSD_WIDTH, SSD_XBC, 2 * SSD_HEADS, MLA_Q_LORA, MLA_KV_LORA, MLA_ROPE, CONV_WIDTH, CONV_WIDTH, CONV_WIDTH)
D_IN = SSD_WIDTH + SSD_XBC + 2 * SSD_HEADS + MLA_Q_LORA + MLA_KV_LORA + MLA_ROPE + 3 * CONV_WIDTH

kernel_name = 'hybrid_ssd_mla_conv_macaron_encoder'


def rms_norm(x, g):
    xf = x.astype(jnp.float32)
    y = xf * lax.rsqrt(jnp.mean(xf * xf, axis=-1, keepdims=True) + RMS_EPS)
    return (y * g.astype(jnp.float32)).astype(x.dtype)


def group_rms_norm(x, g, n_groups):
    shp = x.shape
    xg = x.reshape(shp[:-1] + (n_groups, shp[-1] // n_groups))
    return rms_norm(xg, g.reshape(n_groups, -1)).reshape(shp)


def swiglu(x, w_gate, w_up, w_down):
    return (jax.nn.silu(x @ w_gate) * (x @ w_up)) @ w_down


def depthwise_conv(x, w):
    k, c = w.shape
    return lax.conv_general_dilated(x, w[:, None, :].astype(x.dtype), window_strides=(1,),
                                    padding=[(k // 2, k // 2)],
                                    dimension_numbers=('NWC', 'WIO', 'NWC'),
                                    feature_group_count=c)


def rope(x, pos):
    half = x.shape[-1] // 2
    inv = ROPE_BASE ** (-jnp.arange(half, dtype=jnp.float32) / half)
    ang = pos.astype(jnp.float32)[..., None] * inv
    cos = jnp.cos(ang)[:, :, None, :]
    sin = jnp.sin(ang)[:, :, None, :]
    xf = x.astype(jnp.float32)
    x1, x2 = xf[..., :half], xf[..., half:]
    return jnp.concatenate([x1 * cos - x2 * sin, x1 * sin + x2 * cos], axis=-1).astype(x.dtype)


def split_points(sizes):
    pts, acc = [], 0
    for n in sizes[:-1]:
        acc += n
        pts.append(acc)
    return pts


def ssd_chunked(x, dt, a, bm, cm):
    bsz, s = x.shape[0], x.shape[1]
    nc, L = s // SSD_CHUNK, SSD_CHUNK
    xc = x.reshape(bsz, nc, L, SSD_GROUPS, SSD_HPG, SSD_HEADDIM)
    dtc = dt.reshape(bsz, nc, L, SSD_GROUPS, SSD_HPG)
    bc = bm.reshape(bsz, nc, L, SSD_GROUPS, SSD_STATE)
    cc = cm.reshape(bsz, nc, L, SSD_GROUPS, SSD_STATE)
    xd = xc * dtc[..., None]
    a_cs = jnp.cumsum(dtc * a, axis=2)
    seg = a_cs[:, :, :, None] - a_cs[:, :, None, :]
    lower = jnp.tril(jnp.ones((L, L), dtype=bool))[None, None, :, :, None, None]
    decay = jnp.exp(jnp.where(lower, seg, -jnp.inf))
    cb = jnp.einsum('bclgn,bcsgn->bclsg', cc, bc)
    y_diag = jnp.einsum('bclsg,bclsgr,bcsgrp->bclgrp', cb, decay, xd)
    decay_states = jnp.exp(a_cs[:, :, -1:] - a_cs)
    states = jnp.einsum('bclgn,bclgr,bclgrp->bcgrpn', bc, decay_states, xd)
    chunk_decay = jnp.exp(a_cs[:, :, -1])

    def step(carry, inp):
        st, dec = inp
        return carry * dec[..., None, None] + st, carry

    init = jnp.zeros_like(states[:, 0])
    _, prev = lax.scan(step, init, (jnp.moveaxis(states, 1, 0), jnp.moveaxis(chunk_decay, 1, 0)))
    prev = jnp.moveaxis(prev, 0, 1)
    y_off = jnp.einsum('bclgn,bcgrpn,bclgr->bclgrp', cc, prev, jnp.exp(a_cs))
    return (y_diag + y_off).reshape(bsz, s, SSD_GROUPS, SSD_HPG, SSD_HEADDIM)


def ssd_mixer(z, xbc, dt_raw, conv_w, conv_b, dt_bias, a_log, d_skip, norm_g):
    bsz, s = z.shape[0], z.shape[1]
    xbc = jax.nn.silu(depthwise_conv(xbc, conv_w) + conv_b)
    xs = xbc[..., :SSD_WIDTH].reshape(bsz, s, SSD_GROUPS, SSD_HPG, SSD_HEADDIM).astype(jnp.float32)
    bm = xbc[..., SSD_WIDTH:SSD_WIDTH + SSD_GROUPS * SSD_STATE].reshape(bsz, s, SSD_GROUPS, SSD_STATE).astype(jnp.float32)
    cm = xbc[..., SSD_WIDTH + SSD_GROUPS * SSD_STATE:].reshape(bsz, s, SSD_GROUPS, SSD_STATE).astype(jnp.float32)
    dt = jax.nn.softplus(dt_raw.astype(jnp.float32).reshape(bsz, s, 2, SSD_GROUPS, SSD_HPG)
                         + dt_bias.astype(jnp.float32).reshape(2, SSD_GROUPS, SSD_HPG))
    a = -jnp.exp(a_log.astype(jnp.float32)).reshape(2, SSD_GROUPS, SSD_HPG)
    y_fwd = ssd_chunked(xs, dt[:, :, 0], a[0], bm, cm)
    y_bwd = jnp.flip(ssd_chunked(jnp.flip(xs, 1), jnp.flip(dt[:, :, 1], 1), a[1],
                                 jnp.flip(bm, 1), jnp.flip(cm, 1)), 1)
    y = y_fwd + y_bwd + xs * d_skip.astype(jnp.float32).reshape(SSD_GROUPS, SSD_HPG)[..., None]
    y = y.reshape(bsz, s, SSD_WIDTH) * jax.nn.silu(z.astype(jnp.float32))
    return group_rms_norm(y, norm_g, SSD_GROUPS).astype(z.dtype)


def mla_mixer(q_lat, kv_lat, k_pe, positions, q_norm, w_uq, kv_norm, w_ukv,
              q_head_norm, k_head_norm, out_norm):
    bsz, s = q_lat.shape[0], q_lat.shape[1]
    q = (rms_norm(q_lat, q_norm) @ w_uq).reshape(bsz, s, MLA_HEADS, MLA_QK)
    kv = (rms_norm(kv_lat, kv_norm) @ w_ukv).reshape(bsz, s, MLA_HEADS, MLA_NOPE + MLA_V)
    k_nope, v = kv[..., :MLA_NOPE], kv[..., MLA_NOPE:]
    k = jnp.concatenate([k_nope, jnp.broadcast_to(k_pe[:, :, None, :], (bsz, s, MLA_HEADS, MLA_ROPE))], axis=-1)
    q = rms_norm(q, q_head_norm)
    k = rms_norm(k, k_head_norm)
    q = jnp.concatenate([q[..., :MLA_NOPE], rope(q[..., MLA_NOPE:], positions)], axis=-1)
    k = jnp.concatenate([k[..., :MLA_NOPE], rope(k[..., MLA_NOPE:], positions)], axis=-1)
    scale = MLA_QK ** -0.5
    nb = s // ATTN_BLOCK
    q_blocks = q.reshape(bsz, nb, ATTN_BLOCK, MLA_HEADS, MLA_QK).swapaxes(0, 1)

    def attend(qb):
        sc = jnp.einsum('bqhd,bkhd->bhqk', qb, k).astype(jnp.float32) * scale
        p = jax.nn.softmax(sc, axis=-1).astype(v.dtype)
        return jnp.einsum('bhqk,bkhd->bqhd', p, v)

    o = lax.map(attend, q_blocks).swapaxes(0, 1).reshape(bsz, s, MLA_HEADS, MLA_V)
    return rms_norm(o, out_norm.reshape(MLA_HEADS, MLA_V)).reshape(bsz, s, MLA_WIDTH)


def conv_mixer(h_in, b_gate, c_gate, conv_w, out_norm):
    y = b_gate * depthwise_conv(c_gate * h_in, conv_w)
    return group_rms_norm(y, out_norm, CONV_GROUPS)


def setup_inputs(seed: int = 0) -> dict:
    key = jax.random.key(seed)
    k = jax.random.split(key, 32)
    f32 = jnp.float32

    def w(i, shape, fan_in):
        return jax.random.normal(k[i], shape, f32) * (fan_in ** -0.5)

    def gain(i, shape):
        return 1.0 + 0.05 * jax.random.normal(k[i], shape, f32)

    u = jax.random.uniform(k[9], (DEPTH, 2, SSD_HEADS), f32)
    dt0 = jnp.exp(u * (math.log(0.1) - math.log(1e-3)) + math.log(1e-3))
    ssd_dt_bias = dt0 + jnp.log(-jnp.expm1(-dt0))
    ssd_a_log = jnp.log(jax.random.uniform(k[10], (DEPTH, 2, SSD_HEADS), f32, 1.0, 16.0))
    positions = jnp.broadcast_to(jnp.arange(SEQ, dtype=jnp.int32)[None, :], (BATCH, SEQ))
    return {
        'x': jax.random.normal(k[0], (BATCH, SEQ, D_MODEL), f32),
        'positions': positions,
        'ffn1_norm': gain(1, (DEPTH, D_MODEL)),
        'ffn1_w_gate': w(2, (DEPTH, D_MODEL, D_FF), D_MODEL),
        'ffn1_w_up': w(3, (DEPTH, D_MODEL, D_FF), D_MODEL),
        'ffn1_w_down': w(4, (DEPTH, D_FF, D_MODEL), D_FF),
        'mix_norm': gain(5, (DEPTH, D_MODEL)),
        'w_in': w(6, (DEPTH, D_MODEL, D_IN), D_MODEL),
        'ssd_conv_w': w(7, (DEPTH, SSD_CONV, SSD_XBC), SSD_CONV),
        'ssd_conv_b': 0.02 * jax.random.normal(k[8], (DEPTH, SSD_XBC), f32),
        'ssd_dt_bias': ssd_dt_bias,
        'ssd_a_log': ssd_a_log,
        'ssd_d': 1.0 + 0.1 * jax.random.normal(k[11], (DEPTH, SSD_HEADS), f32),
        'ssd_norm': gain(12, (DEPTH, SSD_WIDTH)),
        'mla_q_norm': gain(13, (DEPTH, MLA_Q_LORA)),
        'mla_w_uq': w(14, (DEPTH, MLA_Q_LORA, MLA_HEADS * MLA_QK), MLA_Q_LORA),
        'mla_kv_norm': gain(15, (DEPTH, MLA_KV_LORA)),
        'mla_w_ukv': w(16, (DEPTH, MLA_KV_LORA, MLA_HEADS * (MLA_NOPE + MLA_V)), MLA_KV_LORA),
        'mla_q_head_norm': gain(17, (DEPTH, MLA_QK)),
        'mla_k_head_norm': gain(18, (DEPTH, MLA_QK)),
        'mla_out_norm': gain(19, (DEPTH, MLA_WIDTH)),
        'conv_w': w(20, (DEPTH, CONV_K, CONV_WIDTH), CONV_K),
        'conv_out_norm': gain(21, (DEPTH, CONV_WIDTH)),
        'w_out': w(22, (DEPTH, D_MIX, D_MODEL), D_MIX),
        'ffn2_norm': gain(23, (DEPTH, D_MODEL)),
        'ffn2_w_gate': w(24, (DEPTH, D_MODEL, D_FF), D_MODEL),
        'ffn2_w_up': w(25, (DEPTH, D_MODEL, D_FF), D_MODEL),
        'ffn2_w_down': w(26, (DEPTH, D_FF, D_MODEL), D_FF),
    }


def reference(x, positions, ffn1_norm, ffn1_w_gate, ffn1_w_up, ffn1_w_down, mix_norm, w_in,
              ssd_conv_w, ssd_conv_b, ssd_dt_bias, ssd_a_log, ssd_d, ssd_norm,
              mla_q_norm, mla_w_uq, mla_kv_norm, mla_w_ukv, mla_q_head_norm, mla_k_head_norm,
              mla_out_norm, conv_w, conv_out_norm, w_out,
              ffn2_norm, ffn2_w_gate, ffn2_w_up, ffn2_w_down):
    pts = split_points(IN_SIZES)
    for l in range(DEPTH):
        x = x + 0.5 * swiglu(rms_norm(x, ffn1_norm[l]), ffn1_w_gate[l], ffn1_w_up[l], ffn1_w_down[l])
        h = rms_norm(x, mix_norm[l])
        u = h @ w_in[l]
        z, xbc, dt_raw, q_lat, kv_lat, k_pe, c_h, c_b, c_c = jnp.split(u, pts, axis=-1)
        y_ssd = ssd_mixer(z, xbc, dt_raw, ssd_conv_w[l], ssd_conv_b[l], ssd_dt_bias[l],
                          ssd_a_log[l], ssd_d[l], ssd_norm[l])
        y_mla = mla_mixer(q_lat, kv_lat, k_pe, positions, mla_q_norm[l], mla_w_uq[l], mla_kv_norm[l],
                          mla_w_ukv[l], mla_q_head_norm[l], mla_k_head_norm[l], mla_out_norm[l])
        y_conv = conv_mixer(c_h, c_b, c_c, conv_w[l], conv_out_norm[l])
        x = x + jnp.concatenate([y_ssd, y_mla, y_conv], axis=-1) @ w_out[l]
        x = x + 0.5 * swiglu(rms_norm(x, ffn2_norm[l]), ffn2_w_gate[l], ffn2_w_up[l], ffn2_w_down[l])
    return x
```

```python
import math
import numpy as np
from contextlib import ExitStack
import concourse.bass as bass
import concourse.mybir as mybir
from concourse.bass_utils import run_bass_kernel_spmd

F32 = mybir.dt.float32
BF16 = mybir.dt.bfloat16
I32 = mybir.dt.int32
AF = mybir.ActivationFunctionType
ALU = mybir.AluOpType
AX = mybir.AxisListType

D = 1024
S = 2048
NT = 16
DFF = 2816
NFF = 22
DEPTH = 4
D_IN = 4544
EPS = 1e-6
FFN_GROUPS = [(0, 6), (6, 12), (12, 17), (17, 22)]

C_Z = 0
C_X = 1024
C_B = 2048
C_C = 2304
C_DT = 2560
C_QL = 2592
C_KVL = 2848
C_KPE = 2976
C_CH = 3008
C_CB = 3520
C_CC = 4032


class R:
    __slots__ = ("name", "w", "r")

    def __init__(self, name):
        self.name = name
        self.w = None
        self.r = {}


class KB:
    def __init__(self, nc, es):
        self.nc = nc
        self.es = es
        self.eng = dict(pe=nc.tensor, act=nc.scalar, dve=nc.vector, pool=nc.gpsimd, sp=nc.sync)
        self.psem = {e: es.enter_context(nc.semaphore("p_" + e)) for e in self.eng}
        self.cnt = {e: 0 for e in self.eng}
        self.seen = {e: {} for e in self.eng}
        self.dsem = {}
        self.nwait = 0

    def _semh(self, key):
        if key[0] == "e":
            return self.psem[key[1]]
        return self.dsem[key][0]

    def wait(self, eng, dep):
        key, val = dep
        if key[0] == "e" and key[1] == eng:
            if eng in ("pe", "sp"):
                return
            if self.cnt[eng] - val >= 3:
                return
        if key[0] == "d":
            val = max(val, 16 * self.dsem[key][1])
        if self.seen[eng].get(key, 0) >= val:
            return
        self.eng[eng].wait_ge(self._semh(key), val)
        self.seen[eng][key] = val
        self.nwait += 1

    def _deps(self, eng, reads, writes):
        for r in reads:
            if r.w is not None:
                self.wait(eng, r.w)
        for w in writes:
            if w.w is not None:
                self.wait(eng, w.w)
            for k, v in w.r.items():
                self.wait(eng, (k, v))

    def _mark(self, me, reads, writes):
        k, v = me
        for r in reads:
            if r.r.get(k, 0) < v:
                r.r[k] = v
        for w in writes:
            w.w = me
            w.r = {}

    def op(self, eng, fn, reads=(), writes=()):
        self._deps(eng, reads, writes)
        ins = fn(self.eng[eng])
        self.cnt[eng] += 1
        ins.then_inc(self.psem[eng], 1)
        self._mark((("e", eng), self.cnt[eng]), reads, writes)
        return ins

    def dma(self, q, out, in_, reads=(), writes=(), semres=None, **kw):
        self._deps(q, reads, writes)
        sr = semres if semres is not None else writes[0]
        key = ("d", sr.name)
        if key not in self.dsem:
            self.dsem[key] = [self.es.enter_context(self.nc.semaphore("d_" + sr.name)), 0]
        ent = self.dsem[key]
        ent[1] += 1
        self.eng[q].dma_start(out=out, in_=in_, **kw).then_inc(ent[0], 16)
        self._mark((key, 16 * ent[1]), reads, writes)

    def barrier(self):
        for e in self.eng:
            for e2 in self.eng:
                if e2 != e and self.cnt[e2] > 0:
                    self.wait(e, (("e", e2), self.cnt[e2]))
            for key, ent in self.dsem.items():
                if ent[1] > 0:
                    self.wait(e, (key, 16 * ent[1]))

    def final_wait(self, eng="sp"):
        for key, ent in self.dsem.items():
            if ent[1] > 0:
                self.wait(eng, (key, 16 * ent[1]))
        for e2 in self.eng:
            if e2 != eng and self.cnt[e2] > 0:
                self.wait(eng, (("e", e2), self.cnt[e2]))


class Prog:
    def __init__(self, n_layers=DEPTH, stages=("ffn1", "conv", "mla", "ssd", "ffn2")):
        self.n_layers = n_layers
        self.stages = stages

    def build(self):
        nc = bass.Bass("TRN2", target_bir_lowering=False)
        self.nc = nc
        L = DEPTH

        def din(name, shape, dt=F32):
            return nc.dram_tensor(name, list(shape), dt, kind="ExternalInput").ap()

        self.d = d = {}
        d["x"] = din("x", [S, D])
        d["positions"] = din("positions", [S, 1], I32)
        d["ffn1_norm"] = din("ffn1_norm", [L, D])
        d["ffn1_w_gate"] = din("ffn1_w_gate", [L, D, DFF])
        d["ffn1_w_up"] = din("ffn1_w_up", [L, D, DFF])
        d["ffn1_w_down"] = din("ffn1_w_down", [L, DFF, D])
        d["mix_norm"] = din("mix_norm", [L, D])
        d["w_in"] = din("w_in", [L, D, D_IN])
        d["ssd_conv_w"] = din("ssd_conv_w", [L, 5, 1536])
        d["ssd_conv_b"] = din("ssd_conv_b", [L, 1536])
        d["ssd_dt_bias"] = din("ssd_dt_bias", [L, 32])
        d["ssd_a_log"] = din("ssd_a_log", [L, 32])
        d["ssd_d"] = din("ssd_d", [L, 16])
        d["ssd_norm"] = din("ssd_norm", [L, 1024])
        d["mla_q_norm"] = din("mla_q_norm", [L, 256])
        d["mla_w_uq"] = din("mla_w_uq", [L, 256, 768])
        d["mla_kv_norm"] = din("mla_kv_norm", [L, 128])
        d["mla_w_ukv"] = din("mla_w_ukv", [L, 128, 1024])
        d["mla_q_head_norm"] = din("mla_q_head_norm", [L, 96])
        d["mla_k_head_norm"] = din("mla_k_head_norm", [L, 96])
        d["mla_out_norm"] = din("mla_out_norm", [L, 512])
        d["conv_w"] = din("conv_w", [L, 3, 512])
        d["conv_out_norm"] = din("conv_out_norm", [L, 512])
        d["w_out"] = din("w_out", [L, 2048, D])
        d["ffn2_norm"] = din("ffn2_norm", [L, D])
        d["ffn2_w_gate"] = din("ffn2_w_gate", [L, D, DFF])
        d["ffn2_w_up"] = din("ffn2_w_up", [L, D, DFF])
        d["ffn2_w_down"] = din("ffn2_w_down", [L, DFF, D])
        self.out = nc.dram_tensor("out", [S, D], F32, kind="ExternalOutput").ap()

        with ExitStack() as es:
            self.es = es
            K = self.K = KB(nc, es)

            def sb(name, shape, dt):
                return es.enter_context(nc.sbuf_tensor(name, list(shape), dt))

            self.X = sb("X", [128, NT, D], F32)
            self.RX = [R(f"X{t}") for t in range(NT)]
            self.RXs = R("Xsem")
            self.Rxsps = R("xspsem")
            self.hT = sb("hT", [128, 8, S], BF16)
            self.RhT = [R(f"hT{b}") for b in range(4)]
            self.WS = sb("WS", [128, 36864], BF16)
            self.EX = sb("EX", [128, 14336], BF16)
            self.ident = sb("ident", [128, 128], BF16)
            self.identf = sb("identf", [128, 128], F32)
            self.grow = sb("grow", [128, D], F32)
            self.Rgrow = R("grow")
            self.ss = sb("ss", [128, 2 * NT], F32)
            self.SV = sb("SV", [128, 512], F32)
            self.RSV = R("SV")
            self.BD = sb("BD", [128, 128], BF16)
            self.xsp = nc.dram_tensor("xspill", [S, D], F32, kind="Internal").ap()
            self.Rxsp = [R(f"xsp{t}") for t in range(NT)]
            self.ymT = self.X[:].rearrange("p t c -> p (t c)").bitcast(BF16).rearrange("p (j s) -> p j s", j=16)
            self.RymT = [R(f"ymT{j}") for j in range(16)]
            self.Rss = R("ss")
            self.junk = self.EX[:, 9216:10240]
            self.Rjunk = R("junk")
            self.ps = [es.enter_context(nc.psum_tensor(f"ps{i}", [128, 512], F32)) for i in range(8)]
            self.Rps = [R(f"ps{i}") for i in range(8)]
            self.Rconst = R("const")

            K.op("pool", lambda e: e.memset(self.ident[:], 0.0), writes=[self.Rconst])
            K.op("pool", lambda e: e.affine_select(out=self.ident[:], in_=self.ident[:], pattern=[[-1, 128]],
                                                   compare_op=ALU.not_equal, fill=1.0, base=0, channel_multiplier=1),
                 writes=[self.Rconst])
            K.op("pool", lambda e: e.memset(self.identf[:], 0.0), writes=[self.Rconst])
            K.op("pool", lambda e: e.affine_select(out=self.identf[:], in_=self.identf[:], pattern=[[-1, 128]],
                                                   compare_op=ALU.not_equal, fill=1.0, base=0, channel_multiplier=1),
                 writes=[self.Rconst])

            K.op("pool", lambda e: e.memset(self.BD[:], 0.0), writes=[self.Rconst])
            K.op("pool", lambda e: e.memset(self.BD[0:64, 0:64], 1.0), writes=[self.Rconst])
            K.op("pool", lambda e: e.memset(self.BD[64:128, 64:128], 1.0), writes=[self.Rconst])
            self.cs = sb("cs", [128, 2, NT, 16], F32)
            self.tri = sb("tri", [128, 128], F32)
            self.triT = sb("triT", [128, 128], F32)
            self.onesf = sb("onesf", [128, 128], F32)
            for tt_, st_, cm_ in ((self.tri, 1, -1), (self.triT, -1, 1)):
                K.op("pool", lambda e, tt_=tt_: e.memset(tt_[:], 1.0), writes=[self.Rconst])
                K.op("pool", lambda e, tt_=tt_, st_=st_, cm_=cm_: e.affine_select(out=tt_[:], in_=tt_[:], pattern=[[st_, 128]], compare_op=ALU.is_ge, fill=0.0,
                                                                                 base=0, channel_multiplier=cm_), writes=[self.Rconst])
            K.op("pool", lambda e: e.memset(self.onesf[:], 1.0), writes=[self.Rconst])
            self.Rcs = R("cs")
            self.rope_setup()
            xv = d["x"].rearrange("(t p) c -> p t c", p=128)
            for t in range(NT):
                K.dma("sp", out=self.X[:, t, :], in_=xv[:, t, :], writes=[self.RX[t]], semres=self.RXs)

            for l in range(self.n_layers):
                if "ffn1" in self.stages:
                    self.norm_stage(d["ffn1_norm"][l])
                    self.ffn_stage(d["ffn1_w_gate"][l], d["ffn1_w_up"][l], d["ffn1_w_down"][l])
                    K.barrier()
                mix = [s for s in self.stages if s in ("conv", "mla", "ssd")]
                if mix:
                    self.norm_stage(d["mix_norm"][l])
                    self.mixer_begin(l)
                    K.barrier()
                    if "ssd" in mix:
                        self.ssd_stage(l)
                        K.barrier()
                    if "conv" in mix:
                        self.conv_stage(l)
                        K.barrier()
                    if "mla" in mix:
                        self.mla_stage(l)
                        K.barrier()
                    self.mixer_end(l, mix)
                    K.barrier()
                if "ffn2" in self.stages:
                    self.norm_stage(d["ffn2_norm"][l])
                    self.ffn_stage(d["ffn2_w_gate"][l], d["ffn2_w_up"][l], d["ffn2_w_down"][l])
                    K.barrier()

            ov = self.out.rearrange("(t p) c -> p t c", p=128)
            Rout = R("out")
            for t in range(NT):
                K.dma("sp", out=ov[:, t, :], in_=self.X[:, t, :], reads=[self.RX[t]], writes=[Rout])
            K.final_wait("sp")
        return nc

    def wsv(self, off, n, dt=BF16, base=None):
        base = self.WS if base is None else base
        if dt == F32:
            return base[:, off:off + 2 * n].bitcast(F32)
        return base[:, off:off + n]

    def norm_stage(self, gain):
        K = self.K
        X, hT = self.X, self.hT
        K.dma("sp", out=self.grow[:], in_=gain.partition_broadcast(128), writes=[self.Rgrow])
        for t in range(NT):
            K.op("act", lambda e, t=t: e.activation(out=self.junk, in_=X[:, t, :], func=AF.Square,
                                                    accum_out=self.ss[:, t:t + 1]),
                 reads=[self.RX[t]], writes=[self.Rss])
        K.op("act", lambda e: e.activation(out=self.ss[:, NT:2 * NT], in_=self.ss[:, 0:NT], func=AF.Sqrt,
                                           scale=1.0 / D, bias=EPS),
             reads=[self.Rss], writes=[self.Rss])
        K.op("dve", lambda e: e.reciprocal(out=self.ss[:, NT:2 * NT], in_=self.ss[:, NT:2 * NT]),
             reads=[self.Rss], writes=[self.Rss])
        xs = [self.wsv(7168, D, BF16, self.EX), self.wsv(8192, D, BF16, self.EX)]
        Rxs = [R("xs0"), R("xs1")]
        for t in range(NT):
            j = t % 2
            K.op("dve", lambda e, t=t, j=j: e.scalar_tensor_tensor(out=xs[j], in0=X[:, t, :],
                                                                   scalar=self.ss[:, NT + t:NT + t + 1],
                                                                   in1=self.grow[:], op0=ALU.mult, op1=ALU.mult),
                 reads=[self.RX[t], self.Rss, self.Rgrow], writes=[Rxs[j]])
            bank = t % 2
            pb = self.ps[bank][:].bitcast(BF16).rearrange("p (c k) -> p c k", c=8)
            for c in range(8):
                K.op("pe", lambda e, c=c, j=j, pb=pb: e.transpose(out=pb[:, c, :], in_=xs[j][:, c * 128:(c + 1) * 128],
                                                                  identity=self.ident[:]),
                     reads=[Rxs[j], self.Rconst], writes=[self.Rps[bank]])
            K.op("act", lambda e, t=t, pb=pb: e.activation(out=hT[:, :, t * 128:(t + 1) * 128], in_=pb, func=AF.Copy),
                 reads=[self.Rps[bank]], writes=[self.RhT[t // 4]])

    def ffn_stage(self, Wg, Wu, Wd):
        K = self.K
        X, hT = self.X, self.hT
        SL = 18432
        WG = [self.wsv(s * SL, 6144).rearrange("p (k c) -> p k c", k=8) for s in range(2)]
        WU = [self.wsv(s * SL + 6144, 6144).rearrange("p (k c) -> p k c", k=8) for s in range(2)]
        WD = [self.wsv(s * SL + 12288, 6144).rearrange("p (f c) -> p f c", f=6) for s in range(2)]
        RW = [R("ffw0"), R("ffw1")]
        act = [self.wsv(a * 3072, 3072, BF16, self.EX).rearrange("p (f c) -> p f c", f=6) for a in range(2)]
        Ract = [R("act0"), R("act1")]
        sil = [self.wsv(6144 + a * 512, 512, BF16, self.EX) for a in range(2)]
        Rsil = [R("sil0"), R("sil1")]
        Wgv = Wg.rearrange("(k p) c -> p k c", p=128)
        Wuv = Wu.rearrange("(k p) c -> p k c", p=128)

        def load(q):
            f0, f1 = FFN_GROUPS[q]
            nf = f1 - f0
            s = q % 2
            K.dma("pool", out=WG[s][:, :, 0:nf * 128], in_=Wgv[:, :, f0 * 128:f1 * 128], writes=[RW[s]])
            K.dma("pool", out=WU[s][:, :, 0:nf * 128], in_=Wuv[:, :, f0 * 128:f1 * 128], writes=[RW[s]])
            K.dma("pool", out=WD[s][:, 0:nf, :], in_=Wd[f0 * 128:f1 * 128, :].rearrange("(f p) c -> p f c", p=128),
                  writes=[RW[s]])

        load(0)
        it = 0
        ab = 0
        for q in range(4):
            if q + 1 < 4:
                load(q + 1)
            f0, f1 = FFN_GROUPS[q]
            nf = f1 - f0
            s = q % 2
            for tb in range(4):
                for f in range(nf):
                    bg = (it % 2) * 2
                    bu = bg + 1
                    si = it % 2
                    it += 1
                    for k in range(8):
                        K.op("pe", lambda e, k=k, f=f, bg=bg: e.matmul(self.ps[bg][:], lhsT=WG[s][:, k, f * 128:(f + 1) * 128],
                                                                       rhs=hT[:, k, tb * 512:(tb + 1) * 512], start=(k == 0), stop=(k == 7)),
                             reads=[RW[s], self.RhT[tb]], writes=[self.Rps[bg]])
                    for k in range(8):
                        K.op("pe", lambda e, k=k, f=f, bu=bu: e.matmul(self.ps[bu][:], lhsT=WU[s][:, k, f * 128:(f + 1) * 128],
                                                                       rhs=hT[:, k, tb * 512:(tb + 1) * 512], start=(k == 0), stop=(k == 7)),
                             reads=[RW[s], self.RhT[tb]], writes=[self.Rps[bu]])
                    K.op("act", lambda e, bg=bg, si=si: e.activation(out=sil[si], in_=self.ps[bg][:], func=AF.Silu),
                         reads=[self.Rps[bg]], writes=[Rsil[si]])
                    K.op("dve", lambda e, bu=bu, si=si, f=f: e.tensor_tensor(out=act[ab][:, f, :], in0=self.ps[bu][:], in1=sil[si], op=ALU.mult),
                         reads=[self.Rps[bu], Rsil[si]], writes=[Ract[ab]])
                for tt in range(4):
                    t = tb * 4 + tt
                    for half in range(2):
                        bo = 4 + (t % 2) * 2 + half
                        for f in range(nf):
                            K.op("pe", lambda e, f=f, bo=bo, tt=tt, half=half: e.matmul(
                                self.ps[bo][:], lhsT=act[ab][:, f, tt * 128:(tt + 1) * 128],
                                rhs=WD[s][:, f, half * 512:(half + 1) * 512], start=(f == 0), stop=(f == nf - 1)),
                                 reads=[Ract[ab], RW[s]], writes=[self.Rps[bo]])
                        K.op("dve", lambda e, bo=bo, t=t, half=half: e.scalar_tensor_tensor(
                            out=X[:, t, half * 512:(half + 1) * 512], in0=self.ps[bo][:], scalar=0.5,
                            in1=X[:, t, half * 512:(half + 1) * 512], op0=ALU.mult, op1=ALU.add),
                             reads=[self.Rps[bo], self.RX[t]], writes=[self.RX[t]])
                ab ^= 1

    def mixer_begin(self, l):
        K = self.K
        xv = self.xsp.rearrange("(t p) c -> p t c", p=128)
        for t in range(NT):
            K.dma("sp", out=xv[:, t, :], in_=self.X[:, t, :], reads=[self.RX[t]], writes=[self.Rxsp[t]], semres=self.Rxsps)

    def mixer_end(self, l, mix):
        K = self.K
        chunks = []
        if "ssd" in mix:
            chunks += list(range(0, 8))
        if "mla" in mix:
            chunks += list(range(8, 12))
        if "conv" in mix:
            chunks += list(range(12, 16))
        wo = self.wsv(0, 16384).rearrange("p (j c) -> p j c", j=16)
        Rwo = R("wo")
        wov = self.d["w_out"][l].rearrange("(j p) c -> p j c", p=128)
        for j0 in range(0, 16, 4):
            K.dma("pool", out=wo[:, j0:j0 + 4, :], in_=wov[:, j0:j0 + 4, :], writes=[Rwo])
        xt = [self.wsv(a * 2048, 1024, F32, self.EX) for a in range(2)]
        Rxt = [R("xt0"), R("xt1")]
        xv = self.xsp.rearrange("(t p) c -> p t c", p=128)
        for t in range(NT):
            a = t % 2
            K.dma("sp", out=xt[a], in_=xv[:, t, :], reads=[self.Rxsp[t]], writes=[Rxt[a]])
            for half in range(2):
                bo = (t % 2) * 2 + half
                for i, j in enumerate(chunks):
                    K.op("pe", lambda e, j=j, bo=bo, half=half, i=i: e.matmul(
                        self.ps[bo][:], lhsT=self.ymT[:, j, t * 128:(t + 1) * 128],
                        rhs=wo[:, j, half * 512:(half + 1) * 512], start=(i == 0), stop=(i == len(chunks) - 1)),
                         reads=[self.RymT[j], Rwo], writes=[self.Rps[bo]])
                K.op("dve", lambda e, bo=bo, a=a, half=half: e.tensor_tensor(
                    out=xt[a][:, half * 512:(half + 1) * 512], in0=self.ps[bo][:],
                    in1=xt[a][:, half * 512:(half + 1) * 512], op=ALU.add),
                     reads=[self.Rps[bo], Rxt[a]], writes=[Rxt[a]])
            K.dma("sp", out=xv[:, t, :], in_=xt[a], reads=[Rxt[a]], writes=[self.Rxsp[t]], semres=self.Rxsps)
        K.barrier()
        for t in range(NT):
            K.dma("sp", out=self.X[:, t, :], in_=xv[:, t, :], reads=[self.Rxsp[t]], writes=[self.RX[t]], semres=self.RXs)

    def conv_stage(self, l):
        K = self.K
        d = self.d
        hT = self.hT
        SV = self.SV
        cw = SV[:, 0:12].rearrange("p (j k) -> p j k", j=4)
        gcol = SV[:, 12:16]
        for j in range(4):
            K.dma("sp", out=cw[:, j, :], in_=d["conv_w"][l][:, j * 128:(j + 1) * 128].rearrange("k p -> p k"), writes=[self.RSV],
                  allow_slow_non_contiguous=True)
        K.dma("sp", out=gcol, in_=d["conv_out_norm"][l].rearrange("(j p) -> p j", p=128), writes=[self.RSV],
              allow_slow_non_contiguous=True)
        Winv = d["w_in"][l].rearrange("(k p) c -> p k c", p=128)
        wc = [[self.wsv(sl * 3072 + i * 1024, 1024).rearrange("p (k c) -> p k c", k=8) for i in range(3)] for sl in range(2)]
        Rwc = [R("wc0"), R("wc1")]
        o = 6144
        m = self.wsv(o, 2052, F32); o += 4104
        cbuf = self.wsv(o, 2048, F32); o += 4096
        y = self.wsv(o, 2048, F32); o += 4096
        ysq = self.wsv(o, 2048); o += 2048
        rs = [self.wsv(o + a * 1024, 512, F32) for a in range(2)]; o += 2048
        tmp = [self.wsv(o + a * 1024, 512, F32) for a in range(2)]; o += 2048
        Rm, Rcb, Ry, Rysq = R("cm"), R("ccb"), R("cy"), R("cysq")
        Rrs = [R("crs0"), R("crs1")]
        Rtmp = [R("ctmp0"), R("ctmp1")]
        K.op("dve", lambda e: e.memset(m[:, 0:1], 0.0), writes=[Rm])
        K.op("dve", lambda e: e.memset(m[:, 2049:2052], 0.0), writes=[Rm])
        cols = (C_CH, C_CB, C_CC)
        it = 0
        for j in range(4):
            sl = j % 2
            for i in range(3):
                K.dma("pool", out=wc[sl][i], in_=Winv[:, :, cols[i] + j * 128:cols[i] + (j + 1) * 128], writes=[Rwc[sl]])
            for tb in range(4):
                b0 = 3 * (it % 2)
                a = it % 2
                it += 1
                for i in range(3):
                    for k in range(8):
                        K.op("pe", lambda e, i=i, k=k, b0=b0: e.matmul(self.ps[b0 + i][:], lhsT=wc[sl][i][:, k, :],
                                                                     rhs=hT[:, k, tb * 512:(tb + 1) * 512], start=(k == 0), stop=(k == 7)),
                             reads=[Rwc[sl], self.RhT[tb]], writes=[self.Rps[b0 + i]])
                K.op("act", lambda e, b0=b0, a=a: e.activation(out=tmp[a], in_=self.ps[b0 + 2][:], func=AF.Copy),
                     reads=[self.Rps[b0 + 2]], writes=[Rtmp[a]])
                K.op("dve", lambda e, b0=b0, a=a, tb=tb: e.tensor_tensor(out=m[:, 1 + tb * 512:1 + (tb + 1) * 512], in0=self.ps[b0][:],
                                                                       in1=tmp[a], op=ALU.mult),
                     reads=[self.Rps[b0], Rtmp[a]], writes=[Rm])
                K.op("act", lambda e, b0=b0, tb=tb: e.activation(out=cbuf[:, tb * 512:(tb + 1) * 512], in_=self.ps[b0 + 1][:], func=AF.Copy),
                     reads=[self.Rps[b0 + 1]], writes=[Rcb])
            K.op("dve", lambda e, j=j: e.tensor_scalar(out=y, in0=m[:, 0:2048], scalar1=cw[:, j, 0:1], scalar2=None, op0=ALU.mult),
                 reads=[Rm, self.RSV], writes=[Ry])
            for kk in (1, 2):
                K.op("dve", lambda e, j=j, kk=kk: e.scalar_tensor_tensor(out=y, in0=m[:, kk:kk + 2048], scalar=cw[:, j, kk:kk + 1],
                                                                        in1=y, op0=ALU.mult, op1=ALU.add),
                     reads=[Rm, self.RSV, Ry], writes=[Ry])
            K.op("dve", lambda e: e.tensor_tensor(out=y, in0=y, in1=cbuf, op=ALU.mult), reads=[Ry, Rcb], writes=[Ry])
            K.op("act", lambda e: e.activation(out=ysq, in_=y, func=AF.Square), reads=[Ry], writes=[Rysq])
            for tb in range(4):
                a = tb % 2
                bb = 6 + a
                K.op("pe", lambda e, bb=bb, tb=tb: e.matmul(self.ps[bb][:], lhsT=self.BD[:], rhs=ysq[:, tb * 512:(tb + 1) * 512], start=True, stop=True),
                     reads=[Rysq, self.Rconst], writes=[self.Rps[bb]])
                K.op("act", lambda e, bb=bb, a=a: e.activation(out=rs[a], in_=self.ps[bb][:], func=AF.Sqrt, scale=1.0 / 64, bias=EPS),
                     reads=[self.Rps[bb]], writes=[Rrs[a]])
                K.op("dve", lambda e, a=a: e.reciprocal(out=rs[a], in_=rs[a]), reads=[Rrs[a]], writes=[Rrs[a]])
                K.op("dve", lambda e, a=a, tb=tb, j=j: e.scalar_tensor_tensor(out=self.ymT[:, 12 + j, tb * 512:(tb + 1) * 512],
                                                                            in0=y[:, tb * 512:(tb + 1) * 512], scalar=gcol[:, j:j + 1],
                                                                            in1=rs[a], op0=ALU.mult, op1=ALU.mult),
                     reads=[Ry, Rrs[a], self.RSV], writes=[self.RymT[12 + j]])

    def rope_setup(self):
        K = self.K
        EX = self.EX
        posi = EX[:, 0:32].bitcast(I32)
        posf = self.wsv(32, 16, F32, EX)
        ang = self.wsv(64, 256, F32, EX).rearrange("p (t i) -> p t i", t=NT)
        nf = self.wsv(576, 256, F32, EX).rearrange("p (t i) -> p t i", t=NT)
        ni = EX[:, 1088:1600].bitcast(I32).rearrange("p (t i) -> p t i", t=NT)
        msk = self.wsv(1600, 256, F32, EX).rearrange("p (t i) -> p t i", t=NT)
        yy = self.wsv(2112, 256, F32, EX).rearrange("p (t i) -> p t i", t=NT)
        Rr = R("ropetmp")
        K.dma("sp", out=posi, in_=self.d["positions"].rearrange("(t p) o -> p (t o)", p=128), writes=[Rr],
              allow_slow_non_contiguous=True)
        K.op("dve", lambda e: e.tensor_copy(out=posf, in_=posi), reads=[Rr], writes=[Rr])
        for i in range(16):
            inv = float(10000.0 ** (-i / 16.0))
            K.op("dve", lambda e, i=i, inv=inv: e.tensor_scalar(out=ang[:, :, i], in0=posf, scalar1=inv, scalar2=None, op0=ALU.mult),
                 reads=[Rr], writes=[Rr])
        TWO_PI = 2.0 * math.pi
        C1 = 6.28125
        C2 = TWO_PI - C1
        K.op("dve", lambda e: e.tensor_scalar(out=nf, in0=ang, scalar1=1.0 / TWO_PI, scalar2=None, op0=ALU.mult), reads=[Rr], writes=[Rr])
        K.op("dve", lambda e: e.tensor_copy(out=ni, in_=nf), reads=[Rr], writes=[Rr])
        K.op("dve", lambda e: e.tensor_copy(out=nf, in_=ni), reads=[Rr], writes=[Rr])
        K.op("dve", lambda e: e.scalar_tensor_tensor(out=ang, in0=nf, scalar=-C1, in1=ang, op0=ALU.mult, op1=ALU.add), reads=[Rr], writes=[Rr])
        K.op("dve", lambda e: e.scalar_tensor_tensor(out=ang, in0=nf, scalar=-C2, in1=ang, op0=ALU.mult, op1=ALU.add), reads=[Rr], writes=[Rr])
        for which, shift in ((1, 0.0), (0, math.pi / 2)):
            K.op("dve", lambda e, shift=shift: e.tensor_scalar(out=yy, in0=ang, scalar1=shift, scalar2=None, op0=ALU.add), reads=[Rr], writes=[Rr])
            for _ in range(2):
                K.op("dve", lambda e: e.tensor_scalar(out=msk, in0=yy, scalar1=math.pi, scalar2=-TWO_PI, op0=ALU.is_gt, op1=ALU.mult), reads=[Rr], writes=[Rr])
                K.op("dve", lambda e: e.tensor_tensor(out=yy, in0=yy, in1=msk, op=ALU.add), reads=[Rr], writes=[Rr])
                K.op("dve", lambda e: e.tensor_scalar(out=msk, in0=yy, scalar1=-math.pi, scalar2=TWO_PI, op0=ALU.is_lt, op1=ALU.mult), reads=[Rr], writes=[Rr])
                K.op("dve", lambda e: e.tensor_tensor(out=yy, in0=yy, in1=msk, op=ALU.add), reads=[Rr], writes=[Rr])
            K.op("dve", lambda e: e.tensor_scalar(out=yy, in0=yy, scalar1=math.pi, scalar2=-math.pi, op0=ALU.min, op1=ALU.max), reads=[Rr], writes=[Rr])
            K.op("act", lambda e, which=which: e.activation(out=self.cs[:, which, :, :], in_=yy, func=AF.Sin), reads=[Rr], writes=[self.Rcs])
        K.barrier()

    def rstd_from_ss(self, ss_ap, n, Rs):
        K = self.K
        K.op("act", lambda e: e.activation(out=ss_ap, in_=ss_ap, func=AF.Sqrt, scale=1.0 / n, bias=EPS), reads=[Rs], writes=[Rs])
        K.op("dve", lambda e: e.reciprocal(out=ss_ap, in_=ss_ap), reads=[Rs], writes=[Rs])

    def rope(self, x, t, nh, tmp, Rx, Rt):
        K = self.K
        cosb = self.cs[:, 0, t, :].unsqueeze(1).to_broadcast([128, nh, 16])
        sinb = self.cs[:, 1, t, :].unsqueeze(1).to_broadcast([128, nh, 16])
        x1 = x[:, :, 0:16]
        x2 = x[:, :, 16:32]
        for i, (a, b) in enumerate(((x1, cosb), (x2, sinb), (x1, sinb), (x2, cosb))):
            K.op("dve", lambda e, i=i, a=a, b=b: e.tensor_tensor(out=tmp[:, i, :, :], in0=a, in1=b, op=ALU.mult),
                 reads=[Rx, self.Rcs], writes=[Rt])
        K.op("dve", lambda e: e.tensor_tensor(out=x1, in0=tmp[:, 0, :, :], in1=tmp[:, 1, :, :], op=ALU.subtract), reads=[Rt], writes=[Rx])
        K.op("dve", lambda e: e.tensor_tensor(out=x2, in0=tmp[:, 2, :, :], in1=tmp[:, 3, :, :], op=ALU.add), reads=[Rt], writes=[Rx])

    def mla_stage(self, l):
        K = self.K
        d = self.d
        hT = self.hT
        EX = self.EX
        grow = self.grow
        SV = self.SV
        Rg = self.Rgrow
        K.dma("sp", out=grow[:, 0:256], in_=d["mla_q_norm"][l].partition_broadcast(128), writes=[Rg])
        K.dma("sp", out=grow[:, 256:384], in_=d["mla_kv_norm"][l].partition_broadcast(128), writes=[Rg])
        K.dma("sp", out=grow[:, 384:480], in_=d["mla_q_head_norm"][l].partition_broadcast(128), writes=[Rg])
        K.dma("sp", out=grow[:, 480:576], in_=d["mla_k_head_norm"][l].partition_broadcast(128), writes=[Rg])
        K.dma("sp", out=SV[:, 0:512], in_=d["mla_out_norm"][l].partition_broadcast(128), writes=[self.RSV])
        K.op("dve", lambda e: e.tensor_scalar(out=grow[:, 384:480], in0=grow[:, 384:480], scalar1=float(96 ** -0.5), scalar2=None, op0=ALU.mult),
             reads=[Rg], writes=[Rg])
        gq = grow[:, 0:256]
        gkv = grow[:, 256:384]
        gqh = grow[:, 384:480]
        gkh = grow[:, 480:576]
        Winv = d["w_in"][l].rearrange("(k p) c -> p k c", p=128)
        wm = self.wsv(0, 3328).rearrange("p (k c) -> p k c", k=8)
        qnkT = self.wsv(3328, 6144).rearrange("p (j s) -> p j s", j=3)
        kpe = self.wsv(9472, 512, F32).rearrange("p (t i) -> p t i", t=NT)
        Rwm, RqnkT, Rkpe = R("wm"), R("qnkT"), R("kpe")
        K.dma("pool", out=wm, in_=Winv[:, :, C_QL:C_QL + 416], writes=[Rwm])
        qn = [self.wsv(a * 384, 384, BF16, EX) for a in range(2)]
        Rqn = [R("qn0"), R("qn1")]
        ssa = self.ss
        Rss = self.Rss
        for t in range(NT):
            a = t % 2
            b = t % 2
            for k in range(8):
                K.op("pe", lambda e, k=k, b=b, t=t: e.matmul(self.ps[b][:, 0:416], lhsT=hT[:, k, t * 128:(t + 1) * 128], rhs=wm[:, k, :],
                                                          start=(k == 0), stop=(k == 7)),
                     reads=[self.RhT[t // 4], Rwm], writes=[self.Rps[b]])
            K.op("act", lambda e, b=b, t=t: e.activation(out=self.junk[:, 0:256], in_=self.ps[b][:, 0:256], func=AF.Square, accum_out=ssa[:, 2 * a:2 * a + 1]),
                 reads=[self.Rps[b]], writes=[Rss])
            K.op("act", lambda e, b=b, t=t: e.activation(out=self.junk[:, 256:384], in_=self.ps[b][:, 256:384], func=AF.Square, accum_out=ssa[:, 2 * a + 1:2 * a + 2]),
                 reads=[self.Rps[b]], writes=[Rss])
            self.rstd_from_ss(ssa[:, 2 * a:2 * a + 1], 256, Rss)
            self.rstd_from_ss(ssa[:, 2 * a + 1:2 * a + 2], 128, Rss)
            K.op("dve", lambda e, b=b, a=a: e.scalar_tensor_tensor(out=qn[a][:, 0:256], in0=self.ps[b][:, 0:256], scalar=ssa[:, 2 * a:2 * a + 1], in1=gq,
                                                                  op0=ALU.mult, op1=ALU.mult), reads=[self.Rps[b], Rss, Rg], writes=[Rqn[a]])
            K.op("dve", lambda e, b=b, a=a: e.scalar_tensor_tensor(out=qn[a][:, 256:384], in0=self.ps[b][:, 256:384], scalar=ssa[:, 2 * a + 1:2 * a + 2], in1=gkv,
                                                                  op0=ALU.mult, op1=ALU.mult), reads=[self.Rps[b], Rss, Rg], writes=[Rqn[a]])
            K.op("act", lambda e, b=b, t=t: e.activation(out=kpe[:, t, :], in_=self.ps[b][:, 384:416], func=AF.Copy), reads=[self.Rps[b]], writes=[Rkpe])
            bt = 2 + t % 2
            pb = self.ps[bt][:].bitcast(BF16).rearrange("p (c k) -> p c k", c=8)
            for c in range(3):
                K.op("pe", lambda e, c=c, a=a, pb=pb: e.transpose(out=pb[:, c, :], in_=qn[a][:, c * 128:(c + 1) * 128], identity=self.ident[:]),
                     reads=[Rqn[a], self.Rconst], writes=[self.Rps[bt]])
            K.op("act", lambda e, t=t, pb=pb: e.activation(out=qnkT[:, :, t * 128:(t + 1) * 128], in_=pb[:, 0:3, :], func=AF.Copy),
                 reads=[self.Rps[bt]], writes=[RqnkT])
        K.barrier()
        qT = self.hT
        kT = self.wsv(10496, 16384).rearrange("p (h s) -> p h s", h=8)
        v1 = self.wsv(26880, 8320).rearrange("p (t h c) -> p t h c", t=NT, h=8)
        RqT, RkT, Rv1 = R("qT"), R("kT"), R("v1")
        wuq = self.wsv(0, 1536, BF16, EX).rearrange("p (j c) -> p j c", j=2)
        wukv = self.wsv(1536, 1024, BF16, EX)
        osb = self.wsv(2560, 8192, BF16, EX).rearrange("p (t h c) -> p t h c", t=NT, h=8)
        ET = [self.wsv(10752 + a * 512, 512, BF16, EX) for a in range(3)]
        Rwu, Rosb = R("wu"), R("osb")
        RET = [R(f"ET{a}") for a in range(3)]
        K.dma("pool", out=wuq, in_=d["mla_w_uq"][l].rearrange("(j p) c -> p j c", p=128), writes=[Rwu])
        K.dma("pool", out=wukv, in_=d["mla_w_ukv"][l], writes=[Rwu])
        K.op("pool", lambda e: e.memset(v1[:, :, :, 64:65], 1.0), writes=[Rv1])
        tq = self.wsv(0, 384, F32).rearrange("p (h c) -> p h c", h=4)
        tk = self.wsv(768, 384, F32).rearrange("p (h c) -> p h c", h=4)
        tsq = self.wsv(1536, 384, F32).rearrange("p (h c) -> p h c", h=4)
        trope = self.wsv(2304, 256, F32).rearrange("p (i h c) -> p i h c", i=4, h=4)
        qkb = self.wsv(2816, 384).rearrange("p (h c) -> p h c", h=4)
        Rtq, Rtk, Rtsq, Rtrope, Rqkb = R("tq"), R("tk"), R("tsq"), R("trope"), R("qkb")
        s4 = self.ss[:, 8:12]
        s1 = self.ss[:, 12:13]
        for t in range(NT):
            K.op("act", lambda e, t=t: e.activation(out=self.junk[:, 0:32], in_=kpe[:, t, :], func=AF.Square, accum_out=s1), reads=[Rkpe], writes=[Rss])
            for hh in range(2):
                bq = (2 * t + hh) % 2
                bk = 2 + (2 * t + hh) % 2
                for j in range(2):
                    K.op("pe", lambda e, j=j, bq=bq, hh=hh, t=t: e.matmul(self.ps[bq][:, 0:384], lhsT=qnkT[:, j, t * 128:(t + 1) * 128],
                                                                        rhs=wuq[:, j, hh * 384:(hh + 1) * 384], start=(j == 0), stop=(j == 1)),
                         reads=[RqnkT, Rwu], writes=[self.Rps[bq]])
                K.op("pe", lambda e, bk=bk, hh=hh, t=t: e.matmul(self.ps[bk][:], lhsT=qnkT[:, 2, t * 128:(t + 1) * 128],
                                                               rhs=wukv[:, hh * 512:(hh + 1) * 512], start=True, stop=True),
                     reads=[RqnkT, Rwu], writes=[self.Rps[bk]])
                psq = self.ps[bq][:, 0:384].rearrange("p (h c) -> p h c", h=4)
                pskv = self.ps[bk][:].rearrange("p (h c) -> p h c", h=4)
                K.op("act", lambda e, psq=psq: e.activation(out=tsq, in_=psq, func=AF.Square), reads=[self.Rps[bq]], writes=[Rtsq])
                K.op("dve", lambda e: e.tensor_reduce(out=s4, in_=tsq, axis=AX.X, op=ALU.add), reads=[Rtsq], writes=[Rss])
                self.rstd_from_ss(s4, 96, Rss)
                K.op("dve", lambda e, psq=psq: e.tensor_tensor(out=tq, in0=psq, in1=s4.unsqueeze(2).to_broadcast([128, 4, 96]), op=ALU.mult),
                     reads=[self.Rps[bq], Rss], writes=[Rtq])
                K.op("dve", lambda e: e.tensor_tensor(out=tq, in0=tq, in1=gqh.unsqueeze(1).to_broadcast([128, 4, 96]), op=ALU.mult),
                     reads=[Rtq, Rg], writes=[Rtq])
                self.rope(tq[:, :, 64:96], t, 4, trope, Rtq, Rtrope)
                K.op("act", lambda e: e.activation(out=qkb, in_=tq, func=AF.Copy), reads=[Rtq], writes=[Rqkb])
                bt = 4 + (2 * t + hh) % 2
                pb = self.ps[bt][:].bitcast(BF16).rearrange("p (c k) -> p c k", c=8)
                for h in range(4):
                    K.op("pe", lambda e, h=h, pb=pb: e.transpose(out=pb[0:96, h, :], in_=qkb[:, h, :], identity=self.ident[:]),
                         reads=[Rqkb, self.Rconst], writes=[self.Rps[bt]])
                K.op("act", lambda e, pb=pb, hh=hh, t=t: e.activation(out=qT[0:96, hh * 4:hh * 4 + 4, t * 128:(t + 1) * 128], in_=pb[0:96, 0:4, :], func=AF.Copy),
                     reads=[self.Rps[bt]], writes=[RqT])
                K.op("act", lambda e, pskv=pskv, hh=hh, t=t: e.activation(out=v1[:, t, hh * 4:hh * 4 + 4, 0:64], in_=pskv[:, :, 64:128], func=AF.Copy),
                     reads=[self.Rps[bk]], writes=[Rv1])
                K.op("act", lambda e, pskv=pskv: e.activation(out=tsq[:, :, 0:64], in_=pskv[:, :, 0:64], func=AF.Square), reads=[self.Rps[bk]], writes=[Rtsq])
                K.op("dve", lambda e: e.tensor_reduce(out=s4, in_=tsq[:, :, 0:64], axis=AX.X, op=ALU.add), reads=[Rtsq], writes=[Rss])
                K.op("dve", lambda e: e.tensor_scalar(out=s4, in0=s4, scalar1=s1, scalar2=None, op0=ALU.add), reads=[Rss], writes=[Rss])
                self.rstd_from_ss(s4, 96, Rss)
                K.op("dve", lambda e, pskv=pskv: e.tensor_tensor(out=tk[:, :, 0:64], in0=pskv[:, :, 0:64], in1=s4.unsqueeze(2).to_broadcast([128, 4, 64]), op=ALU.mult),
                     reads=[self.Rps[bk], Rss], writes=[Rtk])
                K.op("dve", lambda e, t=t: e.tensor_tensor(out=tk[:, :, 64:96], in0=kpe[:, t, :].unsqueeze(1).to_broadcast([128, 4, 32]),
                                                          in1=s4.unsqueeze(2).to_broadcast([128, 4, 32]), op=ALU.mult),
                     reads=[Rkpe, Rss], writes=[Rtk])
                K.op("dve", lambda e: e.tensor_tensor(out=tk, in0=tk, in1=gkh.unsqueeze(1).to_broadcast([128, 4, 96]), op=ALU.mult),
                     reads=[Rtk, Rg], writes=[Rtk])
                self.rope(tk[:, :, 64:96], t, 4, trope, Rtk, Rtrope)
                K.op("act", lambda e: e.activation(out=qkb, in_=tk, func=AF.Copy), reads=[Rtk], writes=[Rqkb])
                bt2 = 6 + (2 * t + hh) % 2
                pb2 = self.ps[bt2][:].bitcast(BF16).rearrange("p (c k) -> p c k", c=8)
                for h in range(4):
                    K.op("pe", lambda e, h=h, pb2=pb2: e.transpose(out=pb2[0:96, h, :], in_=qkb[:, h, :], identity=self.ident[:]),
                         reads=[Rqkb, self.Rconst], writes=[self.Rps[bt2]])
                K.op("act", lambda e, pb2=pb2, hh=hh, t=t: e.activation(out=kT[0:96, hh * 4:hh * 4 + 4, t * 128:(t + 1) * 128], in_=pb2[0:96, 0:4, :], func=AF.Copy),
                     reads=[self.Rps[bt2]], writes=[RkT])
        K.barrier()
        it = 0
        rinv = self.ss[:, 16:20]
        for h in range(8):
            for qb in range(4):
                for kt in range(NT):
                    sb_ = it % 3
                    eb = it % 3
                    it += 1
                    K.op("pe", lambda e, sb_=sb_, h=h, kt=kt, qb=qb: e.matmul(self.ps[sb_][:], lhsT=kT[0:96, h, kt * 128:(kt + 1) * 128],
                                                                          rhs=qT[0:96, h, qb * 512:(qb + 1) * 512], start=True, stop=True),
                         reads=[RkT, RqT], writes=[self.Rps[sb_]])
                    K.op("act", lambda e, sb_=sb_, eb=eb: e.activation(out=ET[eb], in_=self.ps[sb_][:], func=AF.Exp),
                         reads=[self.Rps[sb_]], writes=[RET[eb]])
                    for qt in range(4):
                        K.op("pe", lambda e, qt=qt, eb=eb, kt=kt, h=h: e.matmul(self.ps[4 + qt][:, 0:65], lhsT=ET[eb][:, qt * 128:(qt + 1) * 128],
                                                                             rhs=v1[:, kt, h, :], start=(kt == 0), stop=(kt == NT - 1)),
                             reads=[RET[eb], Rv1], writes=[self.Rps[4 + qt]])
                for qt in range(4):
                    t = qb * 4 + qt
                    K.op("dve", lambda e, qt=qt: e.reciprocal(out=rinv[:, qt:qt + 1], in_=self.ps[4 + qt][:, 64:65]), reads=[self.Rps[4 + qt]], writes=[Rss])
                    K.op("dve", lambda e, qt=qt, t=t, h=h: e.tensor_scalar(out=osb[:, t, h, :], in0=self.ps[4 + qt][:, 0:64], scalar1=rinv[:, qt:qt + 1],
                                                                         scalar2=None, op0=ALU.mult),
                         reads=[self.Rps[4 + qt], Rss], writes=[Rosb])
        K.barrier()
        to = self.wsv(0, 512, F32).rearrange("p (h c) -> p h c", h=8)
        ob = self.wsv(1024, 512)
        Rto, Rob = R("to"), R("ob")
        s8 = self.ss[:, 20:28]
        gout = SV[:, 0:512].rearrange("p (h c) -> p h c", h=8)
        for t in range(NT):
            K.op("act", lambda e, t=t: e.activation(out=to, in_=osb[:, t, :, :], func=AF.Square), reads=[Rosb], writes=[Rto])
            K.op("dve", lambda e: e.tensor_reduce(out=s8, in_=to, axis=AX.X, op=ALU.add), reads=[Rto], writes=[Rss])
            self.rstd_from_ss(s8, 64, Rss)
            K.op("dve", lambda e, t=t: e.tensor_tensor(out=to, in0=osb[:, t, :, :], in1=s8.unsqueeze(2).to_broadcast([128, 8, 64]), op=ALU.mult),
                 reads=[Rosb, Rss], writes=[Rto])
            K.op("dve", lambda e: e.tensor_tensor(out=ob.rearrange("p (h c) -> p h c", h=8), in0=to, in1=gout, op=ALU.mult),
                 reads=[Rto, self.RSV], writes=[Rob])
            bt = t % 2
            pb = self.ps[bt][:].bitcast(BF16).rearrange("p (c k) -> p c k", c=8)
            for c in range(4):
                K.op("pe", lambda e, c=c, pb=pb: e.transpose(out=pb[:, c, :], in_=ob[:, c * 128:(c + 1) * 128], identity=self.ident[:]),
                     reads=[Rob, self.Rconst], writes=[self.Rps[bt]])
            K.op("act", lambda e, t=t, pb=pb: e.activation(out=self.ymT[:, 8:12, t * 128:(t + 1) * 128], in_=pb[:, 0:4, :], func=AF.Copy),
                 reads=[self.Rps[bt]], writes=[self.RymT[8], self.RymT[9], self.RymT[10], self.RymT[11]])

    def ssd_stage(self, l):
        K = self.K
        d = self.d
        hT = self.hT
        EX = self.EX
        SV = self.SV
        RSV = self.RSV
        Winv = d["w_in"][l].rearrange("(k p) c -> p k c", p=128)
        dtb_row = SV[:, 0:32]
        a_row = SV[:, 32:64]
        d_row = SV[:, 64:80]
        scw = SV[:, 96:156].rearrange("p (c k) -> p c k", c=12)
        scb = SV[:, 160:172]
        ssn = SV[:, 176:177]
        K.dma("sp", out=dtb_row, in_=d["ssd_dt_bias"][l].partition_broadcast(128), writes=[RSV])
        K.dma("sp", out=a_row, in_=d["ssd_a_log"][l].partition_broadcast(128), writes=[RSV])
        K.dma("sp", out=d_row, in_=d["ssd_d"][l].partition_broadcast(128), writes=[RSV])
        for ci in range(12):
            K.dma("sp", out=scw[:, ci, :], in_=d["ssd_conv_w"][l][:, ci * 128:(ci + 1) * 128].rearrange("k p -> p k"), writes=[RSV],
                  allow_slow_non_contiguous=True)
        K.dma("sp", out=scb, in_=d["ssd_conv_b"][l].rearrange("(c p) -> p c", p=128), writes=[RSV], allow_slow_non_contiguous=True)
        K.dma("sp", out=self.grow[:], in_=d["ssd_norm"][l].partition_broadcast(128), writes=[self.Rgrow])

        def f3(off, n3=NT):
            return self.wsv(off, n3 * 32, F32, EX).rearrange("p (t c) -> p t c", t=n3)
        dt = f3(0)
        dta = f3(1024)
        Pc = f3(2048)
        Tend = f3(3072, 17)
        eoff = f3(4160)
        cd = f3(5184)
        Wdt = self.wsv(6208, 256, BF16, EX).rearrange("p (k c) -> p k c", k=8)
        Rdt = R("ssd_small")
        RWdt = R("Wdt")
        K.dma("pool", out=Wdt, in_=Winv[:, :, C_DT:C_DT + 32], writes=[RWdt])
        K.op("act", lambda e: e.activation(out=a_row, in_=a_row, func=AF.Exp), reads=[RSV], writes=[RSV])
        K.op("dve", lambda e: e.tensor_scalar(out=a_row, in0=a_row, scalar1=-1.0, scalar2=None, op0=ALU.mult), reads=[RSV], writes=[RSV])
        for t in range(NT):
            b = t % 2
            for k in range(8):
                K.op("pe", lambda e, k=k, b=b, t=t: e.matmul(self.ps[b][:, 0:32], lhsT=hT[:, k, t * 128:(t + 1) * 128], rhs=Wdt[:, k, :],
                                                          start=(k == 0), stop=(k == 7)),
                     reads=[self.RhT[t // 4], RWdt], writes=[self.Rps[b]])
            K.op("dve", lambda e, b=b, t=t: e.tensor_tensor(out=dt[:, t, :], in0=self.ps[b][:, 0:32], in1=dtb_row, op=ALU.add),
                 reads=[self.Rps[b], RSV], writes=[Rdt])
        K.op("act", lambda e: e.activation(out=dt, in_=dt, func=AF.Exp), reads=[Rdt], writes=[Rdt])
        K.op("act", lambda e: e.activation(out=dt, in_=dt, func=AF.Ln, bias=1.0, scale=1.0), reads=[Rdt], writes=[Rdt])
        K.op("dve", lambda e: e.tensor_tensor(out=dta, in0=dt, in1=a_row.unsqueeze(1).to_broadcast([128, NT, 32]), op=ALU.mult),
             reads=[Rdt, RSV], writes=[Rdt])
        K.op("dve", lambda e: e.memset(Tend[:, 0, :], 0.0), writes=[Rdt])
        for t in range(NT):
            ba = 2 + (t % 2) * 2
            bb = ba + 1
            K.op("pe", lambda e, ba=ba, t=t: e.matmul(self.ps[ba][:, 0:32], lhsT=self.tri[:], rhs=dta[:, t, :], start=True, stop=True),
                 reads=[Rdt, self.Rconst], writes=[self.Rps[ba]])
            K.op("pe", lambda e, bb=bb, t=t: e.matmul(self.ps[bb][:, 0:32], lhsT=self.onesf[:], rhs=dta[:, t, :], start=True, stop=True),
                 reads=[Rdt, self.Rconst], writes=[self.Rps[bb]])
            K.op("dve", lambda e, ba=ba, t=t: e.tensor_tensor(out=Pc[:, t, :], in0=self.ps[ba][:, 0:32], in1=Tend[:, t, :], op=ALU.add),
                 reads=[self.Rps[ba], Rdt], writes=[Rdt])
            K.op("dve", lambda e, bb=bb, t=t: e.tensor_tensor(out=Tend[:, t + 1, :], in0=self.ps[bb][:, 0:32], in1=Tend[:, t, :], op=ALU.add),
                 reads=[self.Rps[bb], Rdt], writes=[Rdt])
        K.op("dve", lambda e: e.tensor_tensor(out=eoff[:, :, 0:16], in0=Pc[:, :, 0:16], in1=Tend[:, 0:NT, 0:16], op=ALU.subtract), reads=[Rdt], writes=[Rdt])
        K.op("dve", lambda e: e.tensor_tensor(out=Pc[:, :, 16:32], in0=Pc[:, :, 16:32], in1=dta[:, :, 16:32], op=ALU.subtract), reads=[Rdt], writes=[Rdt])
        K.op("dve", lambda e: e.tensor_tensor(out=eoff[:, :, 16:32], in0=Tend[:, 1:NT + 1, 16:32], in1=Pc[:, :, 16:32], op=ALU.subtract), reads=[Rdt], writes=[Rdt])
        K.op("dve", lambda e: e.tensor_tensor(out=cd, in0=Tend[:, 1:NT + 1, :], in1=Tend[:, 0:NT, :], op=ALU.subtract), reads=[Rdt], writes=[Rdt])
        K.op("act", lambda e: e.activation(out=eoff, in_=eoff, func=AF.Exp), reads=[Rdt], writes=[Rdt])
        K.op("act", lambda e: e.activation(out=cd, in_=cd, func=AF.Exp), reads=[Rdt], writes=[Rdt])
        K.barrier()

        xs_g = self.wsv(0, 8192).rearrange("p (t c) -> p t c", t=NT)
        BT = self.wsv(8192, 2048)
        CT = self.wsv(10240, 2048)
        Btok = self.wsv(12288, 2048).rearrange("p (t c) -> p t c", t=NT)
        Wz = self.wsv(14336, 4096).rearrange("p (k c) -> p k c", k=8)
        pre = self.wsv(18432, 2052, F32)
        acc = self.wsv(22536, 2048, F32)
        xsT = self.wsv(26632, 2048)
        wch = [self.wsv(28680 + a * 1024, 1024).rearrange("p (k c) -> p k c", k=8) for a in range(2)]
        rhsP = self.wsv(18432, 1024, F32).rearrange("p (h c) -> p h c", h=8)
        t1 = self.wsv(20480, 512, F32)
        sz = self.wsv(21504, 512, F32)
        yn = self.wsv(22528, 512)
        Dbuf = self.wsv(6464, 1024, F32, EX).rearrange("p (h c) -> p h c", h=8)
        MT = self.wsv(8512, 1024, BF16, EX).rearrange("p (h c) -> p h c", h=8)
        xd = self.wsv(9536, 512, BF16, EX)
        xdw = self.wsv(10048, 512, BF16, EX)
        CBm = self.wsv(10560, 128, F32, EX)
        prev = self.wsv(10816, 512, F32, EX)
        prevb = self.wsv(11840, 512, BF16, EX)
        ysum = self.wsv(12352, 512, F32, EX)
        ybwd = self.X[:, 8:16, :].rearrange("p a (b c) -> p (a b) c", b=2)
        Rxs, RBT, RCT, RBtok, RWz = R("xs_g"), R("BT"), R("CT"), R("Btok"), R("Wz")
        Rpre, Racc, RxsT = R("pre"), R("acc"), R("xsT")
        Rwch = [R("wch0"), R("wch1")]
        RrhsP, Rt1, Rsz, Ryn = R("rhsP"), R("t1"), R("sz"), R("yn")
        RD, RMT, Rxd, Rxdw, RCBm, Rprev, Rprevb, Rysum, Rybwd = (R("Dbuf"), R("MT"), R("xd"), R("xdw"), R("CBm"), R("prev"),
                                                               R("prevb"), R("ysum"), R("ybwd"))
        v8 = lambda ap: ap.rearrange("p (h c) -> p h c", h=8)

        for g in range(2):
            K.dma("pool", out=Wz, in_=Winv[:, :, C_Z + g * 512:C_Z + (g + 1) * 512], writes=[RWz])
            K.op("dve", lambda e: e.memset(pre[:, 0:2], 0.0), writes=[Rpre])
            K.op("dve", lambda e: e.memset(pre[:, 2050:2052], 0.0), writes=[Rpre])
            chunks = [g * 4 + i for i in range(4)] + [8 + g, 10 + g]
            for n_, ci in enumerate(chunks):
                sl = n_ % 2
                K.dma("pool", out=wch[sl], in_=Winv[:, :, C_X + ci * 128:C_X + (ci + 1) * 128], writes=[Rwch[sl]])
                for tb in range(4):
                    b = tb % 2
                    for k in range(8):
                        K.op("pe", lambda e, k=k, b=b, tb=tb, sl=sl: e.matmul(self.ps[b][:], lhsT=wch[sl][:, k, :], rhs=hT[:, k, tb * 512:(tb + 1) * 512],
                                                                          start=(k == 0), stop=(k == 7)),
                             reads=[Rwch[sl], self.RhT[tb]], writes=[self.Rps[b]])
                    K.op("act", lambda e, b=b, tb=tb: e.activation(out=pre[:, 2 + tb * 512:2 + (tb + 1) * 512], in_=self.ps[b][:], func=AF.Copy),
                         reads=[self.Rps[b]], writes=[Rpre])
                K.op("dve", lambda e, ci=ci: e.tensor_scalar(out=acc, in0=pre[:, 0:2048], scalar1=scw[:, ci, 0:1], scalar2=None, op0=ALU.mult),
                     reads=[Rpre, RSV], writes=[Racc])
                for kk in range(1, 5):
                    K.op("dve", lambda e, ci=ci, kk=kk: e.scalar_tensor_tensor(out=acc, in0=pre[:, kk:kk + 2048], scalar=scw[:, ci, kk:kk + 1], in1=acc,
                                                                              op0=ALU.mult, op1=ALU.add), reads=[Rpre, RSV, Racc], writes=[Racc])
                if n_ < 4:
                    dst, Rdst = xsT, RxsT
                elif n_ == 4:
                    dst, Rdst = BT, RBT
                else:
                    dst, Rdst = CT, RCT
                K.op("act", lambda e, ci=ci, dst=dst: e.activation(out=dst, in_=acc, func=AF.Silu, bias=scb[:, ci:ci + 1]),
                     reads=[Racc, RSV], writes=[Rdst])
                if n_ <= 4:
                    for rnd in range(2):
                        bt = 2 + rnd
                        pb = self.ps[bt][:].bitcast(BF16).rearrange("p (c k) -> p c k", c=8)
                        for i in range(8):
                            tt = rnd * 8 + i
                            K.op("pe", lambda e, i=i, tt=tt, pb=pb, dst=dst: e.transpose(out=pb[:, i, :], in_=dst[:, tt * 128:(tt + 1) * 128], identity=self.ident[:]),
                                 reads=[Rdst, self.Rconst], writes=[self.Rps[bt]])
                        if n_ < 4:
                            K.op("act", lambda e, rnd=rnd, pb=pb, n_=n_: e.activation(out=xs_g[:, rnd * 8:(rnd + 1) * 8, n_ * 128:(n_ + 1) * 128], in_=pb, func=AF.Copy),
                                 reads=[self.Rps[bt]], writes=[Rxs])
                        else:
                            K.op("act", lambda e, rnd=rnd, pb=pb: e.activation(out=Btok[:, rnd * 8:(rnd + 1) * 8, :], in_=pb, func=AF.Copy),
                                 reads=[self.Rps[bt]], writes=[RBtok])
            K.barrier()

            for dirn in (1, 0):
                cb = dirn * 16 + g * 8
                mask = self.triT if dirn == 1 else self.tri
                K.op("dve", lambda e: e.memset(prev, 0.0), writes=[Rprev])
                K.op("dve", lambda e: e.memset(prevb, 0.0), writes=[Rprevb])
                order = range(NT - 1, -1, -1) if dirn == 1 else range(NT)
                for t in order:
                    tok = slice(t * 128, (t + 1) * 128)
                    K.op("pe", lambda e, tok=tok: e.matmul(self.ps[0][:, 0:128], lhsT=BT[:, tok], rhs=CT[:, tok], start=True, stop=True),
                         reads=[RBT, RCT], writes=[self.Rps[0]])
                    K.op("dve", lambda e, mask=mask: e.tensor_tensor(out=CBm, in0=self.ps[0][:, 0:128], in1=mask[:], op=ALU.mult),
                         reads=[self.Rps[0], self.Rconst], writes=[RCBm])
                    K.op("dve", lambda e, t=t, cb=cb: e.tensor_tensor(out=rhsP, in0=self.identf[:].unsqueeze(1).to_broadcast([128, 8, 128]),
                                                                     in1=Pc[:, t, cb:cb + 8].unsqueeze(2).to_broadcast([128, 8, 128]), op=ALU.mult),
                         reads=[self.Rconst, Rdt], writes=[RrhsP])
                    for hb in range(2):
                        K.op("pe", lambda e, hb=hb: e.matmul(self.ps[1 + hb][:], lhsT=self.onesf[:], rhs=rhsP[:, hb * 4:(hb + 1) * 4, :], start=True, stop=True),
                             reads=[RrhsP, self.Rconst], writes=[self.Rps[1 + hb]])
                    for h in range(8):
                        src = self.ps[1 + h // 4][:].rearrange("p (h c) -> p h c", h=4)[:, h % 4, :]
                        K.op("dve", lambda e, h=h, src=src, t=t, cb=cb, dirn=dirn: e.tensor_scalar(
                            out=Dbuf[:, h, :], in0=src, scalar1=Pc[:, t, cb + h:cb + h + 1], scalar2=0.0, op0=ALU.subtract,
                            op1=(ALU.max if dirn == 1 else ALU.min)),
                             reads=[self.Rps[1 + h // 4], Rdt], writes=[RD])
                    K.op("act", lambda e, dirn=dirn: e.activation(out=Dbuf, in_=Dbuf, func=AF.Exp, scale=(-1.0 if dirn == 1 else 1.0)),
                         reads=[RD], writes=[RD])
                    K.op("dve", lambda e: e.tensor_tensor(out=MT, in0=Dbuf, in1=CBm.unsqueeze(1).to_broadcast([128, 8, 128]), op=ALU.mult),
                         reads=[RD, RCBm], writes=[RMT])
                    K.op("dve", lambda e, t=t, cb=cb: e.tensor_tensor(out=v8(xd), in0=v8(xs_g[:, t, :]),
                                                                     in1=dt[:, t, cb:cb + 8].unsqueeze(2).to_broadcast([128, 8, 64]), op=ALU.mult),
                         reads=[Rxs, Rdt], writes=[Rxd])
                    col = 0 if dirn == 1 else 127
                    K.op("dve", lambda e, col=col: e.tensor_tensor(out=v8(xdw), in0=v8(xd), in1=Dbuf[:, :, col:col + 1].to_broadcast([128, 8, 64]), op=ALU.mult),
                         reads=[Rxd, RD], writes=[Rxdw])
                    K.op("pe", lambda e, tok=tok: e.matmul(self.ps[4][:], lhsT=CT[:, tok], rhs=prevb, start=True, stop=True),
                         reads=[RCT, Rprevb], writes=[self.Rps[4]])
                    for h in range(8):
                        K.op("pe", lambda e, h=h: e.matmul(self.ps[3][:, h * 64:(h + 1) * 64], lhsT=MT[:, h, :], rhs=xd[:, h * 64:(h + 1) * 64], start=True, stop=True),
                             reads=[RMT, Rxd], writes=[self.Rps[3]])
                    K.op("pe", lambda e, t=t: e.matmul(self.ps[5][:], lhsT=Btok[:, t, :], rhs=xdw, start=True, stop=True),
                         reads=[RBtok, Rxdw], writes=[self.Rps[5]])
                    ydst = ybwd[:, t, :] if dirn == 1 else ysum
                    Rydst = Rybwd if dirn == 1 else Rysum
                    K.op("dve", lambda e, t=t, cb=cb: e.tensor_tensor(out=v8(ysum), in0=v8(self.ps[4][:]),
                                                                     in1=eoff[:, t, cb:cb + 8].unsqueeze(2).to_broadcast([128, 8, 64]), op=ALU.mult),
                         reads=[self.Rps[4], Rdt], writes=[Rysum])
                    K.op("dve", lambda e, ydst=ydst: e.tensor_tensor(out=ydst, in0=ysum, in1=self.ps[3][:], op=ALU.add),
                         reads=[Rysum, self.Rps[3]], writes=[Rydst] if dirn == 1 else [Rysum])
                    K.op("dve", lambda e, t=t, cb=cb: e.tensor_tensor(out=v8(prev), in0=v8(prev), in1=cd[:, t, cb:cb + 8].unsqueeze(2).to_broadcast([128, 8, 64]), op=ALU.mult),
                         reads=[Rprev, Rdt], writes=[Rprev])
                    K.op("dve", lambda e: e.tensor_tensor(out=prev, in0=prev, in1=self.ps[5][:], op=ALU.add), reads=[Rprev, self.Rps[5]], writes=[Rprev])
                    K.op("act", lambda e: e.activation(out=prevb, in_=prev, func=AF.Copy), reads=[Rprev], writes=[Rprevb])
                    if dirn == 0:
                        K.op("dve", lambda e, t=t, g=g: e.tensor_tensor(out=v8(t1), in0=v8(xs_g[:, t, :]),
                                                                       in1=d_row[:, g * 8:(g + 1) * 8].unsqueeze(2).to_broadcast([128, 8, 64]), op=ALU.mult),
                             reads=[Rxs, RSV], writes=[Rt1])
                        K.op("dve", lambda e: e.tensor_tensor(out=t1, in0=t1, in1=ysum, op=ALU.add), reads=[Rt1, Rysum], writes=[Rt1])
                        K.op("dve", lambda e, t=t: e.tensor_tensor(out=t1, in0=t1, in1=ybwd[:, t, :], op=ALU.add), reads=[Rt1, Rybwd], writes=[Rt1])
                        for k in range(8):
                            K.op("pe", lambda e, k=k, tok=tok: e.matmul(self.ps[6][:], lhsT=hT[:, k, tok], rhs=Wz[:, k, :], start=(k == 0), stop=(k == 7)),
                                 reads=[self.RhT[t // 4], RWz], writes=[self.Rps[6]])
                        K.op("act", lambda e: e.activation(out=sz, in_=self.ps[6][:], func=AF.Silu), reads=[self.Rps[6]], writes=[Rsz])
                        K.op("dve", lambda e: e.tensor_tensor(out=t1, in0=t1, in1=sz, op=ALU.mult), reads=[Rt1, Rsz], writes=[Rt1])
                        K.op("act", lambda e: e.activation(out=sz, in_=t1, func=AF.Square, accum_out=ssn), reads=[Rt1], writes=[Rsz, RSV])
                        self.rstd_from_ss(ssn, 512, RSV)
                        K.op("dve", lambda e, g=g: e.scalar_tensor_tensor(out=yn, in0=t1, scalar=ssn, in1=self.grow[:, g * 512:(g + 1) * 512],
                                                                         op0=ALU.mult, op1=ALU.mult), reads=[Rt1, RSV, self.Rgrow], writes=[Ryn])
                        pb = self.ps[7][:].bitcast(BF16).rearrange("p (c k) -> p c k", c=8)
                        for c in range(4):
                            K.op("pe", lambda e, c=c, pb=pb: e.transpose(out=pb[:, c, :], in_=yn[:, c * 128:(c + 1) * 128], identity=self.ident[:]),
                                 reads=[Ryn, self.Rconst], writes=[self.Rps[7]])
                        K.op("act", lambda e, pb=pb, tok=tok, g=g: e.activation(out=self.ymT[:, g * 4:g * 4 + 4, tok], in_=pb[:, 0:4, :], func=AF.Copy),
                             reads=[self.Rps[7]], writes=[self.RymT[g * 4 + c] for c in range(4)])
            K.barrier()


_INPUT_ORDER = ["x", "positions", "ffn1_norm", "ffn1_w_gate", "ffn1_w_up", "ffn1_w_down", "mix_norm", "w_in",
                "ssd_conv_w", "ssd_conv_b", "ssd_dt_bias", "ssd_a_log", "ssd_d", "ssd_norm",
                "mla_q_norm", "mla_w_uq", "mla_kv_norm", "mla_w_ukv", "mla_q_head_norm", "mla_k_head_norm",
                "mla_out_norm", "conv_w", "conv_out_norm", "w_out",
                "ffn2_norm", "ffn2_w_gate", "ffn2_w_up", "ffn2_w_down"]


def make_in_maps(inputs, cores):
    maps = []
    for b in cores:
        m = {}
        for k in _INPUT_ORDER:
            a = np.asarray(inputs[k])
            if k == "x":
                m[k] = np.ascontiguousarray(a[b])
            elif k == "positions":
                m[k] = np.ascontiguousarray(a[b].reshape(S, 1).astype(np.int32))
            elif k in ("ssd_dt_bias", "ssd_a_log"):
                m[k] = np.ascontiguousarray(a.reshape(DEPTH, 32))
            else:
                m[k] = np.ascontiguousarray(a)
        maps.append(m)
    return maps


def kernel(**inputs):
    prog = Prog()
    nc = prog.build()
    in_maps = make_in_maps(inputs, list(range(8)))
    res = run_bass_kernel_spmd(nc, in_maps, core_ids=list(range(8)))
    return np.stack([np.asarray(r["out"]).reshape(S, D) for r in res.results], axis=0).astype(np.float32)
```

```python
import math
import numpy as np
from contextlib import ExitStack
import concourse.bass as bass
import concourse.mybir as mybir
from concourse.bass_utils import run_bass_kernel_spmd

F32 = mybir.dt.float32
BF16 = mybir.dt.bfloat16
I32 = mybir.dt.int32
AF = mybir.ActivationFunctionType
ALU = mybir.AluOpType
AX = mybir.AxisListType

D = 1024
S = 2048
NT = 16
DFF = 2816
NFF = 22
DEPTH = 4
D_IN = 4544
EPS = 1e-6
FFN_GROUPS = [(0, 6), (6, 12), (12, 17), (17, 22)]

C_Z = 0
C_X = 1024
C_B = 2048
C_C = 2304
C_DT = 2560
C_QL = 2592
C_KVL = 2848
C_KPE = 2976
C_CH = 3008
C_CB = 3520
C_CC = 4032


class R:
    __slots__ = ("name", "w", "r")

    def __init__(self, name):
        self.name = name
        self.w = None
        self.r = {}


class KB:
    def __init__(self, nc, es):
        self.nc = nc
        self.es = es
        self.eng = dict(pe=nc.tensor, act=nc.scalar, dve=nc.vector, pool=nc.gpsimd, sp=nc.sync)
        self.psem = {e: es.enter_context(nc.semaphore("p_" + e)) for e in self.eng}
        self.cnt = {e: 0 for e in self.eng}
        self.seen = {e: {} for e in self.eng}
        self.dsem = {}
        self.nwait = 0

    def _semh(self, key):
        if key[0] == "e":
            return self.psem[key[1]]
        return self.dsem[key][0]

    def wait(self, eng, dep):
        key, val = dep
        if key[0] == "e" and key[1] == eng:
            if eng in ("pe", "sp"):
                return
            if self.cnt[eng] - val >= 3:
                return
        if key[0] == "d":
            val = max(val, 16 * self.dsem[key][1])
        if self.seen[eng].get(key, 0) >= val:
            return
        self.eng[eng].wait_ge(self._semh(key), val)
        self.seen[eng][key] = val
        self.nwait += 1

    def _deps(self, eng, reads, writes):
        for r in reads:
            if r.w is not None:
                self.wait(eng, r.w)
        for w in writes:
            if w.w is not None:
                self.wait(eng, w.w)
            for k, v in w.r.items():
                self.wait(eng, (k, v))

    def _mark(self, me, reads, writes):
        k, v = me
        for r in reads:
            if r.r.get(k, 0) < v:
                r.r[k] = v
        for w in writes:
            w.w = me
            w.r = {}

    def op(self, eng, fn, reads=(), writes=()):
        self._deps(eng, reads, writes)
        ins = fn(self.eng[eng])
        self.cnt[eng] += 1
        ins.then_inc(self.psem[eng], 1)
        self._mark((("e", eng), self.cnt[eng]), reads, writes)
        return ins

    def dma(self, q, out, in_, reads=(), writes=(), semres=None, **kw):
        self._deps(q, reads, writes)
        sr = semres if semres is not None else writes[0]
        key = ("d", sr.name)
        if key not in self.dsem:
            self.dsem[key] = [self.es.enter_context(self.nc.semaphore("d_" + sr.name)), 0]
        ent = self.dsem[key]
        ent[1] += 1
        self.eng[q].dma_start(out=out, in_=in_, **kw).then_inc(ent[0], 16)
        self._mark((key, 16 * ent[1]), reads, writes)

    def barrier(self):
        for e in self.eng:
            for e2 in self.eng:
                if e2 != e and self.cnt[e2] > 0:
                    self.wait(e, (("e", e2), self.cnt[e2]))
            for key, ent in self.dsem.items():
                if ent[1] > 0:
                    self.wait(e, (key, 16 * ent[1]))

    def final_wait(self, eng="sp"):
        for key, ent in self.dsem.items():
            if ent[1] > 0:
                self.wait(eng, (key, 16 * ent[1]))
        for e2 in self.eng:
            if e2 != eng and self.cnt[e2] > 0:
                self.wait(eng, (("e", e2), self.cnt[e2]))


class Prog:
    def __init__(self, n_layers=DEPTH, stages=("ffn1", "conv", "mla", "ssd", "ffn2")):
        self.n_layers = n_layers
        self.stages = stages

    def build(self):
        nc = bass.Bass("TRN2", target_bir_lowering=False)
        self.nc = nc
        L = DEPTH

        def din(name, shape, dt=F32):
            return nc.dram_tensor(name, list(shape), dt, kind="ExternalInput").ap()

        self.d = d = {}
        d["x"] = din("x", [S, D])
        d["positions"] = din("positions", [S, 1], I32)
        d["ffn1_norm"] = din("ffn1_norm", [L, D])
        d["ffn1_w_gate"] = din("ffn1_w_gate", [L, D, DFF])
        d["ffn1_w_up"] = din("ffn1_w_up", [L, D, DFF])
        d["ffn1_w_down"] = din("ffn1_w_down", [L, DFF, D])
        d["mix_norm"] = din("mix_norm", [L, D])
        d["w_in"] = din("w_in", [L, D, D_IN])
        d["ssd_conv_w"] = din("ssd_conv_w", [L, 5, 1536])
        d["ssd_conv_b"] = din("ssd_conv_b", [L, 1536])
        d["ssd_dt_bias"] = din("ssd_dt_bias", [L, 32])
        d["ssd_a_log"] = din("ssd_a_log", [L, 32])
        d["ssd_d"] = din("ssd_d", [L, 16])
        d["ssd_norm"] = din("ssd_norm", [L, 1024])
        d["mla_q_norm"] = din("mla_q_norm", [L, 256])
        d["mla_w_uq"] = din("mla_w_uq", [L, 256, 768])
        d["mla_kv_norm"] = din("mla_kv_norm", [L, 128])
        d["mla_w_ukv"] = din("mla_w_ukv", [L, 128, 1024])
        d["mla_q_head_norm"] = din("mla_q_head_norm", [L, 96])
        d["mla_k_head_norm"] = din("mla_k_head_norm", [L, 96])
        d["mla_out_norm"] = din("mla_out_norm", [L, 512])
        d["conv_w"] = din("conv_w", [L, 3, 512])
        d["conv_out_norm"] = din("conv_out_norm", [L, 512])
        d["w_out"] = din("w_out", [L, 2048, D])
        d["ffn2_norm"] = din("ffn2_norm", [L, D])
        d["ffn2_w_gate"] = din("ffn2_w_gate", [L, D, DFF])
        d["ffn2_w_up"] = din("ffn2_w_up", [L, D, DFF])
        d["ffn2_w_down"] = din("ffn2_w_down", [L, DFF, D])
        self.out = nc.dram_tensor("out", [S, D], F32, kind="ExternalOutput").ap()

        with ExitStack() as es:
            self.es = es
            K = self.K = KB(nc, es)

            def sb(name, shape, dt):
                return es.enter_context(nc.sbuf_tensor(name, list(shape), dt))

            self.X = sb("X", [128, NT, D], F32)
            self.RX = [R(f"X{t}") for t in range(NT)]
            self.RXs = R("Xsem")
            self.Rxsps = R("xspsem")
            self.hT = sb("hT", [128, 8, S], BF16)
            self.RhT = [R(f"hT{b}") for b in range(4)]
            self.WS = sb("WS", [128, 36864], BF16)
            self.EX = sb("EX", [128, 14336], BF16)
            self.ident = sb("ident", [128, 128], BF16)
            self.identf = sb("identf", [128, 128], F32)
            self.grow = sb("grow", [128, D], F32)
            self.Rgrow = R("grow")
            self.ss = sb("ss", [128, 2 * NT], F32)
            self.SV = sb("SV", [128, 512], F32)
            self.RSV = R("SV")
            self.BD = sb("BD", [128, 128], BF16)
            self.xsp = nc.dram_tensor("xspill", [S, D], F32, kind="Internal").ap()
            self.Rxsp = [R(f"xsp{t}") for t in range(NT)]
            self.ymT = self.X[:].rearrange("p t c -> p (t c)").bitcast(BF16).rearrange("p (j s) -> p j s", j=16)
            self.RymT = [R(f"ymT{j}") for j in range(16)]
            self.Rss = R("ss")
            self.junk = self.EX[:, 9216:10240]
            self.Rjunk = R("junk")
            self.ps = [es.enter_context(nc.psum_tensor(f"ps{i}", [128, 512], F32)) for i in range(8)]
            self.Rps = [R(f"ps{i}") for i in range(8)]
            self.Rconst = R("const")

            K.op("pool", lambda e: e.memset(self.ident[:], 0.0), writes=[self.Rconst])
            K.op("pool", lambda e: e.affine_select(out=self.ident[:], in_=self.ident[:], pattern=[[-1, 128]],
                                                   compare_op=ALU.not_equal, fill=1.0, base=0, channel_multiplier=1),
                 writes=[self.Rconst])
            K.op("pool", lambda e: e.memset(self.identf[:], 0.0), writes=[self.Rconst])
            K.op("pool", lambda e: e.affine_select(out=self.identf[:], in_=self.identf[:], pattern=[[-1, 128]],
                                                   compare_op=ALU.not_equal, fill=1.0, base=0, channel_multiplier=1),
                 writes=[self.Rconst])

            K.op("pool", lambda e: e.memset(self.BD[:], 0.0), writes=[self.Rconst])
            K.op("pool", lambda e: e.memset(self.BD[0:64, 0:64], 1.0), writes=[self.Rconst])
            K.op("pool", lambda e: e.memset(self.BD[64:128, 64:128], 1.0), writes=[self.Rconst])
            self.cs = sb("cs", [128, 2, NT, 16], F32)
            self.tri = sb("tri", [128, 128], F32)
            self.triT = sb("triT", [128, 128], F32)
            self.onesf = sb("onesf", [128, 128], F32)
            for tt_, st_, cm_ in ((self.tri, 1, -1), (self.triT, -1, 1)):
                K.op("pool", lambda e, tt_=tt_: e.memset(tt_[:], 1.0), writes=[self.Rconst])
                K.op("pool", lambda e, tt_=tt_, st_=st_, cm_=cm_: e.affine_select(out=tt_[:], in_=tt_[:], pattern=[[st_, 128]], compare_op=ALU.is_ge, fill=0.0,
                                                                                 base=0, channel_multiplier=cm_), writes=[self.Rconst])
            K.op("pool", lambda e: e.memset(self.onesf[:], 1.0), writes=[self.Rconst])
            self.Rcs = R("cs")
            self.rope_setup()
            xv = d["x"].rearrange("(t p) c -> p t c", p=128)
            for t in range(NT):
                K.dma("sp", out=self.X[:, t, :], in_=xv[:, t, :], writes=[self.RX[t]], semres=self.RXs)

            for l in range(self.n_layers):
                if "ffn1" in self.stages:
                    self.norm_stage(d["ffn1_norm"][l])
                    self.ffn_stage(d["ffn1_w_gate"][l], d["ffn1_w_up"][l], d["ffn1_w_down"][l])
                    K.barrier()
                mix = [s for s in self.stages if s in ("conv", "mla", "ssd")]
                if mix:
                    self.norm_stage(d["mix_norm"][l])
                    self.mixer_begin(l)
                    K.barrier()
                    if "ssd" in mix:
                        self.ssd_stage(l)
                        K.barrier()
                    if "conv" in mix:
                        self.conv_stage(l)
                        K.barrier()
                    if "mla" in mix:
                        self.mla_stage(l)
                        K.barrier()
                    self.mixer_end(l, mix)
                    K.barrier()
                if "ffn2" in self.stages:
                    self.norm_stage(d["ffn2_norm"][l])
                    self.ffn_stage(d["ffn2_w_gate"][l], d["ffn2_w_up"][l], d["ffn2_w_down"][l])
                    K.barrier()

            ov = self.out.rearrange("(t p) c -> p t c", p=128)
            Rout = R("out")
            for t in range(NT):
                K.dma("sp", out=ov[:, t, :], in_=self.X[:, t, :], reads=[self.RX[t]], writes=[Rout])
            K.final_wait("sp")
        return nc

    def wsv(self, off, n, dt=BF16, base=None):
        base = self.WS if base is None else base
        if dt == F32:
            return base[:, off:off + 2 * n].bitcast(F32)
        return base[:, off:off + n]

    def norm_stage(self, gain):
        K = self.K
        X, hT = self.X, self.hT
        K.dma("sp", out=self.grow[:], in_=gain.partition_broadcast(128), writes=[self.Rgrow])
        for t in range(NT):
            K.op("act", lambda e, t=t: e.activation(out=self.junk, in_=X[:, t, :], func=AF.Square,
                                                    accum_out=self.ss[:, t:t + 1]),
                 reads=[self.RX[t]], writes=[self.Rss])
        K.op("act", lambda e: e.activation(out=self.ss[:, NT:2 * NT], in_=self.ss[:, 0:NT], func=AF.Sqrt,
                                           scale=1.0 / D, bias=EPS),
             reads=[self.Rss], writes=[self.Rss])
        K.op("dve", lambda e: e.reciprocal(out=self.ss[:, NT:2 * NT], in_=self.ss[:, NT:2 * NT]),
             reads=[self.Rss], writes=[self.Rss])
        xs = [self.wsv(7168, D, BF16, self.EX), self.wsv(8192, D, BF16, self.EX)]
        Rxs = [R("xs0"), R("xs1")]
        for t in range(NT):
            j = t % 2
            K.op("dve", lambda e, t=t, j=j: e.scalar_tensor_tensor(out=xs[j], in0=X[:, t, :],
                                                                   scalar=self.ss[:, NT + t:NT + t + 1],
                                                                   in1=self.grow[:], op0=ALU.mult, op1=ALU.mult),
                 reads=[self.RX[t], self.Rss, self.Rgrow], writes=[Rxs[j]])
            bank = t % 2
            pb = self.ps[bank][:].bitcast(BF16).rearrange("p (c k) -> p c k", c=8)
            for c in range(8):
                K.op("pe", lambda e, c=c, j=j, pb=pb: e.transpose(out=pb[:, c, :], in_=xs[j][:, c * 128:(c + 1) * 128],
                                                                  identity=self.ident[:]),
                     reads=[Rxs[j], self.Rconst], writes=[self.Rps[bank]])
            K.op("act", lambda e, t=t, pb=pb: e.activation(out=hT[:, :, t * 128:(t + 1) * 128], in_=pb, func=AF.Copy),
                 reads=[self.Rps[bank]], writes=[self.RhT[t // 4]])

    def ffn_stage(self, Wg, Wu, Wd):
        K = self.K
        X, hT = self.X, self.hT
        SL = 18432
        WG = [self.wsv(s * SL, 6144).rearrange("p (k c) -> p k c", k=8) for s in range(2)]
        WU = [self.wsv(s * SL + 6144, 6144).rearrange("p (k c) -> p k c", k=8) for s in range(2)]
        WD = [self.wsv(s * SL + 12288, 6144).rearrange("p (f c) -> p f c", f=6) for s in range(2)]
        RW = [R("ffw0"), R("ffw1")]
        act = [self.wsv(a * 3072, 3072, BF16, self.EX).rearrange("p (f c) -> p f c", f=6) for a in range(2)]
        Ract = [R("act0"), R("act1")]
        sil = [self.wsv(6144 + a * 512, 512, BF16, self.EX) for a in range(2)]
        Rsil = [R("sil0"), R("sil1")]
        Wgv = Wg.rearrange("(k p) c -> p k c", p=128)
        Wuv = Wu.rearrange("(k p) c -> p k c", p=128)

        def load(q):
            f0, f1 = FFN_GROUPS[q]
            nf = f1 - f0
            s = q % 2
            K.dma("pool", out=WG[s][:, :, 0:nf * 128], in_=Wgv[:, :, f0 * 128:f1 * 128], writes=[RW[s]])
            K.dma("pool", out=WU[s][:, :, 0:nf * 128], in_=Wuv[:, :, f0 * 128:f1 * 128], writes=[RW[s]])
            K.dma("pool", out=WD[s][:, 0:nf, :], in_=Wd[f0 * 128:f1 * 128, :].rearrange("(f p) c -> p f c", p=128),
                  writes=[RW[s]])

        load(0)
        it = 0
        ab = 0
        for q in range(4):
            if q + 1 < 4:
                load(q + 1)
            f0, f1 = FFN_GROUPS[q]
            nf = f1 - f0
            s = q % 2
            for tb in range(4):
                for f in range(nf):
                    bg = (it % 2) * 2
                    bu = bg + 1
                    si = it % 2
                    it += 1
                    for k in range(8):
                        K.op("pe", lambda e, k=k, f=f, bg=bg: e.matmul(self.ps[bg][:], lhsT=WG[s][:, k, f * 128:(f + 1) * 128],
                                                                       rhs=hT[:, k, tb * 512:(tb + 1) * 512], start=(k == 0), stop=(k == 7)),
                             reads=[RW[s], self.RhT[tb]], writes=[self.Rps[bg]])
                    for k in range(8):
                        K.op("pe", lambda e, k=k, f=f, bu=bu: e.matmul(self.ps[bu][:], lhsT=WU[s][:, k, f * 128:(f + 1) * 128],
                                                                       rhs=hT[:, k, tb * 512:(tb + 1) * 512], start=(k == 0), stop=(k == 7)),
                             reads=[RW[s], self.RhT[tb]], writes=[self.Rps[bu]])
                    K.op("act", lambda e, bg=bg, si=si: e.activation(out=sil[si], in_=self.ps[bg][:], func=AF.Silu),
                         reads=[self.Rps[bg]], writes=[Rsil[si]])
                    K.op("dve", lambda e, bu=bu, si=si, f=f: e.tensor_tensor(out=act[ab][:, f, :], in0=self.ps[bu][:], in1=sil[si], op=ALU.mult),
                         reads=[self.Rps[bu], Rsil[si]], writes=[Ract[ab]])
                for tt in range(4):
                    t = tb * 4 + tt
                    for half in range(2):
                        bo = 4 + (t % 2) * 2 + half
                        for f in range(nf):
                            K.op("pe", lambda e, f=f, bo=bo, tt=tt, half=half: e.matmul(
                                self.ps[bo][:], lhsT=act[ab][:, f, tt * 128:(tt + 1) * 128],
                                rhs=WD[s][:, f, half * 512:(half + 1) * 512], start=(f == 0), stop=(f == nf - 1)),
                                 reads=[Ract[ab], RW[s]], writes=[self.Rps[bo]])
                        K.op("dve", lambda e, bo=bo, t=t, half=half: e.scalar_tensor_tensor(
                            out=X[:, t, half * 512:(half + 1) * 512], in0=self.ps[bo][:], scalar=0.5,
                            in1=X[:, t, half * 512:(half + 1) * 512], op0=ALU.mult, op1=ALU.add),
                             reads=[self.Rps[bo], self.RX[t]], writes=[self.RX[t]])
                ab ^= 1

    def mixer_begin(self, l):
        K = self.K
        xv = self.xsp.rearrange("(t p) c -> p t c", p=128)
        for t in range(NT):
            K.dma("sp", out=xv[:, t, :], in_=self.X[:, t, :], reads=[self.RX[t]], writes=[self.Rxsp[t]], semres=self.Rxsps)

    def mixer_end(self, l, mix):
        K = self.K
        chunks = []
        if "ssd" in mix:
            chunks += list(range(0, 8))
        if "mla" in mix:
            chunks += list(range(8, 12))
        if "conv" in mix:
            chunks += list(range(12, 16))
        wo = self.wsv(0, 16384).rearrange("p (j c) -> p j c", j=16)
        Rwo = R("wo")
        wov = self.d["w_out"][l].rearrange("(j p) c -> p j c", p=128)
        for j0 in range(0, 16, 4):
            K.dma("pool", out=wo[:, j0:j0 + 4, :], in_=wov[:, j0:j0 + 4, :], writes=[Rwo])
        xt = [self.wsv(a * 2048, 1024, F32, self.EX) for a in range(2)]
        Rxt = [R("xt0"), R("xt1")]
        xv = self.xsp.rearrange("(t p) c -> p t c", p=128)
        for t in range(NT):
            a = t % 2
            K.dma("sp", out=xt[a], in_=xv[:, t, :], reads=[self.Rxsp[t]], writes=[Rxt[a]])
            for half in range(2):
                bo = (t % 2) * 2 + half
                for i, j in enumerate(chunks):
                    K.op("pe", lambda e, j=j, bo=bo, half=half, i=i: e.matmul(
                        self.ps[bo][:], lhsT=self.ymT[:, j, t * 128:(t + 1) * 128],
                        rhs=wo[:, j, half * 512:(half + 1) * 512], start=(i == 0), stop=(i == len(chunks) - 1)),
                         reads=[self.RymT[j], Rwo], writes=[self.Rps[bo]])
                K.op("dve", lambda e, bo=bo, a=a, half=half: e.tensor_tensor(
                    out=xt[a][:, half * 512:(half + 1) * 512], in0=self.ps[bo][:],
                    in1=xt[a][:, half * 512:(half + 1) * 512], op=ALU.add),
                     reads=[self.Rps[bo], Rxt[a]], writes=[Rxt[a]])
            K.dma("sp", out=xv[:, t, :], in_=xt[a], reads=[Rxt[a]], writes=[self.Rxsp[t]], semres=self.Rxsps)
        K.barrier()
        for t in range(NT):
            K.dma("sp", out=self.X[:, t, :], in_=xv[:, t, :], reads=[self.Rxsp[t]], writes=[self.RX[t]], semres=self.RXs)

    def conv_stage(self, l):
        K = self.K
        d = self.d
        hT = self.hT
        SV = self.SV
        cw = SV[:, 0:12].rearrange("p (j k) -> p j k", j=4)
        gcol = SV[:, 12:16]
        for j in range(4):
            K.dma("sp", out=cw[:, j, :], in_=d["conv_w"][l][:, j * 128:(j + 1) * 128].rearrange("k p -> p k"), writes=[self.RSV],
                  allow_slow_non_contiguous=True)
        K.dma("sp", out=gcol, in_=d["conv_out_norm"][l].rearrange("(j p) -> p j", p=128), writes=[self.RSV],
              allow_slow_non_contiguous=True)
        Winv = d["w_in"][l].rearrange("(k p) c -> p k c", p=128)
        wc = [[self.wsv(sl * 3072 + i * 1024, 1024).rearrange("p (k c) -> p k c", k=8) for i in range(3)] for sl in range(2)]
        Rwc = [R("wc0"), R("wc1")]
        o = 6144
        m = self.wsv(o, 2052, F32); o += 4104
        cbuf = self.wsv(o, 2048, F32); o += 4096
        y = self.wsv(o, 2048, F32); o += 4096
        ysq = self.wsv(o, 2048); o += 2048
        rs = [self.wsv(o + a * 1024, 512, F32) for a in range(2)]; o += 2048
        tmp = [self.wsv(o + a * 1024, 512, F32) for a in range(2)]; o += 2048
        Rm, Rcb, Ry, Rysq = R("cm"), R("ccb"), R("cy"), R("cysq")
        Rrs = [R("crs0"), R("crs1")]
        Rtmp = [R("ctmp0"), R("ctmp1")]
        K.op("dve", lambda e: e.memset(m[:, 0:1], 0.0), writes=[Rm])
        K.op("dve", lambda e: e.memset(m[:, 2049:2052], 0.0), writes=[Rm])
        cols = (C_CH, C_CB, C_CC)
        it = 0
        for j in range(4):
            sl = j % 2
            for i in range(3):
                K.dma("pool", out=wc[sl][i], in_=Winv[:, :, cols[i] + j * 128:cols[i] + (j + 1) * 128], writes=[Rwc[sl]])
            for tb in range(4):
                b0 = 3 * (it % 2)
                a = it % 2
                it += 1
                for i in range(3):
                    for k in range(8):
                        K.op("pe", lambda e, i=i, k=k, b0=b0: e.matmul(self.ps[b0 + i][:], lhsT=wc[sl][i][:, k, :],
                                                                     rhs=hT[:, k, tb * 512:(tb + 1) * 512], start=(k == 0), stop=(k == 7)),
                             reads=[Rwc[sl], self.RhT[tb]], writes=[self.Rps[b0 + i]])
                K.op("act", lambda e, b0=b0, a=a: e.activation(out=tmp[a], in_=self.ps[b0 + 2][:], func=AF.Copy),
                     reads=[self.Rps[b0 + 2]], writes=[Rtmp[a]])
                K.op("dve", lambda e, b0=b0, a=a, tb=tb: e.tensor_tensor(out=m[:, 1 + tb * 512:1 + (tb + 1) * 512], in0=self.ps[b0][:],
                                                                       in1=tmp[a], op=ALU.mult),
                     reads=[self.Rps[b0], Rtmp[a]], writes=[Rm])
                K.op("act", lambda e, b0=b0, tb=tb: e.activation(out=cbuf[:, tb * 512:(tb + 1) * 512], in_=self.ps[b0 + 1][:], func=AF.Copy),
                     reads=[self.Rps[b0 + 1]], writes=[Rcb])
            K.op("dve", lambda e, j=j: e.tensor_scalar(out=y, in0=m[:, 0:2048], scalar1=cw[:, j, 0:1], scalar2=None, op0=ALU.mult),
                 reads=[Rm, self.RSV], writes=[Ry])
            for kk in (1, 2):
                K.op("dve", lambda e, j=j, kk=kk: e.scalar_tensor_tensor(out=y, in0=m[:, kk:kk + 2048], scalar=cw[:, j, kk:kk + 1],
                                                                        in1=y, op0=ALU.mult, op1=ALU.add),
                     reads=[Rm, self.RSV, Ry], writes=[Ry])
            K.op("dve", lambda e: e.tensor_tensor(out=y, in0=y, in1=cbuf, op=ALU.mult), reads=[Ry, Rcb], writes=[Ry])
            K.op("act", lambda e: e.activation(out=ysq, in_=y, func=AF.Square), reads=[Ry], writes=[Rysq])
            for tb in range(4):
                a = tb % 2
                bb = 6 + a
                K.op("pe", lambda e, bb=bb, tb=tb: e.matmul(self.ps[bb][:], lhsT=self.BD[:], rhs=ysq[:, tb * 512:(tb + 1) * 512], start=True, stop=True),
                     reads=[Rysq, self.Rconst], writes=[self.Rps[bb]])
                K.op("act", lambda e, bb=bb, a=a: e.activation(out=rs[a], in_=self.ps[bb][:], func=AF.Sqrt, scale=1.0 / 64, bias=EPS),
                     reads=[self.Rps[bb]], writes=[Rrs[a]])
                K.op("dve", lambda e, a=a: e.reciprocal(out=rs[a], in_=rs[a]), reads=[Rrs[a]], writes=[Rrs[a]])
                K.op("dve", lambda e, a=a, tb=tb, j=j: e.scalar_tensor_tensor(out=self.ymT[:, 12 + j, tb * 512:(tb + 1) * 512],
                                                                            in0=y[:, tb * 512:(tb + 1) * 512], scalar=gcol[:, j:j + 1],
                                                                            in1=rs[a], op0=ALU.mult, op1=ALU.mult),
                     reads=[Ry, Rrs[a], self.RSV], writes=[self.RymT[12 + j]])

    def rope_setup(self):
        K = self.K
        EX = self.EX
        posi = EX[:, 0:32].bitcast(I32)
        posf = self.wsv(32, 16, F32, EX)
        ang = self.wsv(64, 256, F32, EX).rearrange("p (t i) -> p t i", t=NT)
        nf = self.wsv(576, 256, F32, EX).rearrange("p (t i) -> p t i", t=NT)
        ni = EX[:, 1088:1600].bitcast(I32).rearrange("p (t i) -> p t i", t=NT)
        msk = self.wsv(1600, 256, F32, EX).rearrange("p (t i) -> p t i", t=NT)
        yy = self.wsv(2112, 256, F32, EX).rearrange("p (t i) -> p t i", t=NT)
        Rr = R("ropetmp")
        K.dma("sp", out=posi, in_=self.d["positions"].rearrange("(t p) o -> p (t o)", p=128), writes=[Rr],
              allow_slow_non_contiguous=True)
        K.op("dve", lambda e: e.tensor_copy(out=posf, in_=posi), reads=[Rr], writes=[Rr])
        for i in range(16):
            inv = float(10000.0 ** (-i / 16.0))
            K.op("dve", lambda e, i=i, inv=inv: e.tensor_scalar(out=ang[:, :, i], in0=posf, scalar1=inv, scalar2=None, op0=ALU.mult),
                 reads=[Rr], writes=[Rr])
        TWO_PI = 2.0 * math.pi
        C1 = 6.28125
        C2 = TWO_PI - C1
        K.op("dve", lambda e: e.tensor_scalar(out=nf, in0=ang, scalar1=1.0 / TWO_PI, scalar2=None, op0=ALU.mult), reads=[Rr], writes=[Rr])
        K.op("dve", lambda e: e.tensor_copy(out=ni, in_=nf), reads=[Rr], writes=[Rr])
        K.op("dve", lambda e: e.tensor_copy(out=nf, in_=ni), reads=[Rr], writes=[Rr])
        K.op("dve", lambda e: e.scalar_tensor_tensor(out=ang, in0=nf, scalar=-C1, in1=ang, op0=ALU.mult, op1=ALU.add), reads=[Rr], writes=[Rr])
        K.op("dve", lambda e: e.scalar_tensor_tensor(out=ang, in0=nf, scalar=-C2, in1=ang, op0=ALU.mult, op1=ALU.add), reads=[Rr], writes=[Rr])
        for which, shift in ((1, 0.0), (0, math.pi / 2)):
            K.op("dve", lambda e, shift=shift: e.tensor_scalar(out=yy, in0=ang, scalar1=shift, scalar2=None, op0=ALU.add), reads=[Rr], writes=[Rr])
            for _ in range(2):
                K.op("dve", lambda e: e.tensor_scalar(out=msk, in0=yy, scalar1=math.pi, scalar2=-TWO_PI, op0=ALU.is_gt, op1=ALU.mult), reads=[Rr], writes=[Rr])
                K.op("dve", lambda e: e.tensor_tensor(out=yy, in0=yy, in1=msk, op=ALU.add), reads=[Rr], writes=[Rr])
                K.op("dve", lambda e: e.tensor_scalar(out=msk, in0=yy, scalar1=-math.pi, scalar2=TWO_PI, op0=ALU.is_lt, op1=ALU.mult), reads=[Rr], writes=[Rr])
                K.op("dve", lambda e: e.tensor_tensor(out=yy, in0=yy, in1=msk, op=ALU.add), reads=[Rr], writes=[Rr])
            K.op("dve", lambda e: e.tensor_scalar(out=yy, in0=yy, scalar1=math.pi, scalar2=-math.pi, op0=ALU.min, op1=ALU.max), reads=[Rr], writes=[Rr])
            K.op("act", lambda e, which=which: e.activation(out=self.cs[:, which, :, :], in_=yy, func=AF.Sin), reads=[Rr], writes=[self.Rcs])
        K.barrier()

    def rstd_from_ss(self, ss_ap, n, Rs):
        K = self.K
        K.op("act", lambda e: e.activation(out=ss_ap, in_=ss_ap, func=AF.Sqrt, scale=1.0 / n, bias=EPS), reads=[Rs], writes=[Rs])
        K.op("dve", lambda e: e.reciprocal(out=ss_ap, in_=ss_ap), reads=[Rs], writes=[Rs])

    def rope(self, x, t, nh, tmp, Rx, Rt):
        K = self.K
        cosb = self.cs[:, 0, t, :].unsqueeze(1).to_broadcast([128, nh, 16])
        sinb = self.cs[:, 1, t, :].unsqueeze(1).to_broadcast([128, nh, 16])
        x1 = x[:, :, 0:16]
        x2 = x[:, :, 16:32]
        for i, (a, b) in enumerate(((x1, cosb), (x2, sinb), (x1, sinb), (x2, cosb))):
            K.op("dve", lambda e, i=i, a=a, b=b: e.tensor_tensor(out=tmp[:, i, :, :], in0=a, in1=b, op=ALU.mult),
                 reads=[Rx, self.Rcs], writes=[Rt])
        K.op("dve", lambda e: e.tensor_tensor(out=x1, in0=tmp[:, 0, :, :], in1=tmp[:, 1, :, :], op=ALU.subtract), reads=[Rt], writes=[Rx])
        K.op("dve", lambda e: e.tensor_tensor(out=x2, in0=tmp[:, 2, :, :], in1=tmp[:, 3, :, :], op=ALU.add), reads=[Rt], writes=[Rx])

    def mla_stage(self, l):
        K = self.K
        d = self.d
        hT = self.hT
        EX = self.EX
        grow = self.grow
        SV = self.SV
        Rg = self.Rgrow
        K.dma("sp", out=grow[:, 0:256], in_=d["mla_q_norm"][l].partition_broadcast(128), writes=[Rg])
        K.dma("sp", out=grow[:, 256:384], in_=d["mla_kv_norm"][l].partition_broadcast(128), writes=[Rg])
        K.dma("sp", out=grow[:, 384:480], in_=d["mla_q_head_norm"][l].partition_broadcast(128), writes=[Rg])
        K.dma("sp", out=grow[:, 480:576], in_=d["mla_k_head_norm"][l].partition_broadcast(128), writes=[Rg])
        K.dma("sp", out=SV[:, 0:512], in_=d["mla_out_norm"][l].partition_broadcast(128), writes=[self.RSV])
        K.op("dve", lambda e: e.tensor_scalar(out=grow[:, 384:480], in0=grow[:, 384:480], scalar1=float(96 ** -0.5), scalar2=None, op0=ALU.mult),
             reads=[Rg], writes=[Rg])
        gq = grow[:, 0:256]
        gkv = grow[:, 256:384]
        gqh = grow[:, 384:480]
        gkh = grow[:, 480:576]
        Winv = d["w_in"][l].rearrange("(k p) c -> p k c", p=128)
        wm = self.wsv(0, 3328).rearrange("p (k c) -> p k c", k=8)
        qnkT = self.wsv(3328, 6144).rearrange("p (j s) -> p j s", j=3)
        kpe = self.wsv(9472, 512, F32).rearrange("p (t i) -> p t i", t=NT)
        Rwm, RqnkT, Rkpe = R("wm"), R("qnkT"), R("kpe")
        K.dma("pool", out=wm, in_=Winv[:, :, C_QL:C_QL + 416], writes=[Rwm])
        qn = [self.wsv(a * 384, 384, BF16, EX) for a in range(2)]
        Rqn = [R("qn0"), R("qn1")]
        ssa = self.ss
        Rss = self.Rss
        for t in range(NT):
            a = t % 2
            b = t % 2
            for k in range(8):
                K.op("pe", lambda e, k=k, b=b, t=t: e.matmul(self.ps[b][:, 0:416], lhsT=hT[:, k, t * 128:(t + 1) * 128], rhs=wm[:, k, :],
                                                          start=(k == 0), stop=(k == 7)),
                     reads=[self.RhT[t // 4], Rwm], writes=[self.Rps[b]])
            K.op("act", lambda e, b=b, t=t: e.activation(out=self.junk[:, 0:256], in_=self.ps[b][:, 0:256], func=AF.Square, accum_out=ssa[:, 2 * a:2 * a + 1]),
                 reads=[self.Rps[b]], writes=[Rss])
            K.op("act", lambda e, b=b, t=t: e.activation(out=self.junk[:, 256:384], in_=self.ps[b][:, 256:384], func=AF.Square, accum_out=ssa[:, 2 * a + 1:2 * a + 2]),
                 reads=[self.Rps[b]], writes=[Rss])
            self.rstd_from_ss(ssa[:, 2 * a:2 * a + 1], 256, Rss)
            self.rstd_from_ss(ssa[:, 2 * a + 1:2 * a + 2], 128, Rss)
            K.op("dve", lambda e, b=b, a=a: e.scalar_tensor_tensor(out=qn[a][:, 0:256], in0=self.ps[b][:, 0:256], scalar=ssa[:, 2 * a:2 * a + 1], in1=gq,
                                                                  op0=ALU.mult, op1=ALU.mult), reads=[self.Rps[b], Rss, Rg], writes=[Rqn[a]])
            K.op("dve", lambda e, b=b, a=a: e.scalar_tensor_tensor(out=qn[a][:, 256:384], in0=self.ps[b][:, 256:384], scalar=ssa[:, 2 * a + 1:2 * a + 2], in1=gkv,
                                                                  op0=ALU.mult, op1=ALU.mult), reads=[self.Rps[b], Rss, Rg], writes=[Rqn[a]])
            K.op("act", lambda e, b=b, t=t: e.activation(out=kpe[:, t, :], in_=self.ps[b][:, 384:416], func=AF.Copy), reads=[self.Rps[b]], writes=[Rkpe])
            bt = 2 + t % 2
            pb = self.ps[bt][:].bitcast(BF16).rearrange("p (c k) -> p c k", c=8)
            for c in range(3):
                K.op("pe", lambda e, c=c, a=a, pb=pb: e.transpose(out=pb[:, c, :], in_=qn[a][:, c * 128:(c + 1) * 128], identity=self.ident[:]),
                     reads=[Rqn[a], self.Rconst], writes=[self.Rps[bt]])
            K.op("act", lambda e, t=t, pb=pb: e.activation(out=qnkT[:, :, t * 128:(t + 1) * 128], in_=pb[:, 0:3, :], func=AF.Copy),
                 reads=[self.Rps[bt]], writes=[RqnkT])
        K.barrier()
        qT = self.hT
        kT = self.wsv(10496, 16384).rearrange("p (h s) -> p h s", h=8)
        v1 = self.wsv(26880, 8320).rearrange("p (t h c) -> p t h c", t=NT, h=8)
        RqT, RkT, Rv1 = R("qT"), R("kT"), R("v1")
        wuq = self.wsv(0, 1536, BF16, EX).rearrange("p (j c) -> p j c", j=2)
        wukv = self.wsv(1536, 1024, BF16, EX)
        osb = self.wsv(2560, 8192, BF16, EX).rearrange("p (t h c) -> p t h c", t=NT, h=8)
        ET = [self.wsv(10752 + a * 512, 512, BF16, EX) for a in range(4)]
        Rwu, Rosb = R("wu"), R("osb")
        RET = [R(f"ET{a}") for a in range(4)]
        K.dma("pool", out=wuq, in_=d["mla_w_uq"][l].rearrange("(j p) c -> p j c", p=128), writes=[Rwu])
        K.dma("pool", out=wukv, in_=d["mla_w_ukv"][l], writes=[Rwu])
        K.op("pool", lambda e: e.memset(v1[:, :, :, 64:65], 1.0), writes=[Rv1])
        tq = self.wsv(0, 384, F32).rearrange("p (h c) -> p h c", h=4)
        tk = self.wsv(768, 384, F32).rearrange("p (h c) -> p h c", h=4)
        tsq = self.wsv(1536, 384, F32).rearrange("p (h c) -> p h c", h=4)
        trope = self.wsv(2304, 256, F32).rearrange("p (i h c) -> p i h c", i=4, h=4)
        qkb = self.wsv(2816, 384).rearrange("p (h c) -> p h c", h=4)
        Rtq, Rtk, Rtsq, Rtrope, Rqkb = R("tq"), R("tk"), R("tsq"), R("trope"), R("qkb")
        s4 = self.ss[:, 8:12]
        s1 = self.ss[:, 12:13]
        for t in range(NT):
            K.op("act", lambda e, t=t: e.activation(out=self.junk[:, 0:32], in_=kpe[:, t, :], func=AF.Square, accum_out=s1), reads=[Rkpe], writes=[Rss])
            for hh in range(2):
                bq = (2 * t + hh) % 2
                bk = 2 + (2 * t + hh) % 2
                for j in range(2):
                    K.op("pe", lambda e, j=j, bq=bq, hh=hh, t=t: e.matmul(self.ps[bq][:, 0:384], lhsT=qnkT[:, j, t * 128:(t + 1) * 128],
                                                                        rhs=wuq[:, j, hh * 384:(hh + 1) * 384], start=(j == 0), stop=(j == 1)),
                         reads=[RqnkT, Rwu], writes=[self.Rps[bq]])
                K.op("pe", lambda e, bk=bk, hh=hh, t=t: e.matmul(self.ps[bk][:], lhsT=qnkT[:, 2, t * 128:(t + 1) * 128],
                                                               rhs=wukv[:, hh * 512:(hh + 1) * 512], start=True, stop=True),
                     reads=[RqnkT, Rwu], writes=[self.Rps[bk]])
                psq = self.ps[bq][:, 0:384].rearrange("p (h c) -> p h c", h=4)
                pskv = self.ps[bk][:].rearrange("p (h c) -> p h c", h=4)
                K.op("act", lambda e, psq=psq: e.activation(out=tsq, in_=psq, func=AF.Square), reads=[self.Rps[bq]], writes=[Rtsq])
                K.op("dve", lambda e: e.tensor_reduce(out=s4, in_=tsq, axis=AX.X, op=ALU.add), reads=[Rtsq], writes=[Rss])
                self.rstd_from_ss(s4, 96, Rss)
                K.op("dve", lambda e, psq=psq: e.tensor_tensor(out=tq, in0=psq, in1=s4.unsqueeze(2).to_broadcast([128, 4, 96]), op=ALU.mult),
                     reads=[self.Rps[bq], Rss], writes=[Rtq])
                K.op("dve", lambda e: e.tensor_tensor(out=tq, in0=tq, in1=gqh.unsqueeze(1).to_broadcast([128, 4, 96]), op=ALU.mult),
                     reads=[Rtq, Rg], writes=[Rtq])
                self.rope(tq[:, :, 64:96], t, 4, trope, Rtq, Rtrope)
                K.op("act", lambda e: e.activation(out=qkb, in_=tq, func=AF.Copy), reads=[Rtq], writes=[Rqkb])
                bt = 4 + (2 * t + hh) % 2
                pb = self.ps[bt][:].bitcast(BF16).rearrange("p (c k) -> p c k", c=8)
                for h in range(4):
                    K.op("pe", lambda e, h=h, pb=pb: e.transpose(out=pb[0:96, h, :], in_=qkb[:, h, :], identity=self.ident[:]),
                         reads=[Rqkb, self.Rconst], writes=[self.Rps[bt]])
                K.op("act", lambda e, pb=pb, hh=hh, t=t: e.activation(out=qT[0:96, hh * 4:hh * 4 + 4, t * 128:(t + 1) * 128], in_=pb[0:96, 0:4, :], func=AF.Copy),
                     reads=[self.Rps[bt]], writes=[RqT])
                K.op("act", lambda e, pskv=pskv, hh=hh, t=t: e.activation(out=v1[:, t, hh * 4:hh * 4 + 4, 0:64], in_=pskv[:, :, 64:128], func=AF.Copy),
                     reads=[self.Rps[bk]], writes=[Rv1])
                K.op("act", lambda e, pskv=pskv: e.activation(out=tsq[:, :, 0:64], in_=pskv[:, :, 0:64], func=AF.Square), reads=[self.Rps[bk]], writes=[Rtsq])
                K.op("dve", lambda e: e.tensor_reduce(out=s4, in_=tsq[:, :, 0:64], axis=AX.X, op=ALU.add), reads=[Rtsq], writes=[Rss])
                K.op("dve", lambda e: e.tensor_scalar(out=s4, in0=s4, scalar1=s1, scalar2=None, op0=ALU.add), reads=[Rss], writes=[Rss])
                self.rstd_from_ss(s4, 96, Rss)
                K.op("dve", lambda e, pskv=pskv: e.tensor_tensor(out=tk[:, :, 0:64], in0=pskv[:, :, 0:64], in1=s4.unsqueeze(2).to_broadcast([128, 4, 64]), op=ALU.mult),
                     reads=[self.Rps[bk], Rss], writes=[Rtk])
                K.op("dve", lambda e, t=t: e.tensor_tensor(out=tk[:, :, 64:96], in0=kpe[:, t, :].unsqueeze(1).to_broadcast([128, 4, 32]),
                                                          in1=s4.unsqueeze(2).to_broadcast([128, 4, 32]), op=ALU.mult),
                     reads=[Rkpe, Rss], writes=[Rtk])
                K.op("dve", lambda e: e.tensor_tensor(out=tk, in0=tk, in1=gkh.unsqueeze(1).to_broadcast([128, 4, 96]), op=ALU.mult),
                     reads=[Rtk, Rg], writes=[Rtk])
                self.rope(tk[:, :, 64:96], t, 4, trope, Rtk, Rtrope)
                K.op("act", lambda e: e.activation(out=qkb, in_=tk, func=AF.Copy), reads=[Rtk], writes=[Rqkb])
                bt2 = 6 + (2 * t + hh) % 2
                pb2 = self.ps[bt2][:].bitcast(BF16).rearrange("p (c k) -> p c k", c=8)
                for h in range(4):
                    K.op("pe", lambda e, h=h, pb2=pb2: e.transpose(out=pb2[0:96, h, :], in_=qkb[:, h, :], identity=self.ident[:]),
                         reads=[Rqkb, self.Rconst], writes=[self.Rps[bt2]])
                K.op("act", lambda e, pb2=pb2, hh=hh, t=t: e.activation(out=kT[0:96, hh * 4:hh * 4 + 4, t * 128:(t + 1) * 128], in_=pb2[0:96, 0:4, :], func=AF.Copy),
                     reads=[self.Rps[bt2]], writes=[RkT])
        K.barrier()
        rinv = self.ss[:, 16:20]
        its = [(h, qb, kt) for h in range(8) for qb in range(4) for kt in range(NT)]

        def emit_s(i):
            h, qb, kt = its[i]
            sb_ = i % 4
            K.op("pe", lambda e: e.matmul(self.ps[sb_][:], lhsT=kT[0:96, h, kt * 128:(kt + 1) * 128],
                                          rhs=qT[0:96, h, qb * 512:(qb + 1) * 512], start=True, stop=True),
                 reads=[RkT, RqT], writes=[self.Rps[sb_]])
            K.op("act", lambda e: e.activation(out=ET[sb_], in_=self.ps[sb_][:], func=AF.Exp),
                 reads=[self.Rps[sb_]], writes=[RET[sb_]])

        osT = [self.wsv(35200 + a * 1024, 512, F32) for a in range(1)]
        RosT = [R("osT0")]

        def emit_pv(i):
            h, qb, kt = its[i]
            eb = i % 4
            ob_ = 4 + (i // NT) % 2
            K.op("pe", lambda e: e.matmul(self.ps[ob_][0:65, :], lhsT=v1[:, kt, h, :], rhs=ET[eb], start=(kt == 0), stop=(kt == NT - 1)),
                 reads=[RET[eb], Rv1], writes=[self.Rps[ob_]])
            if kt == NT - 1:
                K.op("dve", lambda e: e.tensor_copy(out=osT[0][0:65, :], in_=self.ps[ob_][0:65, :]), reads=[self.Rps[ob_]], writes=[RosT[0]])
                pt = self.ps[6 + (i // NT) % 2][:, 0:260].rearrange("p (q c) -> p q c", q=4)
                Rpt = self.Rps[6 + (i // NT) % 2]
                for qt in range(4):
                    K.op("pe", lambda e, qt=qt: e.transpose(out=pt[:, qt, :], in_=osT[0][0:65, qt * 128:(qt + 1) * 128], identity=self.identf[0:65, 0:65]),
                         reads=[RosT[0], self.Rconst], writes=[Rpt])
                K.op("dve", lambda e: e.reciprocal(out=rinv, in_=pt[:, :, 64]), reads=[Rpt], writes=[Rss])
                K.op("dve", lambda e: e.tensor_tensor(out=osb[:, qb * 4:qb * 4 + 4, h, :], in0=pt[:, :, 0:64],
                                                      in1=rinv.unsqueeze(2).to_broadcast([128, 4, 64]), op=ALU.mult),
                     reads=[Rpt, Rss], writes=[Rosb])

        emit_s(0)
        emit_s(1)
        for i in range(len(its)):
            if i + 2 < len(its):
                emit_s(i + 2)
            emit_pv(i)
        K.barrier()
        to = self.wsv(0, 512, F32).rearrange("p (h c) -> p h c", h=8)
        ob = self.wsv(1024, 512)
        Rto, Rob = R("to"), R("ob")
        s8 = self.ss[:, 20:28]
        gout = SV[:, 0:512].rearrange("p (h c) -> p h c", h=8)
        for t in range(NT):
            K.op("act", lambda e, t=t: e.activation(out=to, in_=osb[:, t, :, :], func=AF.Square), reads=[Rosb], writes=[Rto])
            K.op("dve", lambda e: e.tensor_reduce(out=s8, in_=to, axis=AX.X, op=ALU.add), reads=[Rto], writes=[Rss])
            self.rstd_from_ss(s8, 64, Rss)
            K.op("dve", lambda e, t=t: e.tensor_tensor(out=to, in0=osb[:, t, :, :], in1=s8.unsqueeze(2).to_broadcast([128, 8, 64]), op=ALU.mult),
                 reads=[Rosb, Rss], writes=[Rto])
            K.op("dve", lambda e: e.tensor_tensor(out=ob.rearrange("p (h c) -> p h c", h=8), in0=to, in1=gout, op=ALU.mult),
                 reads=[Rto, self.RSV], writes=[Rob])
            bt = t % 2
            pb = self.ps[bt][:].bitcast(BF16).rearrange("p (c k) -> p c k", c=8)
            for c in range(4):
                K.op("pe", lambda e, c=c, pb=pb: e.transpose(out=pb[:, c, :], in_=ob[:, c * 128:(c + 1) * 128], identity=self.ident[:]),
                     reads=[Rob, self.Rconst], writes=[self.Rps[bt]])
            K.op("act", lambda e, t=t, pb=pb: e.activation(out=self.ymT[:, 8:12, t * 128:(t + 1) * 128], in_=pb[:, 0:4, :], func=AF.Copy),
                 reads=[self.Rps[bt]], writes=[self.RymT[8], self.RymT[9], self.RymT[10], self.RymT[11]])

    def ssd_stage(self, l):
        K = self.K
        d = self.d
        hT = self.hT
        EX = self.EX
        SV = self.SV
        RSV = self.RSV
        Winv = d["w_in"][l].rearrange("(k p) c -> p k c", p=128)
        dtb_row = SV[:, 0:32]
        a_row = SV[:, 32:64]
        d_row = SV[:, 64:80]
        scw = SV[:, 96:156].rearrange("p (c k) -> p c k", c=12)
        scb = SV[:, 160:172]
        ssn = SV[:, 176:177]
        K.dma("sp", out=dtb_row, in_=d["ssd_dt_bias"][l].partition_broadcast(128), writes=[RSV])
        K.dma("sp", out=a_row, in_=d["ssd_a_log"][l].partition_broadcast(128), writes=[RSV])
        K.dma("sp", out=d_row, in_=d["ssd_d"][l].partition_broadcast(128), writes=[RSV])
        for ci in range(12):
            K.dma("sp", out=scw[:, ci, :], in_=d["ssd_conv_w"][l][:, ci * 128:(ci + 1) * 128].rearrange("k p -> p k"), writes=[RSV],
                  allow_slow_non_contiguous=True)
        K.dma("sp", out=scb, in_=d["ssd_conv_b"][l].rearrange("(c p) -> p c", p=128), writes=[RSV], allow_slow_non_contiguous=True)
        K.dma("sp", out=self.grow[:], in_=d["ssd_norm"][l].partition_broadcast(128), writes=[self.Rgrow])

        def f3(off, n3=NT):
            return self.wsv(off, n3 * 32, F32, EX).rearrange("p (t c) -> p t c", t=n3)
        dt = f3(0)
        dta = f3(1024)
        Pc = f3(2048)
        Tend = f3(3072, 17)
        eoff = f3(4160)
        cd = f3(5184)
        Wdt = self.wsv(6208, 256, BF16, EX).rearrange("p (k c) -> p k c", k=8)
        Rdt = R("ssd_small")
        RWdt = R("Wdt")
        K.dma("pool", out=Wdt, in_=Winv[:, :, C_DT:C_DT + 32], writes=[RWdt])
        K.op("act", lambda e: e.activation(out=a_row, in_=a_row, func=AF.Exp), reads=[RSV], writes=[RSV])
        K.op("dve", lambda e: e.tensor_scalar(out=a_row, in0=a_row, scalar1=-1.0, scalar2=None, op0=ALU.mult), reads=[RSV], writes=[RSV])
        for t in range(NT):
            b = t % 2
            for k in range(8):
                K.op("pe", lambda e, k=k, b=b, t=t: e.matmul(self.ps[b][:, 0:32], lhsT=hT[:, k, t * 128:(t + 1) * 128], rhs=Wdt[:, k, :],
                                                          start=(k == 0), stop=(k == 7)),
                     reads=[self.RhT[t // 4], RWdt], writes=[self.Rps[b]])
            K.op("dve", lambda e, b=b, t=t: e.tensor_tensor(out=dt[:, t, :], in0=self.ps[b][:, 0:32], in1=dtb_row, op=ALU.add),
                 reads=[self.Rps[b], RSV], writes=[Rdt])
        K.op("act", lambda e: e.activation(out=dt, in_=dt, func=AF.Exp), reads=[Rdt], writes=[Rdt])
        K.op("act", lambda e: e.activation(out=dt, in_=dt, func=AF.Ln, bias=1.0, scale=1.0), reads=[Rdt], writes=[Rdt])
        K.op("dve", lambda e: e.tensor_tensor(out=dta, in0=dt, in1=a_row.unsqueeze(1).to_broadcast([128, NT, 32]), op=ALU.mult),
             reads=[Rdt, RSV], writes=[Rdt])
        K.op("dve", lambda e: e.memset(Tend[:, 0, :], 0.0), writes=[Rdt])
        for t in range(NT):
            ba = 2 + (t % 2) * 2
            bb = ba + 1
            K.op("pe", lambda e, ba=ba, t=t: e.matmul(self.ps[ba][:, 0:32], lhsT=self.tri[:], rhs=dta[:, t, :], start=True, stop=True),
                 reads=[Rdt, self.Rconst], writes=[self.Rps[ba]])
            K.op("pe", lambda e, bb=bb, t=t: e.matmul(self.ps[bb][:, 0:32], lhsT=self.onesf[:], rhs=dta[:, t, :], start=True, stop=True),
                 reads=[Rdt, self.Rconst], writes=[self.Rps[bb]])
            K.op("dve", lambda e, ba=ba, t=t: e.tensor_tensor(out=Pc[:, t, :], in0=self.ps[ba][:, 0:32], in1=Tend[:, t, :], op=ALU.add),
                 reads=[self.Rps[ba], Rdt], writes=[Rdt])
            K.op("dve", lambda e, bb=bb, t=t: e.tensor_tensor(out=Tend[:, t + 1, :], in0=self.ps[bb][:, 0:32], in1=Tend[:, t, :], op=ALU.add),
                 reads=[self.Rps[bb], Rdt], writes=[Rdt])
        K.op("dve", lambda e: e.tensor_tensor(out=eoff[:, :, 0:16], in0=Pc[:, :, 0:16], in1=Tend[:, 0:NT, 0:16], op=ALU.subtract), reads=[Rdt], writes=[Rdt])
        K.op("dve", lambda e: e.tensor_tensor(out=Pc[:, :, 16:32], in0=Pc[:, :, 16:32], in1=dta[:, :, 16:32], op=ALU.subtract), reads=[Rdt], writes=[Rdt])
        K.op("dve", lambda e: e.tensor_tensor(out=eoff[:, :, 16:32], in0=Tend[:, 1:NT + 1, 16:32], in1=Pc[:, :, 16:32], op=ALU.subtract), reads=[Rdt], writes=[Rdt])
        K.op("dve", lambda e: e.tensor_tensor(out=cd, in0=Tend[:, 1:NT + 1, :], in1=Tend[:, 0:NT, :], op=ALU.subtract), reads=[Rdt], writes=[Rdt])
        K.op("act", lambda e: e.activation(out=eoff, in_=eoff, func=AF.Exp), reads=[Rdt], writes=[Rdt])
        K.op("act", lambda e: e.activation(out=cd, in_=cd, func=AF.Exp), reads=[Rdt], writes=[Rdt])
        nb = dta
        K.op("dve", lambda e: e.tensor_scalar(out=nb[:, :, 0:16], in0=Pc[:, :, 0:16], scalar1=-1.0, scalar2=None, op0=ALU.mult), reads=[Rdt], writes=[Rdt])
        K.op("dve", lambda e: e.tensor_scalar(out=nb[:, :, 16:32], in0=Pc[:, :, 16:32], scalar1=-1.0, scalar2=None, op0=ALU.mult), reads=[Rdt], writes=[Rdt])
        K.barrier()

        xs_g = self.wsv(0, 8192).rearrange("p (t c) -> p t c", t=NT)
        BT = self.wsv(8192, 2048)
        CT = self.wsv(10240, 2048)
        Btok = self.wsv(12288, 2048).rearrange("p (t c) -> p t c", t=NT)
        Wz = self.wsv(14336, 4096).rearrange("p (k c) -> p k c", k=8)
        ybr = self.X[:, 8:16, :].rearrange("p a c -> p (a c)").bitcast(BF16)
        pres = [self.wsv(18432, 2052, F32), self.wsv(0, 2052, F32, ybr)]
        accs = [self.wsv(22536, 2048, F32), self.wsv(4104, 2048, F32, ybr)]
        xsTs = [self.wsv(26632, 2048), self.wsv(8200, 2048, BF16, ybr)]
        wch = [self.wsv(28680 + a * 1024, 1024).rearrange("p (k c) -> p k c", k=8) for a in range(2)]
        MTall = self.wsv(18432, 16384).rearrange("p (t h c) -> p t h c", t=NT, h=8)
        Dbuf = [self.wsv(6464, 1024, F32, EX).rearrange("p (h c) -> p h c", h=8),
                self.wsv(34816, 1024, F32).rearrange("p (h c) -> p h c", h=8)]
        CBm = self.wsv(10560, 128, F32, EX)
        CBm2 = self.wsv(8512, 128, F32, EX)
        ecol = self.wsv(8768, 128, F32, EX).rearrange("p (t h) -> p t h", t=NT)
        xd = [self.wsv(9536, 512, BF16, EX), self.wsv(13376, 512, BF16, EX)]
        xdw = [self.wsv(10048, 512, BF16, EX), self.wsv(9024, 512, BF16, EX)]
        t1s = [self.wsv(18432, 512, F32), self.wsv(19456, 512, F32)]
        szs = [self.wsv(20480, 512, F32), self.wsv(21504, 512, F32)]
        yns = [self.wsv(22528, 512), self.wsv(23040, 512)]
        prev = self.wsv(10816, 512, F32, EX)
        prevb = self.wsv(11840, 512, BF16, EX)
        ysum = self.wsv(12352, 512, F32, EX)
        ybwd = self.X[:, 8:16, :].rearrange("p a (b c) -> p (a b) c", b=2)
        Rxs, RBT, RCT, RBtok, RWz = R("xs_g"), R("BT"), R("CT"), R("Btok"), R("Wz")
        Rpres, Raccs, RxsTs = [R("pre0"), R("pre1")], [R("acc0"), R("acc1")], [R("xsT0"), R("xsT1")]
        Rwch = [R("wch0"), R("wch1")]
        Rprev, Rprevb, Rysum = (R("prev"), R("prevb"), R("ysum"))
        Rxd = [R("xd0"), R("xd1")]
        Rxdw = [R("xdw0"), R("xdw1")]
        Rybwd = [R(f"ybwd{t}") for t in range(NT)]
        Rt1s = [R("t1s0"), R("t1s1")]
        Rszs = [R("szs0"), R("szs1")]
        Ryns = [R("yns0"), R("yns1")]
        RD = [R("Dbuf0"), R("Dbuf1")]
        RMT = [R(f"MT{t}") for t in range(NT)]
        Recol = R("ecol")
        RCBm = [R("CBm0"), R("CBm1")]
        CBms = [CBm, CBm2]
        v8 = lambda ap: ap.rearrange("p (h c) -> p h c", h=8)

        for g in range(2):
            K.dma("pool", out=Wz, in_=Winv[:, :, C_Z + g * 512:C_Z + (g + 1) * 512], writes=[RWz])
            for pp in range(2):
                K.op("pool", lambda e, pp=pp: e.memset(pres[pp][:, 0:2], 0.0), writes=[Rpres[pp]])
                K.op("pool", lambda e, pp=pp: e.memset(pres[pp][:, 2050:2052], 0.0), writes=[Rpres[pp]])
            chunks = [g * 4 + i for i in range(4)] + [8 + g, 10 + g]

            def cdst(n_):
                sl = n_ % 2
                if n_ < 4:
                    return xsTs[sl], RxsTs[sl]
                elif n_ == 4:
                    return BT, RBT
                return CT, RCT

            def stA(n_):
                ci = chunks[n_]
                sl = n_ % 2
                pre, Rpre = pres[sl], Rpres[sl]
                K.dma("pool", out=wch[sl], in_=Winv[:, :, C_X + ci * 128:C_X + (ci + 1) * 128], writes=[Rwch[sl]])
                for tb in range(4):
                    b = tb % 2
                    for k in range(8):
                        K.op("pe", lambda e, k=k: e.matmul(self.ps[b][:], lhsT=wch[sl][:, k, :], rhs=hT[:, k, tb * 512:(tb + 1) * 512],
                                                          start=(k == 0), stop=(k == 7)),
                             reads=[Rwch[sl], self.RhT[tb]], writes=[self.Rps[b]])
                    K.op("act", lambda e: e.activation(out=pre[:, 2 + tb * 512:2 + (tb + 1) * 512], in_=self.ps[b][:], func=AF.Copy),
                         reads=[self.Rps[b]], writes=[Rpre])

            def stB(n_):
                ci = chunks[n_]
                sl = n_ % 2
                pre, acc, Rpre, Racc = pres[sl], accs[sl], Rpres[sl], Raccs[sl]
                K.op("dve", lambda e: e.tensor_scalar(out=acc, in0=pre[:, 0:2048], scalar1=scw[:, ci, 0:1], scalar2=None, op0=ALU.mult),
                     reads=[Rpre, RSV], writes=[Racc])
                for kk in range(1, 5):
                    K.op("dve", lambda e, kk=kk: e.scalar_tensor_tensor(out=acc, in0=pre[:, kk:kk + 2048], scalar=scw[:, ci, kk:kk + 1], in1=acc,
                                                                       op0=ALU.mult, op1=ALU.add), reads=[Rpre, RSV, Racc], writes=[Racc])
                dst, Rdst = cdst(n_)
                K.op("act", lambda e: e.activation(out=dst, in_=acc, func=AF.Silu, bias=scb[:, ci:ci + 1]),
                     reads=[Racc, RSV], writes=[Rdst])

            def stC(n_):
                if n_ > 4:
                    return
                dst, Rdst = cdst(n_)
                for rnd in range(2):
                    bt = 2 + rnd
                    pb = self.ps[bt][:].bitcast(BF16).rearrange("p (c k) -> p c k", c=8)
                    for i in range(8):
                        tt = rnd * 8 + i
                        K.op("pe", lambda e, i=i, tt=tt: e.transpose(out=pb[:, i, :], in_=dst[:, tt * 128:(tt + 1) * 128], identity=self.ident[:]),
                             reads=[Rdst, self.Rconst], writes=[self.Rps[bt]])
                    if n_ < 4:
                        K.op("act", lambda e: e.activation(out=xs_g[:, rnd * 8:(rnd + 1) * 8, n_ * 128:(n_ + 1) * 128], in_=pb, func=AF.Copy),
                             reads=[self.Rps[bt]], writes=[Rxs])
                    else:
                        K.op("act", lambda e: e.activation(out=Btok[:, rnd * 8:(rnd + 1) * 8, :], in_=pb, func=AF.Copy),
                             reads=[self.Rps[bt]], writes=[RBtok])

            for n_ in range(len(chunks) + 2):
                if n_ < len(chunks):
                    stA(n_)
                if 0 <= n_ - 1 < len(chunks):
                    stB(n_ - 1)
                if 0 <= n_ - 2 < len(chunks):
                    stC(n_ - 2)
            K.barrier()

            for dirn in (1, 0):
                cb = dirn * 16 + g * 8
                mask = self.triT if dirn == 1 else self.tri
                order = list(range(NT - 1, -1, -1)) if dirn == 1 else list(range(NT))
                col = 0 if dirn == 1 else 127
                def mt_emit(i, order=order, col=col):
                    t = order[i]
                    a = i % 2
                    K.op("dve", lambda e: e.tensor_tensor(out=MTall[:, t, :, :], in0=Dbuf[a], in1=CBms[a].unsqueeze(1).to_broadcast([128, 8, 128]), op=ALU.mult),
                         reads=[RD[a], RCBm[a]], writes=[RMT[t]])
                    K.op("pool", lambda e: e.tensor_copy(out=ecol[:, t, :], in_=Dbuf[a][:, :, col]), reads=[RD[a]], writes=[Recol])

                for i, t in enumerate(order):
                    a = i % 2
                    tok = slice(t * 128, (t + 1) * 128)
                    K.op("pe", lambda e, tok=tok: e.matmul(self.ps[0][:, 0:128], lhsT=BT[:, tok], rhs=CT[:, tok], start=True, stop=True),
                         reads=[RBT, RCT], writes=[self.Rps[0]])
                    K.op("dve", lambda e, a=a: e.tensor_tensor(out=CBms[a], in0=self.ps[0][:, 0:128], in1=mask[:], op=ALU.mult),
                         reads=[self.Rps[0], self.Rconst], writes=[RCBm[a]])
                    for h in range(8):
                        bnk = 1 + a * 2 + h // 4
                        dstp = self.ps[bnk][:].rearrange("p (h c) -> p h c", h=4)[:, h % 4, :]
                        K.op("pe", lambda e, h=h, t=t, dstp=dstp: e.matmul(dstp, lhsT=Pc[:, t, cb + h:cb + h + 1].to_broadcast([128, 128]), rhs=self.identf[:],
                                                                         start=True, stop=True),
                             reads=[Rdt, self.Rconst], writes=[self.Rps[bnk]])
                    for h in range(8):
                        bnk = 1 + a * 2 + h // 4
                        src = self.ps[bnk][:].rearrange("p (h c) -> p h c", h=4)[:, h % 4, :]
                        if h in (6, 7):
                            K.op("act", lambda e, h=h, src=src, t=t, a=a: e.activation(out=Dbuf[a][:, h, :], in_=src, func=AF.Abs,
                                                                                     scale=1.0, bias=nb[:, t, cb + h:cb + h + 1]),
                                 reads=[self.Rps[bnk], Rdt], writes=[RD[a]])
                        else:
                            K.op("dve", lambda e, h=h, src=src, t=t, a=a: e.tensor_scalar(out=Dbuf[a][:, h, :], in0=src, scalar1=Pc[:, t, cb + h:cb + h + 1],
                                                                                        scalar2=0.0, op0=ALU.subtract, op1=(ALU.max if dirn == 1 else ALU.min)),
                                 reads=[self.Rps[bnk], Rdt], writes=[RD[a]])
                    K.op("act", lambda e, a=a: e.activation(out=Dbuf[a][:, 0:6, :], in_=Dbuf[a][:, 0:6, :], func=AF.Exp, scale=(-1.0 if dirn == 1 else 1.0)),
                         reads=[RD[a]], writes=[RD[a]])
                    K.op("act", lambda e, a=a: e.activation(out=Dbuf[a][:, 6:8, :], in_=Dbuf[a][:, 6:8, :], func=AF.Exp, scale=-1.0), reads=[RD[a]], writes=[RD[a]])
                    if i >= 1:
                        mt_emit(i - 1)
                mt_emit(NT - 1)

                K.op("dve", lambda e: e.memset(prev, 0.0), writes=[Rprev])
                K.op("dve", lambda e: e.memset(prevb, 0.0), writes=[Rprevb])

                def front2(i, dirn=dirn, cb=cb, order=order):
                    t = order[i]
                    a = i % 2
                    K.op("pool", lambda e: e.tensor_tensor(out=v8(xd[a]), in0=v8(xs_g[:, t, :]),
                                                           in1=dt[:, t, cb:cb + 8].unsqueeze(2).to_broadcast([128, 8, 64]), op=ALU.mult),
                         reads=[Rxs, Rdt], writes=[Rxd[a]])
                    K.op("dve", lambda e: e.tensor_tensor(out=v8(xdw[a]), in0=v8(xd[a]), in1=ecol[:, t, :].unsqueeze(2).to_broadcast([128, 8, 64]), op=ALU.mult),
                         reads=[Rxd[a], Recol], writes=[Rxdw[a]])
                    K.op("pe", lambda e: e.matmul(self.ps[5 + a][:], lhsT=Btok[:, t, :], rhs=xdw[a], start=True, stop=True),
                         reads=[RBtok, Rxdw[a]], writes=[self.Rps[5 + a]])

                def back(i, dirn=dirn, cb=cb, order=order):
                    t = order[i]
                    a = i % 2
                    tok = slice(t * 128, (t + 1) * 128)
                    K.op("pe", lambda e: e.matmul(self.ps[0][:], lhsT=CT[:, tok], rhs=prevb, start=True, stop=True),
                         reads=[RCT, Rprevb], writes=[self.Rps[0]])
                    for h in range(8):
                        K.op("pe", lambda e, h=h: e.matmul(self.ps[7][:, h * 64:(h + 1) * 64], lhsT=MTall[:, t, h, :], rhs=xd[a][:, h * 64:(h + 1) * 64], start=True, stop=True),
                             reads=[RMT[t], Rxd[a]], writes=[self.Rps[7]])
                    K.op("dve", lambda e: e.tensor_tensor(out=v8(prev), in0=v8(prev), in1=cd[:, t, cb:cb + 8].unsqueeze(2).to_broadcast([128, 8, 64]), op=ALU.mult),
                         reads=[Rprev, Rdt], writes=[Rprev])
                    K.op("dve", lambda e: e.tensor_tensor(out=prev, in0=prev, in1=self.ps[5 + a][:], op=ALU.add), reads=[Rprev, self.Rps[5 + a]], writes=[Rprev])
                    K.op("act", lambda e: e.activation(out=prevb, in_=prev, func=AF.Copy), reads=[Rprev], writes=[Rprevb])
                    K.op("dve", lambda e: e.tensor_tensor(out=v8(ysum), in0=v8(self.ps[0][:]),
                                                          in1=eoff[:, t, cb:cb + 8].unsqueeze(2).to_broadcast([128, 8, 64]), op=ALU.mult),
                         reads=[self.Rps[0], Rdt], writes=[Rysum])
                    if dirn == 1:
                        K.op("dve", lambda e: e.tensor_tensor(out=ybwd[:, t, :], in0=ysum, in1=self.ps[7][:], op=ALU.add),
                             reads=[Rysum, self.Rps[7]], writes=[Rybwd[t]])
                    else:
                        K.op("dve", lambda e: e.tensor_tensor(out=ysum, in0=ysum, in1=self.ps[7][:], op=ALU.add),
                             reads=[Rysum, self.Rps[7]], writes=[Rysum])
                        K.op("dve", lambda e: e.tensor_tensor(out=ybwd[:, t, :], in0=ybwd[:, t, :], in1=ysum, op=ALU.add),
                             reads=[Rysum, Rybwd[t]], writes=[Rybwd[t]])

                front2(0)
                for i in range(NT):
                    if i + 1 < NT:
                        front2(i + 1)
                    back(i)
            K.barrier()
            for t in range(NT):
                a = t % 2
                tok = slice(t * 128, (t + 1) * 128)
                K.op("pool", lambda e, t=t, a=a: e.tensor_tensor(out=v8(t1s[a]), in0=v8(xs_g[:, t, :]),
                                                                in1=d_row[:, g * 8:(g + 1) * 8].unsqueeze(2).to_broadcast([128, 8, 64]), op=ALU.mult),
                     reads=[Rxs, RSV], writes=[Rt1s[a]])
                K.op("dve", lambda e, t=t, a=a: e.tensor_tensor(out=ybwd[:, t, :], in0=ybwd[:, t, :], in1=t1s[a], op=ALU.add),
                     reads=[Rt1s[a], Rybwd[t]], writes=[Rybwd[t]])
                bz = 6 + a
                for k in range(8):
                    K.op("pe", lambda e, k=k, tok=tok, bz=bz: e.matmul(self.ps[bz][:], lhsT=hT[:, k, tok], rhs=Wz[:, k, :], start=(k == 0), stop=(k == 7)),
                         reads=[self.RhT[t // 4], RWz], writes=[self.Rps[bz]])
                K.op("act", lambda e, a=a, bz=bz: e.activation(out=szs[a], in_=self.ps[bz][:], func=AF.Silu), reads=[self.Rps[bz]], writes=[Rszs[a]])
                K.op("dve", lambda e, t=t, a=a: e.tensor_tensor(out=ybwd[:, t, :], in0=ybwd[:, t, :], in1=szs[a], op=ALU.mult),
                     reads=[Rszs[a], Rybwd[t]], writes=[Rybwd[t]])
                K.op("act", lambda e, t=t, a=a: e.activation(out=t1s[a], in_=ybwd[:, t, :], func=AF.Square, accum_out=self.ss[:, t:t + 1]),
                     reads=[Rybwd[t]], writes=[Rt1s[a], self.Rss])
            self.rstd_from_ss(self.ss[:, 0:NT], 512, self.Rss)
            for t in range(NT):
                a = t % 2
                tok = slice(t * 128, (t + 1) * 128)
                K.op("dve", lambda e, t=t, a=a: e.scalar_tensor_tensor(out=yns[a], in0=ybwd[:, t, :], scalar=self.ss[:, t:t + 1], in1=self.grow[:, g * 512:(g + 1) * 512],
                                                                      op0=ALU.mult, op1=ALU.mult), reads=[Rybwd[t], self.Rss, self.Rgrow], writes=[Ryns[a]])
                bt = 4 + a
                pb = self.ps[bt][:].bitcast(BF16).rearrange("p (c k) -> p c k", c=8)
                for c in range(4):
                    K.op("pe", lambda e, c=c, a=a, pb=pb: e.transpose(out=pb[:, c, :], in_=yns[a][:, c * 128:(c + 1) * 128], identity=self.ident[:]),
                         reads=[Ryns[a], self.Rconst], writes=[self.Rps[bt]])
                K.op("act", lambda e, pb=pb, tok=tok: e.activation(out=self.ymT[:, g * 4:g * 4 + 4, tok], in_=pb[:, 0:4, :], func=AF.Copy),
                     reads=[self.Rps[bt]], writes=[self.RymT[g * 4 + c] for c in range(4)])
            K.barrier()


_INPUT_ORDER = ["x", "positions", "ffn1_norm", "ffn1_w_gate", "ffn1_w_up", "ffn1_w_down", "mix_norm", "w_in",
                "ssd_conv_w", "ssd_conv_b", "ssd_dt_bias", "ssd_a_log", "ssd_d", "ssd_norm",
                "mla_q_norm", "mla_w_uq", "mla_kv_norm", "mla_w_ukv", "mla_q_head_norm", "mla_k_head_norm",
                "mla_out_norm", "conv_w", "conv_out_norm", "w_out",
                "ffn2_norm", "ffn2_w_gate", "ffn2_w_up", "ffn2_w_down"]


def make_in_maps(inputs, cores):
    maps = []
    for b in cores:
        m = {}
        for k in _INPUT_ORDER:
            a = np.asarray(inputs[k])
            if k == "x":
                m[k] = np.ascontiguousarray(a[b])
            elif k == "positions":
                m[k] = np.ascontiguousarray(a[b].reshape(S, 1).astype(np.int32))
            elif k in ("ssd_dt_bias", "ssd_a_log"):
                m[k] = np.ascontiguousarray(a.reshape(DEPTH, 32))
            else:
                m[k] = np.ascontiguousarray(a)
        maps.append(m)
    return maps


def kernel(**inputs):
    prog = Prog()
    nc = prog.build()
    in_maps = make_in_maps(inputs, list(range(8)))
    res = run_bass_kernel_spmd(nc, in_maps, core_ids=list(range(8)))
    return np.stack([np.asarray(r["out"]).reshape(S, D) for r in res.results], axis=0).astype(np.float32)
```

```python
import math
import numpy as np
from contextlib import ExitStack
import concourse.bass as bass
import concourse.mybir as mybir
from concourse.bass_utils import run_bass_kernel_spmd

F32 = mybir.dt.float32
BF16 = mybir.dt.bfloat16
I32 = mybir.dt.int32
AF = mybir.ActivationFunctionType
ALU = mybir.AluOpType
AX = mybir.AxisListType

D = 1024
S = 2048
NT = 16
DFF = 2816
NFF = 22
DEPTH = 4
D_IN = 4544
EPS = 1e-6
FFN_GROUPS = [(0, 6), (6, 12), (12, 17), (17, 22)]

C_Z = 0
C_X = 1024
C_B = 2048
C_C = 2304
C_DT = 2560
C_QL = 2592
C_KVL = 2848
C_KPE = 2976
C_CH = 3008
C_CB = 3520
C_CC = 4032


class R:
    __slots__ = ("name", "w", "r")

    def __init__(self, name):
        self.name = name
        self.w = None
        self.r = {}


class KB:
    def __init__(self, nc, es):
        self.nc = nc
        self.es = es
        self.eng = dict(pe=nc.tensor, act=nc.scalar, dve=nc.vector, pool=nc.gpsimd, sp=nc.sync)
        self.psem = {e: es.enter_context(nc.semaphore("p_" + e)) for e in self.eng}
        self.cnt = {e: 0 for e in self.eng}
        self.seen = {e: {} for e in self.eng}
        self.dsem = {}
        self.nwait = 0

    def _semh(self, key):
        if key[0] == "e":
            return self.psem[key[1]]
        return self.dsem[key][0]

    def wait(self, eng, dep):
        key, val = dep
        if key[0] == "e" and key[1] == eng:
            if eng in ("pe", "sp"):
                return
            if self.cnt[eng] - val >= 3:
                return
        if key[0] == "d":
            val = max(val, 16 * self.dsem[key][1])
        if self.seen[eng].get(key, 0) >= val:
            return
        self.eng[eng].wait_ge(self._semh(key), val)
        self.seen[eng][key] = val
        self.nwait += 1

    def _deps(self, eng, reads, writes):
        for r in reads:
            if r.w is not None:
                self.wait(eng, r.w)
        for w in writes:
            if w.w is not None:
                self.wait(eng, w.w)
            for k, v in w.r.items():
                self.wait(eng, (k, v))

    def _mark(self, me, reads, writes):
        k, v = me
        for r in reads:
            if r.r.get(k, 0) < v:
                r.r[k] = v
        for w in writes:
            w.w = me
            w.r = {}

    def record(self, f):
        old = getattr(self, "rec", None)
        self.rec = []
        f()
        out = self.rec
        self.rec = old
        return out

    def play_interleaved(self, lists):
        n = max(len(x) for x in lists)
        for i in range(n):
            for x in lists:
                if i < len(x):
                    eng, fn, reads, writes = x[i]
                    self.op(eng, fn, reads, writes)

    def op(self, eng, fn, reads=(), writes=()):
        if getattr(self, "rec", None) is not None:
            self.rec.append((eng, fn, list(reads), list(writes)))
            return None
        self._deps(eng, reads, writes)
        ins = fn(self.eng[eng])
        self.cnt[eng] += 1
        ins.then_inc(self.psem[eng], 1)
        self._mark((("e", eng), self.cnt[eng]), reads, writes)
        return ins

    def dma(self, q, out, in_, reads=(), writes=(), semres=None, **kw):
        self._deps(q, reads, writes)
        sr = semres if semres is not None else writes[0]
        key = ("d", sr.name)
        if key not in self.dsem:
            self.dsem[key] = [self.es.enter_context(self.nc.semaphore("d_" + sr.name)), 0]
        ent = self.dsem[key]
        ent[1] += 1
        self.eng[q].dma_start(out=out, in_=in_, **kw).then_inc(ent[0], 16)
        self._mark((key, 16 * ent[1]), reads, writes)

    def barrier(self):
        for e in self.eng:
            for e2 in self.eng:
                if e2 != e and self.cnt[e2] > 0:
                    self.wait(e, (("e", e2), self.cnt[e2]))
            for key, ent in self.dsem.items():
                if ent[1] > 0:
                    self.wait(e, (key, 16 * ent[1]))

    def final_wait(self, eng="sp"):
        for key, ent in self.dsem.items():
            if ent[1] > 0:
                self.wait(eng, (key, 16 * ent[1]))
        for e2 in self.eng:
            if e2 != eng and self.cnt[e2] > 0:
                self.wait(eng, (("e", e2), self.cnt[e2]))


class Prog:
    def __init__(self, n_layers=DEPTH, stages=("ffn1", "conv", "mla", "ssd", "ffn2")):
        self.n_layers = n_layers
        self.stages = stages

    def build(self):
        nc = bass.Bass("TRN2", target_bir_lowering=False)
        self.nc = nc
        L = DEPTH

        def din(name, shape, dt=F32):
            return nc.dram_tensor(name, list(shape), dt, kind="ExternalInput").ap()

        self.d = d = {}
        d["x"] = din("x", [S, D])
        d["positions"] = din("positions", [S, 1], I32)
        d["ffn1_norm"] = din("ffn1_norm", [L, D])
        d["ffn1_w_gate"] = din("ffn1_w_gate", [L, D, DFF])
        d["ffn1_w_up"] = din("ffn1_w_up", [L, D, DFF])
        d["ffn1_w_down"] = din("ffn1_w_down", [L, DFF, D])
        d["mix_norm"] = din("mix_norm", [L, D])
        d["w_in"] = din("w_in", [L, D, D_IN])
        d["ssd_conv_w"] = din("ssd_conv_w", [L, 5, 1536])
        d["ssd_conv_b"] = din("ssd_conv_b", [L, 1536])
        d["ssd_dt_bias"] = din("ssd_dt_bias", [L, 32])
        d["ssd_a_log"] = din("ssd_a_log", [L, 32])
        d["ssd_d"] = din("ssd_d", [L, 16])
        d["ssd_norm"] = din("ssd_norm", [L, 1024])
        d["mla_q_norm"] = din("mla_q_norm", [L, 256])
        d["mla_w_uq"] = din("mla_w_uq", [L, 256, 768])
        d["mla_kv_norm"] = din("mla_kv_norm", [L, 128])
        d["mla_w_ukv"] = din("mla_w_ukv", [L, 128, 1024])
        d["mla_q_head_norm"] = din("mla_q_head_norm", [L, 96])
        d["mla_k_head_norm"] = din("mla_k_head_norm", [L, 96])
        d["mla_out_norm"] = din("mla_out_norm", [L, 512])
        d["conv_w"] = din("conv_w", [L, 3, 512])
        d["conv_out_norm"] = din("conv_out_norm", [L, 512])
        d["w_out"] = din("w_out", [L, 2048, D])
        d["ffn2_norm"] = din("ffn2_norm", [L, D])
        d["ffn2_w_gate"] = din("ffn2_w_gate", [L, D, DFF])
        d["ffn2_w_up"] = din("ffn2_w_up", [L, D, DFF])
        d["ffn2_w_down"] = din("ffn2_w_down", [L, DFF, D])
        self.out = nc.dram_tensor("out", [S, D], F32, kind="ExternalOutput").ap()

        with ExitStack() as es:
            self.es = es
            K = self.K = KB(nc, es)

            def sb(name, shape, dt):
                return es.enter_context(nc.sbuf_tensor(name, list(shape), dt))

            self.X = sb("X", [128, NT, D], F32)
            self.RX = [R(f"X{t}") for t in range(NT)]
            self.RXs = R("Xsem")
            self.Rxsps = R("xspsem")
            self.hT = sb("hT", [128, 8, S], BF16)
            self.RhT = [R(f"hT{b}") for b in range(4)]
            self.WS = sb("WS", [128, 36864], BF16)
            self.EX = sb("EX", [128, 14336], BF16)
            self.ident = sb("ident", [128, 128], BF16)
            self.identf = sb("identf", [128, 128], F32)
            self.grow = sb("grow", [128, D], F32)
            self.Rgrow = R("grow")
            self.ss = sb("ss", [128, 2 * NT], F32)
            self.SV = sb("SV", [128, 512], F32)
            self.st2 = sb("st2", [128, 64], F32)
            self.RSV = R("SV")
            self.BD = sb("BD", [128, 128], BF16)
            self.xsp = nc.dram_tensor("xspill", [S, D], F32, kind="Internal").ap()
            self.Rxsp = [R(f"xsp{t}") for t in range(NT)]
            self.ymT = self.X[:].rearrange("p t c -> p (t c)").bitcast(BF16).rearrange("p (j s) -> p j s", j=16)
            self.RymT = [R(f"ymT{j}") for j in range(16)]
            self.Rss = R("ss")
            self.junk = self.EX[:, 9216:10240]
            self.Rjunk = R("junk")
            self.ps = [es.enter_context(nc.psum_tensor(f"ps{i}", [128, 512], F32)) for i in range(8)]
            self.Rps = [R(f"ps{i}") for i in range(8)]
            self.Rconst = R("const")

            K.op("pool", lambda e: e.memset(self.ident[:], 0.0), writes=[self.Rconst])
            K.op("pool", lambda e: e.affine_select(out=self.ident[:], in_=self.ident[:], pattern=[[-1, 128]],
                                                   compare_op=ALU.not_equal, fill=1.0, base=0, channel_multiplier=1),
                 writes=[self.Rconst])
            K.op("pool", lambda e: e.memset(self.identf[:], 0.0), writes=[self.Rconst])
            K.op("pool", lambda e: e.affine_select(out=self.identf[:], in_=self.identf[:], pattern=[[-1, 128]],
                                                   compare_op=ALU.not_equal, fill=1.0, base=0, channel_multiplier=1),
                 writes=[self.Rconst])

            K.op("pool", lambda e: e.memset(self.BD[:], 0.0), writes=[self.Rconst])
            K.op("pool", lambda e: e.memset(self.BD[0:64, 0:64], 1.0), writes=[self.Rconst])
            K.op("pool", lambda e: e.memset(self.BD[64:128, 64:128], 1.0), writes=[self.Rconst])
            self.cs = sb("cs", [128, 2, NT, 16], F32)
            self.tri = sb("tri", [128, 128], F32)
            self.triT = sb("triT", [128, 128], F32)
            self.onesf = sb("onesf", [128, 128], F32)
            for tt_, st_, cm_ in ((self.tri, 1, -1), (self.triT, -1, 1)):
                K.op("pool", lambda e, tt_=tt_: e.memset(tt_[:], 1.0), writes=[self.Rconst])
                K.op("pool", lambda e, tt_=tt_, st_=st_, cm_=cm_: e.affine_select(out=tt_[:], in_=tt_[:], pattern=[[st_, 128]], compare_op=ALU.is_ge, fill=0.0,
                                                                                 base=0, channel_multiplier=cm_), writes=[self.Rconst])
            K.op("pool", lambda e: e.memset(self.onesf[:], 1.0), writes=[self.Rconst])
            self.Rcs = R("cs")
            self.rope_setup()
            xv = d["x"].rearrange("(t p) c -> p t c", p=128)
            for t in range(NT):
                K.dma("sp", out=self.X[:, t, :], in_=xv[:, t, :], writes=[self.RX[t]], semres=self.RXs)

            for l in range(self.n_layers):
                if "ffn1" in self.stages:
                    self.norm_stage(d["ffn1_norm"][l])
                    self.ffn_stage(d["ffn1_w_gate"][l], d["ffn1_w_up"][l], d["ffn1_w_down"][l])
                    K.barrier()
                mix = [s for s in self.stages if s in ("conv", "mla", "ssd")]
                if mix:
                    self.norm_stage(d["mix_norm"][l])
                    self.mixer_begin(l)
                    K.barrier()
                    if "ssd" in mix:
                        self.ssd_stage(l)
                        K.barrier()
                    if "conv" in mix:
                        self.conv_stage(l)
                        K.barrier()
                    if "mla" in mix:
                        self.mla_stage(l)
                        K.barrier()
                    self.mixer_end(l, mix)
                    K.barrier()
                if "ffn2" in self.stages:
                    self.norm_stage(d["ffn2_norm"][l])
                    self.ffn_stage(d["ffn2_w_gate"][l], d["ffn2_w_up"][l], d["ffn2_w_down"][l])
                    K.barrier()

            ov = self.out.rearrange("(t p) c -> p t c", p=128)
            Rout = R("out")
            for t in range(NT):
                K.dma("sp", out=ov[:, t, :], in_=self.X[:, t, :], reads=[self.RX[t]], writes=[Rout])
            K.final_wait("sp")
        return nc

    def wsv(self, off, n, dt=BF16, base=None):
        base = self.WS if base is None else base
        if dt == F32:
            return base[:, off:off + 2 * n].bitcast(F32)
        return base[:, off:off + n]

    def norm_stage(self, gain):
        K = self.K
        X, hT = self.X, self.hT
        K.dma("sp", out=self.grow[:], in_=gain.partition_broadcast(128), writes=[self.Rgrow])
        for t in range(NT):
            K.op("act", lambda e, t=t: e.activation(out=self.junk, in_=X[:, t, :], func=AF.Square,
                                                    accum_out=self.ss[:, t:t + 1]),
                 reads=[self.RX[t]], writes=[self.Rss])
        K.op("act", lambda e: e.activation(out=self.ss[:, NT:2 * NT], in_=self.ss[:, 0:NT], func=AF.Sqrt,
                                           scale=1.0 / D, bias=EPS),
             reads=[self.Rss], writes=[self.Rss])
        K.op("dve", lambda e: e.reciprocal(out=self.ss[:, NT:2 * NT], in_=self.ss[:, NT:2 * NT]),
             reads=[self.Rss], writes=[self.Rss])
        xs = [self.wsv(7168, D, BF16, self.EX), self.wsv(8192, D, BF16, self.EX)]
        Rxs = [R("xs0"), R("xs1")]
        for t in range(NT):
            j = t % 2
            K.op("dve", lambda e, t=t, j=j: e.scalar_tensor_tensor(out=xs[j], in0=X[:, t, :],
                                                                   scalar=self.ss[:, NT + t:NT + t + 1],
                                                                   in1=self.grow[:], op0=ALU.mult, op1=ALU.mult),
                 reads=[self.RX[t], self.Rss, self.Rgrow], writes=[Rxs[j]])
            bank = t % 2
            pb = self.ps[bank][:].bitcast(BF16).rearrange("p (c k) -> p c k", c=8)
            for c in range(8):
                K.op("pe", lambda e, c=c, j=j, pb=pb: e.transpose(out=pb[:, c, :], in_=xs[j][:, c * 128:(c + 1) * 128],
                                                                  identity=self.ident[:]),
                     reads=[Rxs[j], self.Rconst], writes=[self.Rps[bank]])
            K.op("act", lambda e, t=t, pb=pb: e.activation(out=hT[:, :, t * 128:(t + 1) * 128], in_=pb, func=AF.Copy),
                 reads=[self.Rps[bank]], writes=[self.RhT[t // 4]])

    def ffn_stage(self, Wg, Wu, Wd):
        K = self.K
        X, hT = self.X, self.hT
        SL = 18432
        WG = [self.wsv(s * SL, 6144).rearrange("p (k c) -> p k c", k=8) for s in range(2)]
        WU = [self.wsv(s * SL + 6144, 6144).rearrange("p (k c) -> p k c", k=8) for s in range(2)]
        WD = [self.wsv(s * SL + 12288, 6144).rearrange("p (f c) -> p f c", f=6) for s in range(2)]
        RW = [R("ffw0"), R("ffw1")]
        act = [self.wsv(a * 3072, 3072, BF16, self.EX).rearrange("p (f c) -> p f c", f=6) for a in range(2)]
        Ract = [R("act0"), R("act1")]
        sil = [self.wsv(6144 + a * 512, 512, BF16, self.EX) for a in range(2)]
        Rsil = [R("sil0"), R("sil1")]
        Wgv = Wg.rearrange("(k p) c -> p k c", p=128)
        Wuv = Wu.rearrange("(k p) c -> p k c", p=128)

        def load(q):
            f0, f1 = FFN_GROUPS[q]
            nf = f1 - f0
            s = q % 2
            K.dma("pool", out=WG[s][:, :, 0:nf * 128], in_=Wgv[:, :, f0 * 128:f1 * 128], writes=[RW[s]])
            K.dma("pool", out=WU[s][:, :, 0:nf * 128], in_=Wuv[:, :, f0 * 128:f1 * 128], writes=[RW[s]])
            K.dma("pool", out=WD[s][:, 0:nf, :], in_=Wd[f0 * 128:f1 * 128, :].rearrange("(f p) c -> p f c", p=128),
                  writes=[RW[s]])

        load(0)
        it = 0
        ab = 0
        for q in range(4):
            if q + 1 < 4:
                load(q + 1)
            f0, f1 = FFN_GROUPS[q]
            nf = f1 - f0
            s = q % 2
            for tb in range(4):
                for f in range(nf):
                    bg = (it % 2) * 2
                    bu = bg + 1
                    si = it % 2
                    it += 1
                    for k in range(8):
                        K.op("pe", lambda e, k=k, f=f, bg=bg: e.matmul(self.ps[bg][:], lhsT=WG[s][:, k, f * 128:(f + 1) * 128],
                                                                       rhs=hT[:, k, tb * 512:(tb + 1) * 512], start=(k == 0), stop=(k == 7)),
                             reads=[RW[s], self.RhT[tb]], writes=[self.Rps[bg]])
                    for k in range(8):
                        K.op("pe", lambda e, k=k, f=f, bu=bu: e.matmul(self.ps[bu][:], lhsT=WU[s][:, k, f * 128:(f + 1) * 128],
                                                                       rhs=hT[:, k, tb * 512:(tb + 1) * 512], start=(k == 0), stop=(k == 7)),
                             reads=[RW[s], self.RhT[tb]], writes=[self.Rps[bu]])
                    K.op("act", lambda e, bg=bg, si=si: e.activation(out=sil[si], in_=self.ps[bg][:], func=AF.Silu),
                         reads=[self.Rps[bg]], writes=[Rsil[si]])
                    K.op("dve", lambda e, bu=bu, si=si, f=f: e.tensor_tensor(out=act[ab][:, f, :], in0=self.ps[bu][:], in1=sil[si], op=ALU.mult),
                         reads=[self.Rps[bu], Rsil[si]], writes=[Ract[ab]])
                for tt in range(4):
                    t = tb * 4 + tt
                    for half in range(2):
                        bo = 4 + (t % 2) * 2 + half
                        for f in range(nf):
                            K.op("pe", lambda e, f=f, bo=bo, tt=tt, half=half: e.matmul(
                                self.ps[bo][:], lhsT=act[ab][:, f, tt * 128:(tt + 1) * 128],
                                rhs=WD[s][:, f, half * 512:(half + 1) * 512], start=(f == 0), stop=(f == nf - 1)),
                                 reads=[Ract[ab], RW[s]], writes=[self.Rps[bo]])
                        K.op("dve", lambda e, bo=bo, t=t, half=half: e.scalar_tensor_tensor(
                            out=X[:, t, half * 512:(half + 1) * 512], in0=self.ps[bo][:], scalar=0.5,
                            in1=X[:, t, half * 512:(half + 1) * 512], op0=ALU.mult, op1=ALU.add),
                             reads=[self.Rps[bo], self.RX[t]], writes=[self.RX[t]])
                ab ^= 1

    def mixer_begin(self, l):
        K = self.K
        xv = self.xsp.rearrange("(t p) c -> p t c", p=128)
        for t in range(NT):
            K.dma("sp", out=xv[:, t, :], in_=self.X[:, t, :], reads=[self.RX[t]], writes=[self.Rxsp[t]], semres=self.Rxsps)

    def mixer_end(self, l, mix):
        K = self.K
        chunks = []
        if "ssd" in mix:
            chunks += list(range(0, 8))
        if "mla" in mix:
            chunks += list(range(8, 12))
        if "conv" in mix:
            chunks += list(range(12, 16))
        wo = self.wsv(0, 16384).rearrange("p (j c) -> p j c", j=16)
        Rwo = R("wo")
        wov = self.d["w_out"][l].rearrange("(j p) c -> p j c", p=128)
        for j0 in range(0, 16, 4):
            K.dma("pool", out=wo[:, j0:j0 + 4, :], in_=wov[:, j0:j0 + 4, :], writes=[Rwo])
        xt = [self.wsv(a * 2048, 1024, F32, self.EX) for a in range(2)]
        Rxt = [R("xt0"), R("xt1")]
        xv = self.xsp.rearrange("(t p) c -> p t c", p=128)
        for t in range(NT):
            a = t % 2
            K.dma("sp", out=xt[a], in_=xv[:, t, :], reads=[self.Rxsp[t]], writes=[Rxt[a]])
            for half in range(2):
                bo = (t % 2) * 2 + half
                for i, j in enumerate(chunks):
                    K.op("pe", lambda e, j=j, bo=bo, half=half, i=i: e.matmul(
                        self.ps[bo][:], lhsT=self.ymT[:, j, t * 128:(t + 1) * 128],
                        rhs=wo[:, j, half * 512:(half + 1) * 512], start=(i == 0), stop=(i == len(chunks) - 1)),
                         reads=[self.RymT[j], Rwo], writes=[self.Rps[bo]])
                K.op("dve", lambda e, bo=bo, a=a, half=half: e.tensor_tensor(
                    out=xt[a][:, half * 512:(half + 1) * 512], in0=self.ps[bo][:],
                    in1=xt[a][:, half * 512:(half + 1) * 512], op=ALU.add),
                     reads=[self.Rps[bo], Rxt[a]], writes=[Rxt[a]])
            K.dma("sp", out=xv[:, t, :], in_=xt[a], reads=[Rxt[a]], writes=[self.Rxsp[t]], semres=self.Rxsps)
        K.barrier()
        for t in range(NT):
            K.dma("sp", out=self.X[:, t, :], in_=xv[:, t, :], reads=[self.Rxsp[t]], writes=[self.RX[t]], semres=self.RXs)

    def conv_stage(self, l):
        K = self.K
        d = self.d
        hT = self.hT
        SV = self.SV
        cw = SV[:, 0:12].rearrange("p (j k) -> p j k", j=4)
        gcol = SV[:, 12:16]
        for j in range(4):
            K.dma("sp", out=cw[:, j, :], in_=d["conv_w"][l][:, j * 128:(j + 1) * 128].rearrange("k p -> p k"), writes=[self.RSV],
                  allow_slow_non_contiguous=True)
        K.dma("sp", out=gcol, in_=d["conv_out_norm"][l].rearrange("(j p) -> p j", p=128), writes=[self.RSV],
              allow_slow_non_contiguous=True)
        Winv = d["w_in"][l].rearrange("(k p) c -> p k c", p=128)
        wc = [[self.wsv(sl * 3072 + i * 1024, 1024).rearrange("p (k c) -> p k c", k=8) for i in range(3)] for sl in range(2)]
        Rwc = [R("wc0"), R("wc1")]
        o = 6144
        m = self.wsv(o, 2052, F32); o += 4104
        cbuf = self.wsv(o, 2048, F32); o += 4096
        y = self.wsv(o, 2048, F32); o += 4096
        ysq = self.wsv(o, 2048); o += 2048
        rs = [self.wsv(o + a * 1024, 512, F32) for a in range(2)]; o += 2048
        tmp = [self.wsv(o + a * 1024, 512, F32) for a in range(2)]; o += 2048
        Rm, Rcb, Ry, Rysq = R("cm"), R("ccb"), R("cy"), R("cysq")
        Rrs = [R("crs0"), R("crs1")]
        Rtmp = [R("ctmp0"), R("ctmp1")]
        K.op("dve", lambda e: e.memset(m[:, 0:1], 0.0), writes=[Rm])
        K.op("dve", lambda e: e.memset(m[:, 2049:2052], 0.0), writes=[Rm])
        cols = (C_CH, C_CB, C_CC)
        it = 0
        for j in range(4):
            sl = j % 2
            for i in range(3):
                K.dma("pool", out=wc[sl][i], in_=Winv[:, :, cols[i] + j * 128:cols[i] + (j + 1) * 128], writes=[Rwc[sl]])
            for tb in range(4):
                b0 = 3 * (it % 2)
                a = it % 2
                it += 1
                for i in range(3):
                    for k in range(8):
                        K.op("pe", lambda e, i=i, k=k, b0=b0: e.matmul(self.ps[b0 + i][:], lhsT=wc[sl][i][:, k, :],
                                                                     rhs=hT[:, k, tb * 512:(tb + 1) * 512], start=(k == 0), stop=(k == 7)),
                             reads=[Rwc[sl], self.RhT[tb]], writes=[self.Rps[b0 + i]])
                K.op("act", lambda e, b0=b0, a=a: e.activation(out=tmp[a], in_=self.ps[b0 + 2][:], func=AF.Copy),
                     reads=[self.Rps[b0 + 2]], writes=[Rtmp[a]])
                K.op("dve", lambda e, b0=b0, a=a, tb=tb: e.tensor_tensor(out=m[:, 1 + tb * 512:1 + (tb + 1) * 512], in0=self.ps[b0][:],
                                                                       in1=tmp[a], op=ALU.mult),
                     reads=[self.Rps[b0], Rtmp[a]], writes=[Rm])
                K.op("act", lambda e, b0=b0, tb=tb: e.activation(out=cbuf[:, tb * 512:(tb + 1) * 512], in_=self.ps[b0 + 1][:], func=AF.Copy),
                     reads=[self.Rps[b0 + 1]], writes=[Rcb])
            K.op("dve", lambda e, j=j: e.tensor_scalar(out=y, in0=m[:, 0:2048], scalar1=cw[:, j, 0:1], scalar2=None, op0=ALU.mult),
                 reads=[Rm, self.RSV], writes=[Ry])
            for kk in (1, 2):
                K.op("dve", lambda e, j=j, kk=kk: e.scalar_tensor_tensor(out=y, in0=m[:, kk:kk + 2048], scalar=cw[:, j, kk:kk + 1],
                                                                        in1=y, op0=ALU.mult, op1=ALU.add),
                     reads=[Rm, self.RSV, Ry], writes=[Ry])
            K.op("dve", lambda e: e.tensor_tensor(out=y, in0=y, in1=cbuf, op=ALU.mult), reads=[Ry, Rcb], writes=[Ry])
            K.op("act", lambda e: e.activation(out=ysq, in_=y, func=AF.Square), reads=[Ry], writes=[Rysq])
            for tb in range(4):
                a = tb % 2
                bb = 6 + a
                K.op("pe", lambda e, bb=bb, tb=tb: e.matmul(self.ps[bb][:], lhsT=self.BD[:], rhs=ysq[:, tb * 512:(tb + 1) * 512], start=True, stop=True),
                     reads=[Rysq, self.Rconst], writes=[self.Rps[bb]])
                K.op("act", lambda e, bb=bb, a=a: e.activation(out=rs[a], in_=self.ps[bb][:], func=AF.Sqrt, scale=1.0 / 64, bias=EPS),
                     reads=[self.Rps[bb]], writes=[Rrs[a]])
                K.op("dve", lambda e, a=a: e.reciprocal(out=rs[a], in_=rs[a]), reads=[Rrs[a]], writes=[Rrs[a]])
                K.op("dve", lambda e, a=a, tb=tb, j=j: e.scalar_tensor_tensor(out=self.ymT[:, 12 + j, tb * 512:(tb + 1) * 512],
                                                                            in0=y[:, tb * 512:(tb + 1) * 512], scalar=gcol[:, j:j + 1],
                                                                            in1=rs[a], op0=ALU.mult, op1=ALU.mult),
                     reads=[Ry, Rrs[a], self.RSV], writes=[self.RymT[12 + j]])

    def rope_setup(self):
        K = self.K
        EX = self.EX
        posi = EX[:, 0:32].bitcast(I32)
        posf = self.wsv(32, 16, F32, EX)
        ang = self.wsv(64, 256, F32, EX).rearrange("p (t i) -> p t i", t=NT)
        nf = self.wsv(576, 256, F32, EX).rearrange("p (t i) -> p t i", t=NT)
        ni = EX[:, 1088:1600].bitcast(I32).rearrange("p (t i) -> p t i", t=NT)
        msk = self.wsv(1600, 256, F32, EX).rearrange("p (t i) -> p t i", t=NT)
        yy = self.wsv(2112, 256, F32, EX).rearrange("p (t i) -> p t i", t=NT)
        Rr = R("ropetmp")
        K.dma("sp", out=posi, in_=self.d["positions"].rearrange("(t p) o -> p (t o)", p=128), writes=[Rr],
              allow_slow_non_contiguous=True)
        K.op("dve", lambda e: e.tensor_copy(out=posf, in_=posi), reads=[Rr], writes=[Rr])
        for i in range(16):
            inv = float(10000.0 ** (-i / 16.0))
            K.op("dve", lambda e, i=i, inv=inv: e.tensor_scalar(out=ang[:, :, i], in0=posf, scalar1=inv, scalar2=None, op0=ALU.mult),
                 reads=[Rr], writes=[Rr])
        TWO_PI = 2.0 * math.pi
        C1 = 6.28125
        C2 = TWO_PI - C1
        K.op("dve", lambda e: e.tensor_scalar(out=nf, in0=ang, scalar1=1.0 / TWO_PI, scalar2=None, op0=ALU.mult), reads=[Rr], writes=[Rr])
        K.op("dve", lambda e: e.tensor_copy(out=ni, in_=nf), reads=[Rr], writes=[Rr])
        K.op("dve", lambda e: e.tensor_copy(out=nf, in_=ni), reads=[Rr], writes=[Rr])
        K.op("dve", lambda e: e.scalar_tensor_tensor(out=ang, in0=nf, scalar=-C1, in1=ang, op0=ALU.mult, op1=ALU.add), reads=[Rr], writes=[Rr])
        K.op("dve", lambda e: e.scalar_tensor_tensor(out=ang, in0=nf, scalar=-C2, in1=ang, op0=ALU.mult, op1=ALU.add), reads=[Rr], writes=[Rr])
        for which, shift in ((1, 0.0), (0, math.pi / 2)):
            K.op("dve", lambda e, shift=shift: e.tensor_scalar(out=yy, in0=ang, scalar1=shift, scalar2=None, op0=ALU.add), reads=[Rr], writes=[Rr])
            for _ in range(2):
                K.op("dve", lambda e: e.tensor_scalar(out=msk, in0=yy, scalar1=math.pi, scalar2=-TWO_PI, op0=ALU.is_gt, op1=ALU.mult), reads=[Rr], writes=[Rr])
                K.op("dve", lambda e: e.tensor_tensor(out=yy, in0=yy, in1=msk, op=ALU.add), reads=[Rr], writes=[Rr])
                K.op("dve", lambda e: e.tensor_scalar(out=msk, in0=yy, scalar1=-math.pi, scalar2=TWO_PI, op0=ALU.is_lt, op1=ALU.mult), reads=[Rr], writes=[Rr])
                K.op("dve", lambda e: e.tensor_tensor(out=yy, in0=yy, in1=msk, op=ALU.add), reads=[Rr], writes=[Rr])
            K.op("dve", lambda e: e.tensor_scalar(out=yy, in0=yy, scalar1=math.pi, scalar2=-math.pi, op0=ALU.min, op1=ALU.max), reads=[Rr], writes=[Rr])
            K.op("act", lambda e, which=which: e.activation(out=self.cs[:, which, :, :], in_=yy, func=AF.Sin), reads=[Rr], writes=[self.Rcs])
        K.barrier()

    def rstd_from_ss(self, ss_ap, n, Rs):
        K = self.K
        K.op("act", lambda e: e.activation(out=ss_ap, in_=ss_ap, func=AF.Sqrt, scale=1.0 / n, bias=EPS), reads=[Rs], writes=[Rs])
        K.op("dve", lambda e: e.reciprocal(out=ss_ap, in_=ss_ap), reads=[Rs], writes=[Rs])

    def rope(self, x, t, nh, tmp, Rx, Rt):
        K = self.K
        cosb = self.cs[:, 0, t, :].unsqueeze(1).to_broadcast([128, nh, 16])
        sinb = self.cs[:, 1, t, :].unsqueeze(1).to_broadcast([128, nh, 16])
        x1 = x[:, :, 0:16]
        x2 = x[:, :, 16:32]
        for i, (a, b) in enumerate(((x1, cosb), (x2, sinb), (x1, sinb), (x2, cosb))):
            K.op("dve", lambda e, i=i, a=a, b=b: e.tensor_tensor(out=tmp[:, i, :, :], in0=a, in1=b, op=ALU.mult),
                 reads=[Rx, self.Rcs], writes=[Rt])
        K.op("dve", lambda e: e.tensor_tensor(out=x1, in0=tmp[:, 0, :, :], in1=tmp[:, 1, :, :], op=ALU.subtract), reads=[Rt], writes=[Rx])
        K.op("dve", lambda e: e.tensor_tensor(out=x2, in0=tmp[:, 2, :, :], in1=tmp[:, 3, :, :], op=ALU.add), reads=[Rt], writes=[Rx])

    def mla_stage(self, l):
        K = self.K
        d = self.d
        hT = self.hT
        EX = self.EX
        grow = self.grow
        SV = self.SV
        Rg = self.Rgrow
        K.dma("sp", out=grow[:, 0:256], in_=d["mla_q_norm"][l].partition_broadcast(128), writes=[Rg])
        K.dma("sp", out=grow[:, 256:384], in_=d["mla_kv_norm"][l].partition_broadcast(128), writes=[Rg])
        K.dma("sp", out=grow[:, 384:480], in_=d["mla_q_head_norm"][l].partition_broadcast(128), writes=[Rg])
        K.dma("sp", out=grow[:, 480:576], in_=d["mla_k_head_norm"][l].partition_broadcast(128), writes=[Rg])
        K.dma("sp", out=SV[:, 0:512], in_=d["mla_out_norm"][l].partition_broadcast(128), writes=[self.RSV])
        K.op("dve", lambda e: e.tensor_scalar(out=grow[:, 384:480], in0=grow[:, 384:480], scalar1=float(96 ** -0.5), scalar2=None, op0=ALU.mult),
             reads=[Rg], writes=[Rg])
        gq = grow[:, 0:256]
        gkv = grow[:, 256:384]
        gqh = grow[:, 384:480]
        gkh = grow[:, 480:576]
        Winv = d["w_in"][l].rearrange("(k p) c -> p k c", p=128)
        wm = self.wsv(0, 3328).rearrange("p (k c) -> p k c", k=8)
        qnkT = self.wsv(3328, 6144).rearrange("p (j s) -> p j s", j=3)
        kpe = self.wsv(9472, 512, F32).rearrange("p (t i) -> p t i", t=NT)
        Rwm, RqnkT, Rkpe = R("wm"), R("qnkT"), R("kpe")
        K.dma("pool", out=wm, in_=Winv[:, :, C_QL:C_QL + 416], writes=[Rwm])
        qn = [self.wsv(a * 384, 384, BF16, EX) for a in range(2)]
        Rqn = [R("qn0"), R("qn1")]
        ssa = self.ss
        Rss = self.Rss
        SVs = self.st2
        ssa = [SVs[:, 2 * a:2 + 2 * a] for a in range(2)]
        Rssa = [R("ssa0"), R("ssa1")]

        def A_mm(t):
            b = t % 2
            for k in range(8):
                K.op("pe", lambda e, k=k: e.matmul(self.ps[b][:, 0:416], lhsT=hT[:, k, t * 128:(t + 1) * 128], rhs=wm[:, k, :],
                                                  start=(k == 0), stop=(k == 7)),
                     reads=[self.RhT[t // 4], Rwm], writes=[self.Rps[b]])

        def A_el(t):
            a = t % 2
            b = t % 2
            K.op("act", lambda e: e.activation(out=self.junk[:, 0:256], in_=self.ps[b][:, 0:256], func=AF.Square, accum_out=ssa[a][:, 0:1]),
                 reads=[self.Rps[b]], writes=[Rssa[a]])
            K.op("act", lambda e: e.activation(out=self.junk[:, 256:384], in_=self.ps[b][:, 256:384], func=AF.Square, accum_out=ssa[a][:, 1:2]),
                 reads=[self.Rps[b]], writes=[Rssa[a]])
            self.rstd_from_ss(ssa[a][:, 0:1], 256, Rssa[a])
            self.rstd_from_ss(ssa[a][:, 1:2], 128, Rssa[a])
            K.op("dve", lambda e: e.scalar_tensor_tensor(out=qn[a][:, 0:256], in0=self.ps[b][:, 0:256], scalar=ssa[a][:, 0:1], in1=gq,
                                                         op0=ALU.mult, op1=ALU.mult), reads=[self.Rps[b], Rssa[a], Rg], writes=[Rqn[a]])
            K.op("dve", lambda e: e.scalar_tensor_tensor(out=qn[a][:, 256:384], in0=self.ps[b][:, 256:384], scalar=ssa[a][:, 1:2], in1=gkv,
                                                         op0=ALU.mult, op1=ALU.mult), reads=[self.Rps[b], Rssa[a], Rg], writes=[Rqn[a]])
            K.op("act", lambda e: e.activation(out=kpe[:, t, :], in_=self.ps[b][:, 384:416], func=AF.Copy), reads=[self.Rps[b]], writes=[Rkpe])

        def A_tr(t):
            a = t % 2
            bt = 2 + t % 2
            pb = self.ps[bt][:].bitcast(BF16).rearrange("p (c k) -> p c k", c=8)
            for c in range(3):
                K.op("pe", lambda e, c=c: e.transpose(out=pb[:, c, :], in_=qn[a][:, c * 128:(c + 1) * 128], identity=self.ident[:]),
                     reads=[Rqn[a], self.Rconst], writes=[self.Rps[bt]])
            K.op("act", lambda e: e.activation(out=qnkT[:, :, t * 128:(t + 1) * 128], in_=pb[:, 0:3, :], func=AF.Copy),
                 reads=[self.Rps[bt]], writes=[RqnkT])

        for t in range(NT + 2):
            if t < NT:
                A_mm(t)
            if 0 <= t - 1 < NT:
                A_el(t - 1)
            if 0 <= t - 2 < NT:
                A_tr(t - 2)
        K.barrier()
        qT = self.hT
        kT = self.wsv(10496, 16384).rearrange("p (h s) -> p h s", h=8)
        v1 = self.wsv(26880, 8320).rearrange("p (t h c) -> p t h c", t=NT, h=8)
        RqT, RkT, Rv1 = R("qT"), R("kT"), R("v1")
        wuq = self.wsv(0, 1536, BF16, EX).rearrange("p (j c) -> p j c", j=2)
        wukv = self.wsv(1536, 1024, BF16, EX)
        osb = self.wsv(2560, 8192, BF16, EX).rearrange("p (t h c) -> p t h c", t=NT, h=8)
        ET = [self.wsv(10752 + a * 512, 512, BF16, EX) for a in range(4)]
        Rwu, Rosb = R("wu"), R("osb")
        RET = [R(f"ET{a}") for a in range(4)]
        K.dma("pool", out=wuq, in_=d["mla_w_uq"][l].rearrange("(j p) c -> p j c", p=128), writes=[Rwu])
        K.dma("pool", out=wukv, in_=d["mla_w_ukv"][l], writes=[Rwu])
        K.op("pool", lambda e: e.memset(v1[:, :, :, 64:65], 1.0), writes=[Rv1])

        def f4(base, off, n):
            return self.wsv(off, n, F32, base).rearrange("p (h c) -> p h c", h=4)
        tq = [f4(self.WS, 0, 384), f4(EX, 3072, 384)]
        tk = [f4(self.WS, 768, 384), f4(EX, 3840, 384)]
        tsq = [f4(self.WS, 1536, 384), f4(EX, 4608, 384)]
        qkbq = [self.wsv(2304, 384).rearrange("p (h c) -> p h c", h=4), self.wsv(5376, 384, BF16, EX).rearrange("p (h c) -> p h c", h=4)]
        qkbk = [self.wsv(2688, 384).rearrange("p (h c) -> p h c", h=4), self.wsv(5760, 384, BF16, EX).rearrange("p (h c) -> p h c", h=4)]
        trope = [self.wsv(2560, 256, F32, EX).rearrange("p (i h c) -> p i h c", i=4, h=4),
                 self.wsv(6144, 256, F32, EX).rearrange("p (i h c) -> p i h c", i=4, h=4)]
        Rtq, Rtk, Rtsq = [R("tq0"), R("tq1")], [R("tk0"), R("tk1")], [R("tsq0"), R("tsq1")]
        Rqkbq, Rqkbk, Rtrope = [R("qkbq0"), R("qkbq1")], [R("qkbk0"), R("qkbk1")], [R("trope0"), R("trope1")]
        s4q = [SVs[:, 8 + 4 * p:12 + 4 * p] for p in range(2)]
        s4k = [SVs[:, 16 + 4 * p:20 + 4 * p] for p in range(2)]
        s1 = [SVs[:, 24 + p:25 + p] for p in range(2)]
        Rs4q, Rs4k, Rs1 = [R("s4q0"), R("s4q1")], [R("s4k0"), R("s4k1")], [R("s10"), R("s11")]

        def B_mm(u):
            t, hh = u // 2, u % 2
            bq = u % 2
            bk = 2 + u % 2
            for j in range(2):
                K.op("pe", lambda e, j=j: e.matmul(self.ps[bq][:, 0:384], lhsT=qnkT[:, j, t * 128:(t + 1) * 128],
                                                  rhs=wuq[:, j, hh * 384:(hh + 1) * 384], start=(j == 0), stop=(j == 1)),
                     reads=[RqnkT, Rwu], writes=[self.Rps[bq]])
            K.op("pe", lambda e: e.matmul(self.ps[bk][:], lhsT=qnkT[:, 2, t * 128:(t + 1) * 128],
                                          rhs=wukv[:, hh * 512:(hh + 1) * 512], start=True, stop=True),
                 reads=[RqnkT, Rwu], writes=[self.Rps[bk]])

        def B_el(u):
            t, hh = u // 2, u % 2
            p = u % 2
            bq = u % 2
            bk = 2 + u % 2
            tp = t % 2
            if hh == 0:
                K.op("act", lambda e: e.activation(out=self.junk[:, 0:32], in_=kpe[:, t, :], func=AF.Square, accum_out=s1[tp]), reads=[Rkpe], writes=[Rs1[tp]])
            psq = self.ps[bq][:, 0:384].rearrange("p (h c) -> p h c", h=4)
            pskv = self.ps[bk][:].rearrange("p (h c) -> p h c", h=4)
            K.op("act", lambda e: e.activation(out=tsq[p], in_=psq, func=AF.Square), reads=[self.Rps[bq]], writes=[Rtsq[p]])
            K.op("dve", lambda e: e.tensor_reduce(out=s4q[p], in_=tsq[p], axis=AX.X, op=ALU.add), reads=[Rtsq[p]], writes=[Rs4q[p]])
            self.rstd_from_ss(s4q[p], 96, Rs4q[p])
            K.op("dve", lambda e: e.tensor_tensor(out=tq[p], in0=psq, in1=s4q[p].unsqueeze(2).to_broadcast([128, 4, 96]), op=ALU.mult),
                 reads=[self.Rps[bq], Rs4q[p]], writes=[Rtq[p]])
            K.op("dve", lambda e: e.tensor_tensor(out=tq[p], in0=tq[p], in1=gqh.unsqueeze(1).to_broadcast([128, 4, 96]), op=ALU.mult),
                 reads=[Rtq[p], Rg], writes=[Rtq[p]])
            self.rope(tq[p][:, :, 64:96], t, 4, trope[p], Rtq[p], Rtrope[p])
            K.op("act", lambda e: e.activation(out=qkbq[p], in_=tq[p], func=AF.Copy), reads=[Rtq[p]], writes=[Rqkbq[p]])
            K.op("act", lambda e: e.activation(out=v1[:, t, hh * 4:hh * 4 + 4, 0:64], in_=pskv[:, :, 64:128], func=AF.Copy),
                 reads=[self.Rps[bk]], writes=[Rv1])
            K.op("act", lambda e: e.activation(out=tsq[p][:, :, 0:64], in_=pskv[:, :, 0:64], func=AF.Square), reads=[self.Rps[bk]], writes=[Rtsq[p]])
            K.op("dve", lambda e: e.tensor_reduce(out=s4k[p], in_=tsq[p][:, :, 0:64], axis=AX.X, op=ALU.add), reads=[Rtsq[p]], writes=[Rs4k[p]])
            K.op("dve", lambda e: e.tensor_scalar(out=s4k[p], in0=s4k[p], scalar1=s1[tp], scalar2=None, op0=ALU.add), reads=[Rs4k[p], Rs1[tp]], writes=[Rs4k[p]])
            self.rstd_from_ss(s4k[p], 96, Rs4k[p])
            K.op("dve", lambda e: e.tensor_tensor(out=tk[p][:, :, 0:64], in0=pskv[:, :, 0:64], in1=s4k[p].unsqueeze(2).to_broadcast([128, 4, 64]), op=ALU.mult),
                 reads=[self.Rps[bk], Rs4k[p]], writes=[Rtk[p]])
            K.op("dve", lambda e: e.tensor_tensor(out=tk[p][:, :, 64:96], in0=kpe[:, t, :].unsqueeze(1).to_broadcast([128, 4, 32]),
                                                  in1=s4k[p].unsqueeze(2).to_broadcast([128, 4, 32]), op=ALU.mult),
                 reads=[Rkpe, Rs4k[p]], writes=[Rtk[p]])
            K.op("dve", lambda e: e.tensor_tensor(out=tk[p], in0=tk[p], in1=gkh.unsqueeze(1).to_broadcast([128, 4, 96]), op=ALU.mult),
                 reads=[Rtk[p], Rg], writes=[Rtk[p]])
            self.rope(tk[p][:, :, 64:96], t, 4, trope[p], Rtk[p], Rtrope[p])
            K.op("act", lambda e: e.activation(out=qkbk[p], in_=tk[p], func=AF.Copy), reads=[Rtk[p]], writes=[Rqkbk[p]])

        def B_tr(u):
            t, hh = u // 2, u % 2
            p = u % 2
            for (src, Rsrc, bnk, dstT, RdstT) in ((qkbq[p], Rqkbq[p], 4 + u % 2, qT, RqT), (qkbk[p], Rqkbk[p], 6 + u % 2, kT, RkT)):
                pb = self.ps[bnk][:].bitcast(BF16).rearrange("p (c k) -> p c k", c=8)
                for h in range(4):
                    K.op("pe", lambda e, h=h: e.transpose(out=pb[0:96, h, :], in_=src[:, h, :], identity=self.ident[:]),
                         reads=[Rsrc, self.Rconst], writes=[self.Rps[bnk]])
                K.op("act", lambda e: e.activation(out=dstT[0:96, hh * 4:hh * 4 + 4, t * 128:(t + 1) * 128], in_=pb[0:96, 0:4, :], func=AF.Copy),
                     reads=[self.Rps[bnk]], writes=[RdstT])

        NU = 2 * NT
        for v in range(NT + 1):
            if v < NT:
                B_mm(2 * v)
                B_mm(2 * v + 1)
            if v >= 1:
                B_tr(2 * v - 2)
                B_tr(2 * v - 1)
            if v < NT:
                K.play_interleaved([K.record(lambda: B_el(2 * v)), K.record(lambda: B_el(2 * v + 1))])
        K.barrier()
        rinv = self.ss[:, 16:20]
        its = [(h, qb, kt) for h in range(8) for qb in range(4) for kt in range(NT)]

        def emit_s(i):
            h, qb, kt = its[i]
            sb_ = i % 4
            K.op("pe", lambda e: e.matmul(self.ps[sb_][:], lhsT=kT[0:96, h, kt * 128:(kt + 1) * 128],
                                          rhs=qT[0:96, h, qb * 512:(qb + 1) * 512], start=True, stop=True),
                 reads=[RkT, RqT], writes=[self.Rps[sb_]])
            K.op("act", lambda e: e.activation(out=ET[sb_], in_=self.ps[sb_][:], func=AF.Exp),
                 reads=[self.Rps[sb_]], writes=[RET[sb_]])

        osT = [self.wsv(35200 + a * 1024, 512, F32) for a in range(1)]
        RosT = [R("osT0")]

        def emit_pv(i):
            h, qb, kt = its[i]
            eb = i % 4
            ob_ = 4 + (i // NT) % 2
            K.op("pe", lambda e: e.matmul(self.ps[ob_][0:65, :], lhsT=v1[:, kt, h, :], rhs=ET[eb], start=(kt == 0), stop=(kt == NT - 1)),
                 reads=[RET[eb], Rv1], writes=[self.Rps[ob_]])
            if kt == NT - 1:
                K.op("dve", lambda e: e.tensor_copy(out=osT[0][0:65, :], in_=self.ps[ob_][0:65, :]), reads=[self.Rps[ob_]], writes=[RosT[0]])
                pt = self.ps[6 + (i // NT) % 2][:, 0:260].rearrange("p (q c) -> p q c", q=4)
                Rpt = self.Rps[6 + (i // NT) % 2]
                for qt in range(4):
                    K.op("pe", lambda e, qt=qt: e.transpose(out=pt[:, qt, :], in_=osT[0][0:65, qt * 128:(qt + 1) * 128], identity=self.identf[0:65, 0:65]),
                         reads=[RosT[0], self.Rconst], writes=[Rpt])
                K.op("dve", lambda e: e.reciprocal(out=rinv, in_=pt[:, :, 64]), reads=[Rpt], writes=[Rss])
                K.op("dve", lambda e: e.tensor_tensor(out=osb[:, qb * 4:qb * 4 + 4, h, :], in0=pt[:, :, 0:64],
                                                      in1=rinv.unsqueeze(2).to_broadcast([128, 4, 64]), op=ALU.mult),
                     reads=[Rpt, Rss], writes=[Rosb])

        emit_s(0)
        emit_s(1)
        for i in range(len(its)):
            if i + 2 < len(its):
                emit_s(i + 2)
            emit_pv(i)
        K.barrier()
        to = self.wsv(0, 512, F32).rearrange("p (h c) -> p h c", h=8)
        ob = self.wsv(1024, 512)
        Rto, Rob = R("to"), R("ob")
        s8 = self.ss[:, 20:28]
        gout = SV[:, 0:512].rearrange("p (h c) -> p h c", h=8)
        for t in range(NT):
            K.op("act", lambda e, t=t: e.activation(out=to, in_=osb[:, t, :, :], func=AF.Square), reads=[Rosb], writes=[Rto])
            K.op("dve", lambda e: e.tensor_reduce(out=s8, in_=to, axis=AX.X, op=ALU.add), reads=[Rto], writes=[Rss])
            self.rstd_from_ss(s8, 64, Rss)
            K.op("dve", lambda e, t=t: e.tensor_tensor(out=to, in0=osb[:, t, :, :], in1=s8.unsqueeze(2).to_broadcast([128, 8, 64]), op=ALU.mult),
                 reads=[Rosb, Rss], writes=[Rto])
            K.op("dve", lambda e: e.tensor_tensor(out=ob.rearrange("p (h c) -> p h c", h=8), in0=to, in1=gout, op=ALU.mult),
                 reads=[Rto, self.RSV], writes=[Rob])
            bt = t % 2
            pb = self.ps[bt][:].bitcast(BF16).rearrange("p (c k) -> p c k", c=8)
            for c in range(4):
                K.op("pe", lambda e, c=c, pb=pb: e.transpose(out=pb[:, c, :], in_=ob[:, c * 128:(c + 1) * 128], identity=self.ident[:]),
                     reads=[Rob, self.Rconst], writes=[self.Rps[bt]])
            K.op("act", lambda e, t=t, pb=pb: e.activation(out=self.ymT[:, 8:12, t * 128:(t + 1) * 128], in_=pb[:, 0:4, :], func=AF.Copy),
                 reads=[self.Rps[bt]], writes=[self.RymT[8], self.RymT[9], self.RymT[10], self.RymT[11]])

    def ssd_stage(self, l):
        K = self.K
        d = self.d
        hT = self.hT
        EX = self.EX
        SV = self.SV
        RSV = self.RSV
        Winv = d["w_in"][l].rearrange("(k p) c -> p k c", p=128)
        dtb_row = SV[:, 0:32]
        a_row = SV[:, 32:64]
        d_row = SV[:, 64:80]
        scw = SV[:, 96:156].rearrange("p (c k) -> p c k", c=12)
        scb = SV[:, 160:172]
        ssn = SV[:, 176:177]
        K.dma("sp", out=dtb_row, in_=d["ssd_dt_bias"][l].partition_broadcast(128), writes=[RSV])
        K.dma("sp", out=a_row, in_=d["ssd_a_log"][l].partition_broadcast(128), writes=[RSV])
        K.dma("sp", out=d_row, in_=d["ssd_d"][l].partition_broadcast(128), writes=[RSV])
        for ci in range(12):
            K.dma("sp", out=scw[:, ci, :], in_=d["ssd_conv_w"][l][:, ci * 128:(ci + 1) * 128].rearrange("k p -> p k"), writes=[RSV],
                  allow_slow_non_contiguous=True)
        K.dma("sp", out=scb, in_=d["ssd_conv_b"][l].rearrange("(c p) -> p c", p=128), writes=[RSV], allow_slow_non_contiguous=True)
        K.dma("sp", out=self.grow[:], in_=d["ssd_norm"][l].partition_broadcast(128), writes=[self.Rgrow])

        def f3(off, n3=NT):
            return self.wsv(off, n3 * 32, F32, EX).rearrange("p (t c) -> p t c", t=n3)
        dt = f3(0)
        dta = f3(1024)
        Pc = f3(2048)
        Tend = f3(3072, 17)
        eoff = f3(4160)
        cd = f3(5184)
        Wdt = self.wsv(6208, 256, BF16, EX).rearrange("p (k c) -> p k c", k=8)
        Rdt = R("ssd_small")
        RWdt = R("Wdt")
        K.dma("pool", out=Wdt, in_=Winv[:, :, C_DT:C_DT + 32], writes=[RWdt])
        K.op("act", lambda e: e.activation(out=a_row, in_=a_row, func=AF.Exp), reads=[RSV], writes=[RSV])
        K.op("dve", lambda e: e.tensor_scalar(out=a_row, in0=a_row, scalar1=-1.0, scalar2=None, op0=ALU.mult), reads=[RSV], writes=[RSV])
        for t in range(NT):
            b = t % 2
            for k in range(8):
                K.op("pe", lambda e, k=k, b=b, t=t: e.matmul(self.ps[b][:, 0:32], lhsT=hT[:, k, t * 128:(t + 1) * 128], rhs=Wdt[:, k, :],
                                                          start=(k == 0), stop=(k == 7)),
                     reads=[self.RhT[t // 4], RWdt], writes=[self.Rps[b]])
            K.op("dve", lambda e, b=b, t=t: e.tensor_tensor(out=dt[:, t, :], in0=self.ps[b][:, 0:32], in1=dtb_row, op=ALU.add),
                 reads=[self.Rps[b], RSV], writes=[Rdt])
        K.op("act", lambda e: e.activation(out=dt, in_=dt, func=AF.Exp), reads=[Rdt], writes=[Rdt])
        K.op("act", lambda e: e.activation(out=dt, in_=dt, func=AF.Ln, bias=1.0, scale=1.0), reads=[Rdt], writes=[Rdt])
        K.op("dve", lambda e: e.tensor_tensor(out=dta, in0=dt, in1=a_row.unsqueeze(1).to_broadcast([128, NT, 32]), op=ALU.mult),
             reads=[Rdt, RSV], writes=[Rdt])
        K.op("dve", lambda e: e.memset(Tend[:, 0, :], 0.0), writes=[Rdt])
        for t in range(NT):
            ba = 2 + (t % 2) * 2
            bb = ba + 1
            K.op("pe", lambda e, ba=ba, t=t: e.matmul(self.ps[ba][:, 0:32], lhsT=self.tri[:], rhs=dta[:, t, :], start=True, stop=True),
                 reads=[Rdt, self.Rconst], writes=[self.Rps[ba]])
            K.op("pe", lambda e, bb=bb, t=t: e.matmul(self.ps[bb][:, 0:32], lhsT=self.onesf[:], rhs=dta[:, t, :], start=True, stop=True),
                 reads=[Rdt, self.Rconst], writes=[self.Rps[bb]])
            K.op("dve", lambda e, ba=ba, t=t: e.tensor_tensor(out=Pc[:, t, :], in0=self.ps[ba][:, 0:32], in1=Tend[:, t, :], op=ALU.add),
                 reads=[self.Rps[ba], Rdt], writes=[Rdt])
            K.op("dve", lambda e, bb=bb, t=t: e.tensor_tensor(out=Tend[:, t + 1, :], in0=self.ps[bb][:, 0:32], in1=Tend[:, t, :], op=ALU.add),
                 reads=[self.Rps[bb], Rdt], writes=[Rdt])
        K.op("dve", lambda e: e.tensor_tensor(out=eoff[:, :, 0:16], in0=Pc[:, :, 0:16], in1=Tend[:, 0:NT, 0:16], op=ALU.subtract), reads=[Rdt], writes=[Rdt])
        K.op("dve", lambda e: e.tensor_tensor(out=Pc[:, :, 16:32], in0=Pc[:, :, 16:32], in1=dta[:, :, 16:32], op=ALU.subtract), reads=[Rdt], writes=[Rdt])
        K.op("dve", lambda e: e.tensor_tensor(out=eoff[:, :, 16:32], in0=Tend[:, 1:NT + 1, 16:32], in1=Pc[:, :, 16:32], op=ALU.subtract), reads=[Rdt], writes=[Rdt])
        K.op("dve", lambda e: e.tensor_tensor(out=cd, in0=Tend[:, 1:NT + 1, :], in1=Tend[:, 0:NT, :], op=ALU.subtract), reads=[Rdt], writes=[Rdt])
        K.op("act", lambda e: e.activation(out=eoff, in_=eoff, func=AF.Exp), reads=[Rdt], writes=[Rdt])
        K.op("act", lambda e: e.activation(out=cd, in_=cd, func=AF.Exp), reads=[Rdt], writes=[Rdt])
        nb = dta
        K.op("dve", lambda e: e.tensor_scalar(out=nb[:, :, 0:16], in0=Pc[:, :, 0:16], scalar1=-1.0, scalar2=None, op0=ALU.mult), reads=[Rdt], writes=[Rdt])
        K.op("dve", lambda e: e.tensor_scalar(out=nb[:, :, 16:32], in0=Pc[:, :, 16:32], scalar1=-1.0, scalar2=None, op0=ALU.mult), reads=[Rdt], writes=[Rdt])
        K.barrier()

        xs_g = self.wsv(0, 8192).rearrange("p (t c) -> p t c", t=NT)
        BT = self.wsv(8192, 2048)
        CT = self.wsv(10240, 2048)
        Btok = self.wsv(12288, 2048).rearrange("p (t c) -> p t c", t=NT)
        Wz = self.wsv(14336, 4096).rearrange("p (k c) -> p k c", k=8)
        ybr = self.X[:, 8:16, :].rearrange("p a c -> p (a c)").bitcast(BF16)
        pres = [self.wsv(18432, 2052), self.wsv(0, 2052, BF16, ybr)]
        dgs = [self.wsv(22536 + a * 640, 640).rearrange("p (k c) -> p k c", k=5) for a in range(2)]
        xsTs = [self.wsv(26632, 2048), self.wsv(8200, 2048, BF16, ybr)]
        wch = [self.wsv(28680 + a * 1024, 1024).rearrange("p (k c) -> p k c", k=8) for a in range(2)]
        MTall = self.wsv(18432, 16384).rearrange("p (t h c) -> p t h c", t=NT, h=8)
        Dbuf = [self.wsv(6464, 1024, F32, EX).rearrange("p (h c) -> p h c", h=8),
                self.wsv(34816, 1024, F32).rearrange("p (h c) -> p h c", h=8)]
        CBm = self.wsv(10560, 128, F32, EX)
        CBm2 = self.wsv(8512, 128, F32, EX)
        ecol = self.wsv(8768, 128, F32, EX).rearrange("p (t h) -> p t h", t=NT)
        xd = [self.wsv(9536, 512, BF16, EX), self.wsv(13376, 512, BF16, EX)]
        xdw = [self.wsv(10048, 512, BF16, EX), self.wsv(9024, 512, BF16, EX)]
        t1s = [self.wsv(18432, 512, F32), self.wsv(19456, 512, F32)]
        szs = [self.wsv(20480, 512, F32), self.wsv(21504, 512, F32)]
        yns = [self.wsv(22528, 512), self.wsv(23040, 512)]
        prev = self.wsv(10816, 512, F32, EX)
        prevb = self.wsv(11840, 512, BF16, EX)
        ysum = self.wsv(12352, 512, F32, EX)
        ybwd = self.X[:, 8:16, :].rearrange("p a (b c) -> p (a b) c", b=2)
        Rxs, RBT, RCT, RBtok, RWz = R("xs_g"), R("BT"), R("CT"), R("Btok"), R("Wz")
        Rpres, Rdgs, RxsTs = [R("pre0"), R("pre1")], [R("dg0"), R("dg1")], [R("xsT0"), R("xsT1")]
        Rwch = [R("wch0"), R("wch1")]
        Rprev, Rprevb, Rysum = (R("prev"), R("prevb"), R("ysum"))
        Rxd = [R("xd0"), R("xd1")]
        Rxdw = [R("xdw0"), R("xdw1")]
        Rybwd = [R(f"ybwd{t}") for t in range(NT)]
        Rt1s = [R("t1s0"), R("t1s1")]
        Rszs = [R("szs0"), R("szs1")]
        Ryns = [R("yns0"), R("yns1")]
        RD = [R("Dbuf0"), R("Dbuf1")]
        RMT = [R(f"MT{t}") for t in range(NT)]
        RMT2 = [R(f"MTb{t}") for t in range(NT)]
        Recol = R("ecol")
        RCBm = [R("CBm0"), R("CBm1")]
        CBms = [CBm, CBm2]
        v8 = lambda ap: ap.rearrange("p (h c) -> p h c", h=8)

        for g in range(2):
            K.dma("pool", out=Wz, in_=Winv[:, :, C_Z + g * 512:C_Z + (g + 1) * 512], writes=[RWz])
            for pp in range(2):
                K.op("pool", lambda e, pp=pp: e.memset(pres[pp][:, 0:2], 0.0), writes=[Rpres[pp]])
                K.op("pool", lambda e, pp=pp: e.memset(pres[pp][:, 2050:2052], 0.0), writes=[Rpres[pp]])
            chunks = [g * 4 + i for i in range(4)] + [8 + g, 10 + g]

            def cdst(n_):
                sl = n_ % 2
                if n_ < 4:
                    return xsTs[sl], RxsTs[sl]
                elif n_ == 4:
                    return BT, RBT
                return CT, RCT

            def stA(n_):
                ci = chunks[n_]
                sl = n_ % 2
                pre, Rpre = pres[sl], Rpres[sl]
                K.dma("pool", out=wch[sl], in_=Winv[:, :, C_X + ci * 128:C_X + (ci + 1) * 128], writes=[Rwch[sl]])
                for tb in range(4):
                    b = tb % 2
                    for k in range(8):
                        K.op("pe", lambda e, k=k: e.matmul(self.ps[b][:], lhsT=wch[sl][:, k, :], rhs=hT[:, k, tb * 512:(tb + 1) * 512],
                                                          start=(k == 0), stop=(k == 7)),
                             reads=[Rwch[sl], self.RhT[tb]], writes=[self.Rps[b]])
                    K.op("act", lambda e: e.activation(out=pre[:, 2 + tb * 512:2 + (tb + 1) * 512], in_=self.ps[b][:], func=AF.Copy),
                         reads=[self.Rps[b]], writes=[Rpre])

            def stB(n_):
                ci = chunks[n_]
                sl = n_ % 2
                pre, Rpre, dg, Rdg = pres[sl], Rpres[sl], dgs[sl], Rdgs[sl]
                for kk in range(5):
                    K.op("dve", lambda e, kk=kk: e.tensor_scalar(out=dg[:, kk, :], in0=self.ident[:], scalar1=scw[:, ci, kk:kk + 1], scalar2=None, op0=ALU.mult),
                         reads=[self.Rconst, RSV], writes=[Rdg])
                dst, Rdst = cdst(n_)
                for tb in range(4):
                    bc = 4 + tb % 2
                    for kk in range(5):
                        K.op("pe", lambda e, kk=kk: e.matmul(self.ps[bc][:], lhsT=dg[:, kk, :], rhs=pre[:, tb * 512 + kk:tb * 512 + kk + 512],
                                                            start=(kk == 0), stop=(kk == 4)),
                             reads=[Rdg, Rpre], writes=[self.Rps[bc]])
                    K.op("act", lambda e: e.activation(out=dst[:, tb * 512:(tb + 1) * 512], in_=self.ps[bc][:], func=AF.Silu, bias=scb[:, ci:ci + 1]),
                         reads=[self.Rps[bc], RSV], writes=[Rdst])

            def stC(n_):
                if n_ > 4:
                    return
                dst, Rdst = cdst(n_)
                for rnd in range(2):
                    bt = 2 + rnd
                    pb = self.ps[bt][:].bitcast(BF16).rearrange("p (c k) -> p c k", c=8)
                    for i in range(8):
                        tt = rnd * 8 + i
                        K.op("pe", lambda e, i=i, tt=tt: e.transpose(out=pb[:, i, :], in_=dst[:, tt * 128:(tt + 1) * 128], identity=self.ident[:]),
                             reads=[Rdst, self.Rconst], writes=[self.Rps[bt]])
                    if n_ < 4:
                        K.op("act", lambda e: e.activation(out=xs_g[:, rnd * 8:(rnd + 1) * 8, n_ * 128:(n_ + 1) * 128], in_=pb, func=AF.Copy),
                             reads=[self.Rps[bt]], writes=[Rxs])
                    else:
                        K.op("act", lambda e: e.activation(out=Btok[:, rnd * 8:(rnd + 1) * 8, :], in_=pb, func=AF.Copy),
                             reads=[self.Rps[bt]], writes=[RBtok])

            for n_ in range(len(chunks) + 2):
                if n_ < len(chunks):
                    stA(n_)
                if 0 <= n_ - 1 < len(chunks):
                    stB(n_ - 1)
                if 0 <= n_ - 2 < len(chunks):
                    stC(n_ - 2)
            K.barrier()

            for dirn in (1, 0):
                cb = dirn * 16 + g * 8
                mask = self.triT if dirn == 1 else self.tri
                order = list(range(NT - 1, -1, -1)) if dirn == 1 else list(range(NT))
                col = 0 if dirn == 1 else 127
                def mt_emit(i, order=order, col=col):
                    t = order[i]
                    a = i % 2
                    K.op("dve", lambda e: e.tensor_tensor(out=MTall[:, t, 0:5, :], in0=Dbuf[a][:, 0:5, :], in1=CBms[a].unsqueeze(1).to_broadcast([128, 5, 128]), op=ALU.mult),
                         reads=[RD[a], RCBm[a]], writes=[RMT[t]])
                    K.op("pool", lambda e: e.tensor_tensor(out=MTall[:, t, 5:8, :], in0=Dbuf[a][:, 5:8, :], in1=CBms[a].unsqueeze(1).to_broadcast([128, 3, 128]), op=ALU.mult),
                         reads=[RD[a], RCBm[a]], writes=[RMT2[t]])
                    K.op("pool", lambda e: e.tensor_copy(out=ecol[:, t, :], in_=Dbuf[a][:, :, col]), reads=[RD[a]], writes=[Recol])

                for i, t in enumerate(order):
                    a = i % 2
                    tok = slice(t * 128, (t + 1) * 128)
                    K.op("pe", lambda e, tok=tok: e.matmul(self.ps[0][:, 0:128], lhsT=BT[:, tok], rhs=CT[:, tok], start=True, stop=True),
                         reads=[RBT, RCT], writes=[self.Rps[0]])
                    K.op("dve", lambda e, a=a: e.tensor_tensor(out=CBms[a], in0=self.ps[0][:, 0:128], in1=mask[:], op=ALU.mult),
                         reads=[self.Rps[0], self.Rconst], writes=[RCBm[a]])
                    for h in range(8):
                        bnk = 1 + a * 2 + h // 4
                        dstp = self.ps[bnk][:].rearrange("p (h c) -> p h c", h=4)[:, h % 4, :]
                        K.op("pe", lambda e, h=h, t=t, dstp=dstp: e.matmul(dstp, lhsT=Pc[:, t, cb + h:cb + h + 1].to_broadcast([128, 128]), rhs=self.identf[:],
                                                                         start=True, stop=True),
                             reads=[Rdt, self.Rconst], writes=[self.Rps[bnk]])
                    for h in range(8):
                        bnk = 1 + a * 2 + h // 4
                        src = self.ps[bnk][:].rearrange("p (h c) -> p h c", h=4)[:, h % 4, :]
                        if h in (5, 6, 7):
                            K.op("act", lambda e, h=h, src=src, t=t, a=a: e.activation(out=Dbuf[a][:, h, :], in_=src, func=AF.Abs,
                                                                                     scale=1.0, bias=nb[:, t, cb + h:cb + h + 1]),
                                 reads=[self.Rps[bnk], Rdt], writes=[RD[a]])
                        else:
                            K.op("dve", lambda e, h=h, src=src, t=t, a=a: e.tensor_scalar(out=Dbuf[a][:, h, :], in0=src, scalar1=Pc[:, t, cb + h:cb + h + 1],
                                                                                        scalar2=0.0, op0=ALU.subtract, op1=(ALU.max if dirn == 1 else ALU.min)),
                                 reads=[self.Rps[bnk], Rdt], writes=[RD[a]])
                    K.op("act", lambda e, a=a: e.activation(out=Dbuf[a][:, 0:5, :], in_=Dbuf[a][:, 0:5, :], func=AF.Exp, scale=(-1.0 if dirn == 1 else 1.0)),
                         reads=[RD[a]], writes=[RD[a]])
                    K.op("act", lambda e, a=a: e.activation(out=Dbuf[a][:, 5:8, :], in_=Dbuf[a][:, 5:8, :], func=AF.Exp, scale=-1.0), reads=[RD[a]], writes=[RD[a]])
                    if i >= 1:
                        mt_emit(i - 1)
                mt_emit(NT - 1)

                K.op("dve", lambda e: e.memset(prev, 0.0), writes=[Rprev])
                K.op("dve", lambda e: e.memset(prevb, 0.0), writes=[Rprevb])

                def front2(i, dirn=dirn, cb=cb, order=order):
                    t = order[i]
                    a = i % 2
                    K.op("pool", lambda e: e.tensor_tensor(out=v8(xd[a]), in0=v8(xs_g[:, t, :]),
                                                           in1=dt[:, t, cb:cb + 8].unsqueeze(2).to_broadcast([128, 8, 64]), op=ALU.mult),
                         reads=[Rxs, Rdt], writes=[Rxd[a]])
                    K.op("pool", lambda e: e.tensor_tensor(out=v8(xdw[a]), in0=v8(xd[a]), in1=ecol[:, t, :].unsqueeze(2).to_broadcast([128, 8, 64]), op=ALU.mult),
                         reads=[Rxd[a], Recol], writes=[Rxdw[a]])
                    K.op("pe", lambda e: e.matmul(self.ps[5 + a][:], lhsT=Btok[:, t, :], rhs=xdw[a], start=True, stop=True),
                         reads=[RBtok, Rxdw[a]], writes=[self.Rps[5 + a]])

                def back(i, dirn=dirn, cb=cb, order=order):
                    t = order[i]
                    a = i % 2
                    tok = slice(t * 128, (t + 1) * 128)
                    K.op("pe", lambda e: e.matmul(self.ps[0][:], lhsT=CT[:, tok], rhs=prevb, start=True, stop=True),
                         reads=[RCT, Rprevb], writes=[self.Rps[0]])
                    for h in range(8):
                        K.op("pe", lambda e, h=h: e.matmul(self.ps[7][:, h * 64:(h + 1) * 64], lhsT=MTall[:, t, h, :], rhs=xd[a][:, h * 64:(h + 1) * 64], start=True, stop=True),
                             reads=[RMT[t] if h < 5 else RMT2[t], Rxd[a]], writes=[self.Rps[7]])
                    K.op("dve", lambda e: e.tensor_tensor(out=v8(prev), in0=v8(prev), in1=cd[:, t, cb:cb + 8].unsqueeze(2).to_broadcast([128, 8, 64]), op=ALU.mult),
                         reads=[Rprev, Rdt], writes=[Rprev])
                    K.op("dve", lambda e: e.tensor_tensor(out=prev, in0=prev, in1=self.ps[5 + a][:], op=ALU.add), reads=[Rprev, self.Rps[5 + a]], writes=[Rprev])
                    K.op("act", lambda e: e.activation(out=prevb, in_=prev, func=AF.Copy), reads=[Rprev], writes=[Rprevb])
                    K.op("dve", lambda e: e.tensor_tensor(out=v8(ysum), in0=v8(self.ps[0][:]),
                                                          in1=eoff[:, t, cb:cb + 8].unsqueeze(2).to_broadcast([128, 8, 64]), op=ALU.mult),
                         reads=[self.Rps[0], Rdt], writes=[Rysum])
                    if dirn == 1:
                        K.op("dve", lambda e: e.tensor_tensor(out=ybwd[:, t, :], in0=ysum, in1=self.ps[7][:], op=ALU.add),
                             reads=[Rysum, self.Rps[7]], writes=[Rybwd[t]])
                    else:
                        K.op("dve", lambda e: e.tensor_tensor(out=ysum, in0=ysum, in1=self.ps[7][:], op=ALU.add),
                             reads=[Rysum, self.Rps[7]], writes=[Rysum])
                        K.op("dve", lambda e: e.tensor_tensor(out=ybwd[:, t, :], in0=ybwd[:, t, :], in1=ysum, op=ALU.add),
                             reads=[Rysum, Rybwd[t]], writes=[Rybwd[t]])

                front2(0)
                for i in range(NT):
                    if i + 1 < NT:
                        front2(i + 1)
                    back(i)
            K.barrier()
            for t in range(NT):
                a = t % 2
                tok = slice(t * 128, (t + 1) * 128)
                K.op("pool", lambda e, t=t, a=a: e.tensor_tensor(out=v8(t1s[a]), in0=v8(xs_g[:, t, :]),
                                                                in1=d_row[:, g * 8:(g + 1) * 8].unsqueeze(2).to_broadcast([128, 8, 64]), op=ALU.mult),
                     reads=[Rxs, RSV], writes=[Rt1s[a]])
                K.op("pool", lambda e, t=t, a=a: e.tensor_tensor(out=ybwd[:, t, :], in0=ybwd[:, t, :], in1=t1s[a], op=ALU.add),
                     reads=[Rt1s[a], Rybwd[t]], writes=[Rybwd[t]])
                bz = 6 + a
                for k in range(8):
                    K.op("pe", lambda e, k=k, tok=tok, bz=bz: e.matmul(self.ps[bz][:], lhsT=hT[:, k, tok], rhs=Wz[:, k, :], start=(k == 0), stop=(k == 7)),
                         reads=[self.RhT[t // 4], RWz], writes=[self.Rps[bz]])
                K.op("act", lambda e, a=a, bz=bz: e.activation(out=szs[a], in_=self.ps[bz][:], func=AF.Silu), reads=[self.Rps[bz]], writes=[Rszs[a]])
                K.op("dve", lambda e, t=t, a=a: e.tensor_tensor(out=ybwd[:, t, :], in0=ybwd[:, t, :], in1=szs[a], op=ALU.mult),
                     reads=[Rszs[a], Rybwd[t]], writes=[Rybwd[t]])
                K.op("act", lambda e, t=t, a=a: e.activation(out=t1s[a], in_=ybwd[:, t, :], func=AF.Square, accum_out=self.ss[:, t:t + 1]),
                     reads=[Rybwd[t]], writes=[Rt1s[a], self.Rss])
            self.rstd_from_ss(self.ss[:, 0:NT], 512, self.Rss)
            for t in range(NT):
                a = t % 2
                tok = slice(t * 128, (t + 1) * 128)
                K.op("dve", lambda e, t=t, a=a: e.scalar_tensor_tensor(out=yns[a], in0=ybwd[:, t, :], scalar=self.ss[:, t:t + 1], in1=self.grow[:, g * 512:(g + 1) * 512],
                                                                      op0=ALU.mult, op1=ALU.mult), reads=[Rybwd[t], self.Rss, self.Rgrow], writes=[Ryns[a]])
                bt = 4 + a
                pb = self.ps[bt][:].bitcast(BF16).rearrange("p (c k) -> p c k", c=8)
                for c in range(4):
                    K.op("pe", lambda e, c=c, a=a, pb=pb: e.transpose(out=pb[:, c, :], in_=yns[a][:, c * 128:(c + 1) * 128], identity=self.ident[:]),
                         reads=[Ryns[a], self.Rconst], writes=[self.Rps[bt]])
                K.op("act", lambda e, pb=pb, tok=tok: e.activation(out=self.ymT[:, g * 4:g * 4 + 4, tok], in_=pb[:, 0:4, :], func=AF.Copy),
                     reads=[self.Rps[bt]], writes=[self.RymT[g * 4 + c] for c in range(4)])
            K.barrier()


_INPUT_ORDER = ["x", "positions", "ffn1_norm", "ffn1_w_gate", "ffn1_w_up", "ffn1_w_down", "mix_norm", "w_in",
                "ssd_conv_w", "ssd_conv_b", "ssd_dt_bias", "ssd_a_log", "ssd_d", "ssd_norm",
                "mla_q_norm", "mla_w_uq", "mla_kv_norm", "mla_w_ukv", "mla_q_head_norm", "mla_k_head_norm",
                "mla_out_norm", "conv_w", "conv_out_norm", "w_out",
                "ffn2_norm", "ffn2_w_gate", "ffn2_w_up", "ffn2_w_down"]


def make_in_maps(inputs, cores):
    maps = []
    for b in cores:
        m = {}
        for k in _INPUT_ORDER:
            a = np.asarray(inputs[k])
            if k == "x":
                m[k] = np.ascontiguousarray(a[b])
            elif k == "positions":
                m[k] = np.ascontiguousarray(a[b].reshape(S, 1).astype(np.int32))
            elif k in ("ssd_dt_bias", "ssd_a_log"):
                m[k] = np.ascontiguousarray(a.reshape(DEPTH, 32))
            else:
                m[k] = np.ascontiguousarray(a)
        maps.append(m)
    return maps


def kernel(**inputs):
    prog = Prog()
    nc = prog.build()
    in_maps = make_in_maps(inputs, list(range(8)))
    res = run_bass_kernel_spmd(nc, in_maps, core_ids=list(range(8)))
    return np.stack([np.asarray(r["out"]).reshape(S, D) for r in res.results], axis=0).astype(np.float32)
```

```python
import math
import numpy as np
from contextlib import ExitStack
import concourse.bass as bass
import concourse.mybir as mybir
from concourse.bass_utils import run_bass_kernel_spmd

F32 = mybir.dt.float32
BF16 = mybir.dt.bfloat16
I32 = mybir.dt.int32
AF = mybir.ActivationFunctionType
ALU = mybir.AluOpType
AX = mybir.AxisListType

D = 1024
S = 2048
NT = 16
DFF = 2816
NFF = 22
DEPTH = 4
D_IN = 4544
EPS = 1e-6
FFN_GROUPS = [(0, 6), (6, 12), (12, 17), (17, 22)]

C_Z = 0
C_X = 1024
C_B = 2048
C_C = 2304
C_DT = 2560
C_QL = 2592
C_KVL = 2848
C_KPE = 2976
C_CH = 3008
C_CB = 3520
C_CC = 4032


class R:
    __slots__ = ("name", "w", "r")

    def __init__(self, name):
        self.name = name
        self.w = None
        self.r = {}


class KB:
    def __init__(self, nc, es):
        self.nc = nc
        self.es = es
        self.eng = dict(pe=nc.tensor, act=nc.scalar, dve=nc.vector, pool=nc.gpsimd, sp=nc.sync)
        self.psem = {e: es.enter_context(nc.semaphore("p_" + e)) for e in self.eng}
        self.cnt = {e: 0 for e in self.eng}
        self.seen = {e: {} for e in self.eng}
        self.dsem = {}
        self.nwait = 0

    def _semh(self, key):
        if key[0] == "e":
            return self.psem[key[1]]
        return self.dsem[key][0]

    def wait(self, eng, dep):
        key, val = dep
        if key[0] == "e" and key[1] == eng:
            if eng in ("pe", "sp"):
                return
            if self.cnt[eng] - val >= 3:
                return
        if key[0] == "d":
            val = max(val, 16 * self.dsem[key][1])
        if self.seen[eng].get(key, 0) >= val:
            return
        self.eng[eng].wait_ge(self._semh(key), val)
        self.seen[eng][key] = val
        self.nwait += 1

    def _deps(self, eng, reads, writes):
        for r in reads:
            if r.w is not None:
                self.wait(eng, r.w)
        for w in writes:
            if w.w is not None:
                self.wait(eng, w.w)
            for k, v in w.r.items():
                self.wait(eng, (k, v))

    def _mark(self, me, reads, writes):
        k, v = me
        for r in reads:
            if r.r.get(k, 0) < v:
                r.r[k] = v
        for w in writes:
            w.w = me
            w.r = {}

    def record(self, f):
        old = getattr(self, "rec", None)
        self.rec = []
        f()
        out = self.rec
        self.rec = old
        return out

    def play_interleaved(self, lists):
        n = max(len(x) for x in lists)
        for i in range(n):
            for x in lists:
                if i < len(x):
                    eng, fn, reads, writes = x[i]
                    self.op(eng, fn, reads, writes)

    def op(self, eng, fn, reads=(), writes=()):
        if getattr(self, "rec", None) is not None:
            self.rec.append((eng, fn, list(reads), list(writes)))
            return None
        self._deps(eng, reads, writes)
        ins = fn(self.eng[eng])
        self.cnt[eng] += 1
        ins.then_inc(self.psem[eng], 1)
        self._mark((("e", eng), self.cnt[eng]), reads, writes)
        return ins

    def dma(self, q, out, in_, reads=(), writes=(), semres=None, **kw):
        self._deps(q, reads, writes)
        sr = semres if semres is not None else writes[0]
        key = ("d", sr.name)
        if key not in self.dsem:
            self.dsem[key] = [self.es.enter_context(self.nc.semaphore("d_" + sr.name)), 0]
        ent = self.dsem[key]
        ent[1] += 1
        self.eng[q].dma_start(out=out, in_=in_, **kw).then_inc(ent[0], 16)
        self._mark((key, 16 * ent[1]), reads, writes)

    def barrier(self):
        for e in self.eng:
            for e2 in self.eng:
                if e2 != e and self.cnt[e2] > 0:
                    self.wait(e, (("e", e2), self.cnt[e2]))
            for key, ent in self.dsem.items():
                if ent[1] > 0:
                    self.wait(e, (key, 16 * ent[1]))

    def final_wait(self, eng="sp"):
        for key, ent in self.dsem.items():
            if ent[1] > 0:
                self.wait(eng, (key, 16 * ent[1]))
        for e2 in self.eng:
            if e2 != eng and self.cnt[e2] > 0:
                self.wait(eng, (("e", e2), self.cnt[e2]))


class Prog:
    def __init__(self, n_layers=DEPTH, stages=("ffn1", "conv", "mla", "ssd", "ffn2")):
        self.n_layers = n_layers
        self.stages = stages

    def build(self):
        nc = bass.Bass("TRN2", target_bir_lowering=False)
        self.nc = nc
        L = DEPTH

        def din(name, shape, dt=F32):
            return nc.dram_tensor(name, list(shape), dt, kind="ExternalInput").ap()

        self.d = d = {}
        d["x"] = din("x", [S, D])
        d["positions"] = din("positions", [S, 1], I32)
        d["ffn1_norm"] = din("ffn1_norm", [L, D])
        d["ffn1_w_gate"] = din("ffn1_w_gate", [L, D, DFF])
        d["ffn1_w_up"] = din("ffn1_w_up", [L, D, DFF])
        d["ffn1_w_down"] = din("ffn1_w_down", [L, DFF, D])
        d["mix_norm"] = din("mix_norm", [L, D])
        d["w_in"] = din("w_in", [L, D, D_IN])
        d["ssd_conv_w"] = din("ssd_conv_w", [L, 5, 1536])
        d["ssd_conv_b"] = din("ssd_conv_b", [L, 1536])
        d["ssd_dt_bias"] = din("ssd_dt_bias", [L, 32])
        d["ssd_a_log"] = din("ssd_a_log", [L, 32])
        d["ssd_d"] = din("ssd_d", [L, 16])
        d["ssd_norm"] = din("ssd_norm", [L, 1024])
        d["mla_q_norm"] = din("mla_q_norm", [L, 256])
        d["mla_w_uq"] = din("mla_w_uq", [L, 256, 768])
        d["mla_kv_norm"] = din("mla_kv_norm", [L, 128])
        d["mla_w_ukv"] = din("mla_w_ukv", [L, 128, 1024])
        d["mla_q_head_norm"] = din("mla_q_head_norm", [L, 96])
        d["mla_k_head_norm"] = din("mla_k_head_norm", [L, 96])
        d["mla_out_norm"] = din("mla_out_norm", [L, 512])
        d["conv_w"] = din("conv_w", [L, 3, 512])
        d["conv_out_norm"] = din("conv_out_norm", [L, 512])
        d["w_out"] = din("w_out", [L, 2048, D])
        d["ffn2_norm"] = din("ffn2_norm", [L, D])
        d["ffn2_w_gate"] = din("ffn2_w_gate", [L, D, DFF])
        d["ffn2_w_up"] = din("ffn2_w_up", [L, D, DFF])
        d["ffn2_w_down"] = din("ffn2_w_down", [L, DFF, D])
        self.out = nc.dram_tensor("out", [S, D], F32, kind="ExternalOutput").ap()

        with ExitStack() as es:
            self.es = es
            K = self.K = KB(nc, es)

            def sb(name, shape, dt):
                return es.enter_context(nc.sbuf_tensor(name, list(shape), dt))

            self.X = sb("X", [128, NT, D], F32)
            self.RX = [R(f"X{t}") for t in range(NT)]
            self.RXs = R("Xsem")
            self.Rxsps = R("xspsem")
            self.hT = sb("hT", [128, 8, S], BF16)
            self.RhT = [R(f"hT{b}") for b in range(4)]
            self.WS = sb("WS", [128, 36864], BF16)
            self.EX = sb("EX", [128, 14336], BF16)
            self.ident = sb("ident", [128, 128], BF16)
            self.identf = sb("identf", [128, 128], F32)
            self.grow = sb("grow", [128, D], F32)
            self.Rgrow = R("grow")
            self.ss = sb("ss", [128, 2 * NT], F32)
            self.SV = sb("SV", [128, 512], F32)
            self.st2 = sb("st2", [128, 64], F32)
            self.RSV = R("SV")
            self.BD = sb("BD", [128, 128], BF16)
            self.xsp = nc.dram_tensor("xspill", [S, D], F32, kind="Internal").ap()
            self.Rxsp = [R(f"xsp{t}") for t in range(NT)]
            self.ymT = self.X[:].rearrange("p t c -> p (t c)").bitcast(BF16).rearrange("p (j s) -> p j s", j=16)
            self.RymT = [R(f"ymT{j}") for j in range(16)]
            self.Rss = R("ss")
            self.junk = self.EX[:, 9216:10240]
            self.Rjunk = R("junk")
            self.ps = [es.enter_context(nc.psum_tensor(f"ps{i}", [128, 512], F32)) for i in range(8)]
            self.Rps = [R(f"ps{i}") for i in range(8)]
            self.Rconst = R("const")

            K.op("pool", lambda e: e.memset(self.ident[:], 0.0), writes=[self.Rconst])
            K.op("pool", lambda e: e.affine_select(out=self.ident[:], in_=self.ident[:], pattern=[[-1, 128]],
                                                   compare_op=ALU.not_equal, fill=1.0, base=0, channel_multiplier=1),
                 writes=[self.Rconst])
            K.op("pool", lambda e: e.memset(self.identf[:], 0.0), writes=[self.Rconst])
            K.op("pool", lambda e: e.affine_select(out=self.identf[:], in_=self.identf[:], pattern=[[-1, 128]],
                                                   compare_op=ALU.not_equal, fill=1.0, base=0, channel_multiplier=1),
                 writes=[self.Rconst])

            K.op("pool", lambda e: e.memset(self.BD[:], 0.0), writes=[self.Rconst])
            K.op("pool", lambda e: e.memset(self.BD[0:64, 0:64], 1.0), writes=[self.Rconst])
            K.op("pool", lambda e: e.memset(self.BD[64:128, 64:128], 1.0), writes=[self.Rconst])
            self.cs = sb("cs", [128, 2, NT, 16], F32)
            self.tri = sb("tri", [128, 128], F32)
            self.triT = sb("triT", [128, 128], F32)
            self.onesf = sb("onesf", [128, 128], F32)
            for tt_, st_, cm_ in ((self.tri, 1, -1), (self.triT, -1, 1)):
                K.op("pool", lambda e, tt_=tt_: e.memset(tt_[:], 1.0), writes=[self.Rconst])
                K.op("pool", lambda e, tt_=tt_, st_=st_, cm_=cm_: e.affine_select(out=tt_[:], in_=tt_[:], pattern=[[st_, 128]], compare_op=ALU.is_ge, fill=0.0,
                                                                                 base=0, channel_multiplier=cm_), writes=[self.Rconst])
            K.op("pool", lambda e: e.memset(self.onesf[:], 1.0), writes=[self.Rconst])
            self.Rcs = R("cs")
            self.rope_setup()
            xv = d["x"].rearrange("(t p) c -> p t c", p=128)
            for t in range(NT):
                K.dma("sp", out=self.X[:, t, :], in_=xv[:, t, :], writes=[self.RX[t]], semres=self.RXs)

            for l in range(self.n_layers):
                if "ffn1" in self.stages:
                    self.norm_stage(d["ffn1_norm"][l])
                    self.ffn_stage(d["ffn1_w_gate"][l], d["ffn1_w_up"][l], d["ffn1_w_down"][l])
                mix = [s for s in self.stages if s in ("conv", "mla", "ssd")]
                if mix:
                    self.norm_stage(d["mix_norm"][l])
                    self.mixer_begin(l)
                    K.barrier()
                    if "ssd" in mix:
                        self.ssd_stage(l)
                        K.barrier()
                    if "conv" in mix:
                        self.conv_stage(l)
                        K.barrier()
                    if "mla" in mix:
                        self.mla_stage(l)
                        K.barrier()
                    self.mixer_end(l, mix)
                    K.barrier()
                if "ffn2" in self.stages:
                    self.norm_stage(d["ffn2_norm"][l])
                    self.ffn_stage(d["ffn2_w_gate"][l], d["ffn2_w_up"][l], d["ffn2_w_down"][l])

            ov = self.out.rearrange("(t p) c -> p t c", p=128)
            Rout = R("out")
            for t in range(NT):
                K.dma("sp", out=ov[:, t, :], in_=self.X[:, t, :], reads=[self.RX[t]], writes=[Rout])
            K.final_wait("sp")
        return nc

    def wsv(self, off, n, dt=BF16, base=None):
        base = self.WS if base is None else base
        if dt == F32:
            return base[:, off:off + 2 * n].bitcast(F32)
        return base[:, off:off + n]

    def norm_stage(self, gain):
        K = self.K
        X, hT = self.X, self.hT
        K.dma("sp", out=self.grow[:], in_=gain.partition_broadcast(128), writes=[self.Rgrow])
        for t in range(NT):
            K.op("act", lambda e, t=t: e.activation(out=self.junk, in_=X[:, t, :], func=AF.Square,
                                                    accum_out=self.ss[:, t:t + 1]),
                 reads=[self.RX[t]], writes=[self.Rss])
        K.op("act", lambda e: e.activation(out=self.ss[:, NT:2 * NT], in_=self.ss[:, 0:NT], func=AF.Sqrt,
                                           scale=1.0 / D, bias=EPS),
             reads=[self.Rss], writes=[self.Rss])
        K.op("dve", lambda e: e.reciprocal(out=self.ss[:, NT:2 * NT], in_=self.ss[:, NT:2 * NT]),
             reads=[self.Rss], writes=[self.Rss])
        xs = [self.wsv(7168, D, BF16, self.EX), self.wsv(8192, D, BF16, self.EX)]
        if not hasattr(self, "Rxsn"):
            self.Rxsn = [R("xs0"), R("xs1")]
        Rxs = self.Rxsn
        for t in range(NT):
            j = t % 2
            K.op("dve", lambda e, t=t, j=j: e.scalar_tensor_tensor(out=xs[j], in0=X[:, t, :],
                                                                   scalar=self.ss[:, NT + t:NT + t + 1],
                                                                   in1=self.grow[:], op0=ALU.mult, op1=ALU.mult),
                 reads=[self.RX[t], self.Rss, self.Rgrow], writes=[Rxs[j]])
            bank = t % 2
            pb = self.ps[bank][:].bitcast(BF16).rearrange("p (c k) -> p c k", c=8)
            for c in range(8):
                K.op("pe", lambda e, c=c, j=j, pb=pb: e.transpose(out=pb[:, c, :], in_=xs[j][:, c * 128:(c + 1) * 128],
                                                                  identity=self.ident[:]),
                     reads=[Rxs[j], self.Rconst], writes=[self.Rps[bank]])
            K.op("act", lambda e, t=t, pb=pb: e.activation(out=hT[:, :, t * 128:(t + 1) * 128], in_=pb, func=AF.Copy),
                 reads=[self.Rps[bank]], writes=[self.RhT[t // 4]])

    def ffn_stage(self, Wg, Wu, Wd):
        K = self.K
        X, hT = self.X, self.hT
        SL = 18432
        WG = [self.wsv(s * SL, 6144).rearrange("p (k c) -> p k c", k=8) for s in range(2)]
        WU = [self.wsv(s * SL + 6144, 6144).rearrange("p (k c) -> p k c", k=8) for s in range(2)]
        WD = [self.wsv(s * SL + 12288, 6144).rearrange("p (f c) -> p f c", f=6) for s in range(2)]
        if not hasattr(self, "RWffn"):
            self.RWffn = [R("ffw0"), R("ffw1")]
            self.Ractffn = [R("act0"), R("act1")]
            self.Rsilffn = [R("sil0"), R("sil1")]
        RW = self.RWffn
        act = [self.wsv(a * 3072, 3072, BF16, self.EX).rearrange("p (f c) -> p f c", f=6) for a in range(2)]
        Ract = self.Ractffn
        sil = [self.wsv(6144 + a * 512, 512, BF16, self.EX) for a in range(2)]
        Rsil = self.Rsilffn
        Wgv = Wg.rearrange("(k p) c -> p k c", p=128)
        Wuv = Wu.rearrange("(k p) c -> p k c", p=128)

        def load(q):
            f0, f1 = FFN_GROUPS[q]
            nf = f1 - f0
            s = q % 2
            K.dma("pool", out=WG[s][:, :, 0:nf * 128], in_=Wgv[:, :, f0 * 128:f1 * 128], writes=[RW[s]])
            K.dma("pool", out=WU[s][:, :, 0:nf * 128], in_=Wuv[:, :, f0 * 128:f1 * 128], writes=[RW[s]])
            K.dma("pool", out=WD[s][:, 0:nf, :], in_=Wd[f0 * 128:f1 * 128, :].rearrange("(f p) c -> p f c", p=128),
                  writes=[RW[s]])

        load(0)
        it = 0
        ab = 0
        for q in range(4):
            if q + 1 < 4:
                load(q + 1)
            f0, f1 = FFN_GROUPS[q]
            nf = f1 - f0
            s = q % 2
            for tb in range(4):
                for f in range(nf):
                    bg = (it % 2) * 2
                    bu = bg + 1
                    si = it % 2
                    it += 1
                    for k in range(8):
                        K.op("pe", lambda e, k=k, f=f, bg=bg: e.matmul(self.ps[bg][:], lhsT=WG[s][:, k, f * 128:(f + 1) * 128],
                                                                       rhs=hT[:, k, tb * 512:(tb + 1) * 512], start=(k == 0), stop=(k == 7)),
                             reads=[RW[s], self.RhT[tb]], writes=[self.Rps[bg]])
                    for k in range(8):
                        K.op("pe", lambda e, k=k, f=f, bu=bu: e.matmul(self.ps[bu][:], lhsT=WU[s][:, k, f * 128:(f + 1) * 128],
                                                                       rhs=hT[:, k, tb * 512:(tb + 1) * 512], start=(k == 0), stop=(k == 7)),
                             reads=[RW[s], self.RhT[tb]], writes=[self.Rps[bu]])
                    K.op("act", lambda e, bg=bg, si=si: e.activation(out=sil[si], in_=self.ps[bg][:], func=AF.Silu),
                         reads=[self.Rps[bg]], writes=[Rsil[si]])
                    K.op("dve", lambda e, bu=bu, si=si, f=f: e.tensor_tensor(out=act[ab][:, f, :], in0=self.ps[bu][:], in1=sil[si], op=ALU.mult),
                         reads=[self.Rps[bu], Rsil[si]], writes=[Ract[ab]])
                for tt in range(4):
                    t = tb * 4 + tt
                    for half in range(2):
                        bo = 4 + (t % 2) * 2 + half
                        for f in range(nf):
                            K.op("pe", lambda e, f=f, bo=bo, tt=tt, half=half: e.matmul(
                                self.ps[bo][:], lhsT=act[ab][:, f, tt * 128:(tt + 1) * 128],
                                rhs=WD[s][:, f, half * 512:(half + 1) * 512], start=(f == 0), stop=(f == nf - 1)),
                                 reads=[Ract[ab], RW[s]], writes=[self.Rps[bo]])
                        K.op("dve", lambda e, bo=bo, t=t, half=half: e.scalar_tensor_tensor(
                            out=X[:, t, half * 512:(half + 1) * 512], in0=self.ps[bo][:], scalar=0.5,
                            in1=X[:, t, half * 512:(half + 1) * 512], op0=ALU.mult, op1=ALU.add),
                             reads=[self.Rps[bo], self.RX[t]], writes=[self.RX[t]])
                ab ^= 1

    def mixer_begin(self, l):
        K = self.K
        xv = self.xsp.rearrange("(t p) c -> p t c", p=128)
        for t in range(NT):
            K.dma("sp", out=xv[:, t, :], in_=self.X[:, t, :], reads=[self.RX[t]], writes=[self.Rxsp[t]], semres=self.Rxsps)

    def mixer_end(self, l, mix):
        K = self.K
        chunks = []
        if "ssd" in mix:
            chunks += list(range(0, 8))
        if "mla" in mix:
            chunks += list(range(8, 12))
        if "conv" in mix:
            chunks += list(range(12, 16))
        wo = self.wsv(0, 16384).rearrange("p (j c) -> p j c", j=16)
        Rwo = R("wo")
        wov = self.d["w_out"][l].rearrange("(j p) c -> p j c", p=128)
        for j0 in range(0, 16, 4):
            K.dma("pool", out=wo[:, j0:j0 + 4, :], in_=wov[:, j0:j0 + 4, :], writes=[Rwo])
        stg = [self.wsv(16384 + i * 2048, 1024, F32) for i in range(10)] + [self.wsv(i * 2048, 1024, F32, self.EX) for i in range(6)]
        Rstg = [R(f"stg{t}") for t in range(NT)]
        Rstgs = R("stgsem")
        xv = self.xsp.rearrange("(t p) c -> p t c", p=128)
        for t in range(NT):
            K.dma("sp", out=stg[t], in_=xv[:, t, :], reads=[self.Rxsp[t], Rwo], writes=[Rstg[t]], semres=Rstgs)
        for t in range(NT):
            for half in range(2):
                bo = (t % 2) * 2 + half
                for i, j in enumerate(chunks):
                    K.op("pe", lambda e, j=j, bo=bo, half=half, i=i: e.matmul(
                        self.ps[bo][:], lhsT=self.ymT[:, j, t * 128:(t + 1) * 128],
                        rhs=wo[:, j, half * 512:(half + 1) * 512], start=(i == 0), stop=(i == len(chunks) - 1)),
                         reads=[self.RymT[j], Rwo], writes=[self.Rps[bo]])
                K.op("dve", lambda e, bo=bo, t=t, half=half: e.tensor_tensor(
                    out=stg[t][:, half * 512:(half + 1) * 512], in0=self.ps[bo][:],
                    in1=stg[t][:, half * 512:(half + 1) * 512], op=ALU.add),
                     reads=[self.Rps[bo], Rstg[t]], writes=[Rstg[t]])
        K.barrier()
        engs = ["act", "dve", "act", "pool"]
        for t in range(NT):
            en = engs[t % 4]
            if en == "act":
                K.op("act", lambda e, t=t: e.activation(out=self.X[:, t, :], in_=stg[t], func=AF.Copy), reads=[Rstg[t]], writes=[self.RX[t]])
            else:
                K.op(en, lambda e, t=t: e.tensor_copy(out=self.X[:, t, :], in_=stg[t]), reads=[Rstg[t]], writes=[self.RX[t]])

    def conv_stage(self, l):
        K = self.K
        d = self.d
        hT = self.hT
        SV = self.SV
        cw = SV[:, 0:12].rearrange("p (j k) -> p j k", j=4)
        gcol = SV[:, 12:16]
        for j in range(4):
            K.dma("sp", out=cw[:, j, :], in_=d["conv_w"][l][:, j * 128:(j + 1) * 128].rearrange("k p -> p k"), writes=[self.RSV],
                  allow_slow_non_contiguous=True)
        K.dma("sp", out=gcol, in_=d["conv_out_norm"][l].rearrange("(j p) -> p j", p=128), writes=[self.RSV],
              allow_slow_non_contiguous=True)
        Winv = d["w_in"][l].rearrange("(k p) c -> p k c", p=128)
        EX = self.EX
        wcall = [self.wsv(i * 4096, 4096).rearrange("p (k c) -> p k c", k=8) for i in range(3)]
        Rwc = [R("wcall"), R("wcall_")]
        ms = [self.wsv(12288, 2052), self.wsv(0, 2052, BF16, EX)]
        dgc = [self.wsv(32776 + a * 384, 384).rearrange("p (k c) -> p k c", k=3) for a in range(2)]
        Rdgc = [R("dgc0"), R("dgc1")]
        cbufs = [self.wsv(16392, 2048, F32), self.wsv(4104, 2048, F32, EX)]
        ys = [self.wsv(20488, 2048, F32), self.wsv(8200, 2048, F32, EX)]
        ysqs = [self.wsv(24584, 2048), self.wsv(26632, 2048)]
        rs = [self.wsv(28680 + a * 1024, 512, F32) for a in range(2)]
        tmp = [self.wsv(30728 + a * 1024, 512, F32) for a in range(2)]
        for i in range(3):
            K.dma("pool", out=wcall[i], in_=Winv[:, :, (C_CH, C_CB, C_CC)[i]:(C_CH, C_CB, C_CC)[i] + 512], writes=[Rwc[0]])
        Rms, Rcbs, Rys, Rysqs = [R("cm0"), R("cm1")], [R("ccb0"), R("ccb1")], [R("cy0"), R("cy1")], [R("cysq0"), R("cysq1")]
        Rrs = [R("crs0"), R("crs1")]
        Rtmp = [R("ctmp0"), R("ctmp1")]
        for pp in range(2):
            K.op("pool", lambda e, pp=pp: e.memset(ms[pp][:, 0:1], 0.0), writes=[Rms[pp]])
            K.op("pool", lambda e, pp=pp: e.memset(ms[pp][:, 2049:2052], 0.0), writes=[Rms[pp]])
        cols = (C_CH, C_CB, C_CC)

        def stA(j):
            sl = j % 2
            m, cbuf = ms[sl], cbufs[sl]
            for tb in range(4):
                b0 = 3 * (tb % 2)
                a = tb % 2
                for i in range(3):
                    for k in range(8):
                        K.op("pe", lambda e, i=i, k=k: e.matmul(self.ps[b0 + i][:], lhsT=wcall[i][:, k, j * 128:(j + 1) * 128],
                                                               rhs=hT[:, k, tb * 512:(tb + 1) * 512], start=(k == 0), stop=(k == 7)),
                             reads=[Rwc[0], self.RhT[tb]], writes=[self.Rps[b0 + i]])
                K.op("act", lambda e: e.activation(out=tmp[a], in_=self.ps[b0 + 2][:], func=AF.Copy),
                     reads=[self.Rps[b0 + 2]], writes=[Rtmp[a]])
                K.op("dve", lambda e: e.tensor_tensor(out=m[:, 1 + tb * 512:1 + (tb + 1) * 512], in0=self.ps[b0][:], in1=tmp[a], op=ALU.mult),
                     reads=[self.Rps[b0], Rtmp[a]], writes=[Rms[sl]])
                K.op("act", lambda e: e.activation(out=cbuf[:, tb * 512:(tb + 1) * 512], in_=self.ps[b0 + 1][:], func=AF.Copy),
                     reads=[self.Rps[b0 + 1]], writes=[Rcbs[sl]])

        def stB(j):
            sl = j % 2
            m, cbuf, y, ysq = ms[sl], cbufs[sl], ys[sl], ysqs[sl]
            for kk in range(3):
                K.op("dve", lambda e, kk=kk: e.tensor_scalar(out=dgc[sl][:, kk, :], in0=self.ident[:], scalar1=cw[:, j, kk:kk + 1], scalar2=None, op0=ALU.mult),
                     reads=[self.Rconst, self.RSV], writes=[Rdgc[sl]])
            for tb in range(4):
                for kk in range(3):
                    K.op("pe", lambda e, kk=kk: e.matmul(self.ps[6][:], lhsT=dgc[sl][:, kk, :], rhs=m[:, tb * 512 + kk:tb * 512 + kk + 512],
                                                        start=(kk == 0), stop=(kk == 2)),
                         reads=[Rdgc[sl], Rms[sl]], writes=[self.Rps[6]])
                K.op("dve", lambda e: e.tensor_tensor(out=y[:, tb * 512:(tb + 1) * 512], in0=self.ps[6][:], in1=cbuf[:, tb * 512:(tb + 1) * 512], op=ALU.mult),
                     reads=[self.Rps[6], Rcbs[sl]], writes=[Rys[sl]])
            K.op("act", lambda e: e.activation(out=ysq, in_=y, func=AF.Square), reads=[Rys[sl]], writes=[Rysqs[sl]])

        def stC(j):
            sl = j % 2
            y, ysq = ys[sl], ysqs[sl]
            for tb in range(4):
                a = tb % 2
                bb = 7
                K.op("pe", lambda e: e.matmul(self.ps[bb][:], lhsT=self.BD[:], rhs=ysq[:, tb * 512:(tb + 1) * 512], start=True, stop=True),
                     reads=[Rysqs[sl], self.Rconst], writes=[self.Rps[bb]])
                K.op("act", lambda e: e.activation(out=rs[a], in_=self.ps[bb][:], func=AF.Sqrt, scale=1.0 / 64, bias=EPS),
                     reads=[self.Rps[bb]], writes=[Rrs[a]])
                K.op("dve", lambda e: e.reciprocal(out=rs[a], in_=rs[a]), reads=[Rrs[a]], writes=[Rrs[a]])
                K.op("dve", lambda e: e.scalar_tensor_tensor(out=self.ymT[:, 12 + j, tb * 512:(tb + 1) * 512],
                                                             in0=y[:, tb * 512:(tb + 1) * 512], scalar=gcol[:, j:j + 1],
                                                             in1=rs[a], op0=ALU.mult, op1=ALU.mult),
                     reads=[Rys[sl], Rrs[a], self.RSV], writes=[self.RymT[12 + j]])

        for j in range(4 + 2):
            if j < 4:
                stA(j)
            if 0 <= j - 1 < 4:
                stB(j - 1)
            if 0 <= j - 2 < 4:
                stC(j - 2)

    def rope_setup(self):
        K = self.K
        EX = self.EX
        posi = EX[:, 0:32].bitcast(I32)
        posf = self.wsv(32, 16, F32, EX)
        ang = self.wsv(64, 256, F32, EX).rearrange("p (t i) -> p t i", t=NT)
        nf = self.wsv(576, 256, F32, EX).rearrange("p (t i) -> p t i", t=NT)
        ni = EX[:, 1088:1600].bitcast(I32).rearrange("p (t i) -> p t i", t=NT)
        msk = self.wsv(1600, 256, F32, EX).rearrange("p (t i) -> p t i", t=NT)
        yy = self.wsv(2112, 256, F32, EX).rearrange("p (t i) -> p t i", t=NT)
        Rr = R("ropetmp")
        K.dma("sp", out=posi, in_=self.d["positions"].rearrange("(t p) o -> p (t o)", p=128), writes=[Rr],
              allow_slow_non_contiguous=True)
        K.op("dve", lambda e: e.tensor_copy(out=posf, in_=posi), reads=[Rr], writes=[Rr])
        for i in range(16):
            inv = float(10000.0 ** (-i / 16.0))
            K.op("dve", lambda e, i=i, inv=inv: e.tensor_scalar(out=ang[:, :, i], in0=posf, scalar1=inv, scalar2=None, op0=ALU.mult),
                 reads=[Rr], writes=[Rr])
        TWO_PI = 2.0 * math.pi
        C1 = 6.28125
        C2 = TWO_PI - C1
        K.op("dve", lambda e: e.tensor_scalar(out=nf, in0=ang, scalar1=1.0 / TWO_PI, scalar2=None, op0=ALU.mult), reads=[Rr], writes=[Rr])
        K.op("dve", lambda e: e.tensor_copy(out=ni, in_=nf), reads=[Rr], writes=[Rr])
        K.op("dve", lambda e: e.tensor_copy(out=nf, in_=ni), reads=[Rr], writes=[Rr])
        K.op("dve", lambda e: e.scalar_tensor_tensor(out=ang, in0=nf, scalar=-C1, in1=ang, op0=ALU.mult, op1=ALU.add), reads=[Rr], writes=[Rr])
        K.op("dve", lambda e: e.scalar_tensor_tensor(out=ang, in0=nf, scalar=-C2, in1=ang, op0=ALU.mult, op1=ALU.add), reads=[Rr], writes=[Rr])
        for which, shift in ((1, 0.0), (0, math.pi / 2)):
            K.op("dve", lambda e, shift=shift: e.tensor_scalar(out=yy, in0=ang, scalar1=shift, scalar2=None, op0=ALU.add), reads=[Rr], writes=[Rr])
            for _ in range(2):
                K.op("dve", lambda e: e.tensor_scalar(out=msk, in0=yy, scalar1=math.pi, scalar2=-TWO_PI, op0=ALU.is_gt, op1=ALU.mult), reads=[Rr], writes=[Rr])
                K.op("dve", lambda e: e.tensor_tensor(out=yy, in0=yy, in1=msk, op=ALU.add), reads=[Rr], writes=[Rr])
                K.op("dve", lambda e: e.tensor_scalar(out=msk, in0=yy, scalar1=-math.pi, scalar2=TWO_PI, op0=ALU.is_lt, op1=ALU.mult), reads=[Rr], writes=[Rr])
                K.op("dve", lambda e: e.tensor_tensor(out=yy, in0=yy, in1=msk, op=ALU.add), reads=[Rr], writes=[Rr])
            K.op("dve", lambda e: e.tensor_scalar(out=yy, in0=yy, scalar1=math.pi, scalar2=-math.pi, op0=ALU.min, op1=ALU.max), reads=[Rr], writes=[Rr])
            K.op("act", lambda e, which=which: e.activation(out=self.cs[:, which, :, :], in_=yy, func=AF.Sin), reads=[Rr], writes=[self.Rcs])
        K.barrier()

    def rstd_from_ss(self, ss_ap, n, Rs):
        K = self.K
        K.op("act", lambda e: e.activation(out=ss_ap, in_=ss_ap, func=AF.Sqrt, scale=1.0 / n, bias=EPS), reads=[Rs], writes=[Rs])
        K.op("dve", lambda e: e.reciprocal(out=ss_ap, in_=ss_ap), reads=[Rs], writes=[Rs])

    def rope(self, x, t, nh, tmp, Rx, Rt):
        K = self.K
        cosb = self.cs[:, 0, t, :].unsqueeze(1).to_broadcast([128, nh, 16])
        sinb = self.cs[:, 1, t, :].unsqueeze(1).to_broadcast([128, nh, 16])
        x1 = x[:, :, 0:16]
        x2 = x[:, :, 16:32]
        for i, (a, b) in enumerate(((x1, cosb), (x2, sinb), (x1, sinb), (x2, cosb))):
            K.op("dve", lambda e, i=i, a=a, b=b: e.tensor_tensor(out=tmp[:, i, :, :], in0=a, in1=b, op=ALU.mult),
                 reads=[Rx, self.Rcs], writes=[Rt])
        K.op("dve", lambda e: e.tensor_tensor(out=x1, in0=tmp[:, 0, :, :], in1=tmp[:, 1, :, :], op=ALU.subtract), reads=[Rt], writes=[Rx])
        K.op("dve", lambda e: e.tensor_tensor(out=x2, in0=tmp[:, 2, :, :], in1=tmp[:, 3, :, :], op=ALU.add), reads=[Rt], writes=[Rx])

    def mla_stage(self, l):
        K = self.K
        d = self.d
        hT = self.hT
        EX = self.EX
        grow = self.grow
        SV = self.SV
        Rg = self.Rgrow
        K.dma("sp", out=grow[:, 0:256], in_=d["mla_q_norm"][l].partition_broadcast(128), writes=[Rg])
        K.dma("sp", out=grow[:, 256:384], in_=d["mla_kv_norm"][l].partition_broadcast(128), writes=[Rg])
        K.dma("sp", out=grow[:, 384:480], in_=d["mla_q_head_norm"][l].partition_broadcast(128), writes=[Rg])
        K.dma("sp", out=grow[:, 480:576], in_=d["mla_k_head_norm"][l].partition_broadcast(128), writes=[Rg])
        K.dma("sp", out=SV[:, 0:512], in_=d["mla_out_norm"][l].partition_broadcast(128), writes=[self.RSV])
        K.op("dve", lambda e: e.tensor_scalar(out=grow[:, 384:480], in0=grow[:, 384:480], scalar1=float(96 ** -0.5), scalar2=None, op0=ALU.mult),
             reads=[Rg], writes=[Rg])
        gq = grow[:, 0:256]
        gkv = grow[:, 256:384]
        gqh = grow[:, 384:480]
        gkh = grow[:, 480:576]
        Winv = d["w_in"][l].rearrange("(k p) c -> p k c", p=128)
        wm = self.wsv(0, 3328).rearrange("p (k c) -> p k c", k=8)
        qnkT = self.wsv(3328, 6144).rearrange("p (j s) -> p j s", j=3)
        kpe = self.wsv(9472, 512, F32).rearrange("p (t i) -> p t i", t=NT)
        Rwm, RqnkT, Rkpe = R("wm"), R("qnkT"), R("kpe")
        K.dma("pool", out=wm, in_=Winv[:, :, C_QL:C_QL + 416], writes=[Rwm])
        qn = [self.wsv(a * 384, 384, BF16, EX) for a in range(2)]
        Rqn = [R("qn0"), R("qn1")]
        ssa = self.ss
        Rss = self.Rss
        SVs = self.st2
        ssa = [SVs[:, 2 * a:2 + 2 * a] for a in range(2)]
        Rssa = [R("ssa0"), R("ssa1")]

        def A_mm(t):
            b = t % 2
            for k in range(8):
                K.op("pe", lambda e, k=k: e.matmul(self.ps[b][:, 0:416], lhsT=hT[:, k, t * 128:(t + 1) * 128], rhs=wm[:, k, :],
                                                  start=(k == 0), stop=(k == 7)),
                     reads=[self.RhT[t // 4], Rwm], writes=[self.Rps[b]])

        def A_el(t):
            a = t % 2
            b = t % 2
            K.op("act", lambda e: e.activation(out=self.junk[:, 0:256], in_=self.ps[b][:, 0:256], func=AF.Square, accum_out=ssa[a][:, 0:1]),
                 reads=[self.Rps[b]], writes=[Rssa[a]])
            K.op("act", lambda e: e.activation(out=self.junk[:, 256:384], in_=self.ps[b][:, 256:384], func=AF.Square, accum_out=ssa[a][:, 1:2]),
                 reads=[self.Rps[b]], writes=[Rssa[a]])
            self.rstd_from_ss(ssa[a][:, 0:1], 256, Rssa[a])
            self.rstd_from_ss(ssa[a][:, 1:2], 128, Rssa[a])
            K.op("dve", lambda e: e.scalar_tensor_tensor(out=qn[a][:, 0:256], in0=self.ps[b][:, 0:256], scalar=ssa[a][:, 0:1], in1=gq,
                                                         op0=ALU.mult, op1=ALU.mult), reads=[self.Rps[b], Rssa[a], Rg], writes=[Rqn[a]])
            K.op("dve", lambda e: e.scalar_tensor_tensor(out=qn[a][:, 256:384], in0=self.ps[b][:, 256:384], scalar=ssa[a][:, 1:2], in1=gkv,
                                                         op0=ALU.mult, op1=ALU.mult), reads=[self.Rps[b], Rssa[a], Rg], writes=[Rqn[a]])
            K.op("act", lambda e: e.activation(out=kpe[:, t, :], in_=self.ps[b][:, 384:416], func=AF.Copy), reads=[self.Rps[b]], writes=[Rkpe])

        def A_tr(t):
            a = t % 2
            bt = 2 + t % 2
            pb = self.ps[bt][:].bitcast(BF16).rearrange("p (c k) -> p c k", c=8)
            for c in range(3):
                K.op("pe", lambda e, c=c: e.transpose(out=pb[:, c, :], in_=qn[a][:, c * 128:(c + 1) * 128], identity=self.ident[:]),
                     reads=[Rqn[a], self.Rconst], writes=[self.Rps[bt]])
            K.op("act", lambda e: e.activation(out=qnkT[:, :, t * 128:(t + 1) * 128], in_=pb[:, 0:3, :], func=AF.Copy),
                 reads=[self.Rps[bt]], writes=[RqnkT])

        for t in range(NT + 2):
            if t < NT:
                A_mm(t)
            if 0 <= t - 1 < NT:
                A_el(t - 1)
            if 0 <= t - 2 < NT:
                A_tr(t - 2)
        K.barrier()
        qT = self.hT
        kT = self.wsv(10496, 16384).rearrange("p (h s) -> p h s", h=8)
        v1 = self.wsv(26880, 8320).rearrange("p (t h c) -> p t h c", t=NT, h=8)
        RqT, RkT, Rv1 = R("qT"), R("kT"), R("v1")
        wuq = self.wsv(0, 1536, BF16, EX).rearrange("p (j c) -> p j c", j=2)
        wukv = self.wsv(1536, 1024, BF16, EX)
        osb = self.wsv(2560, 8192, BF16, EX).rearrange("p (t h c) -> p t h c", t=NT, h=8)
        ET = [self.wsv(10752 + a * 512, 512, BF16, EX) for a in range(4)]
        Rwu, Rosb = R("wu"), R("osb")
        RET = [R(f"ET{a}") for a in range(4)]
        K.dma("pool", out=wuq, in_=d["mla_w_uq"][l].rearrange("(j p) c -> p j c", p=128), writes=[Rwu])
        K.dma("pool", out=wukv, in_=d["mla_w_ukv"][l], writes=[Rwu])
        K.op("pool", lambda e: e.memset(v1[:, :, :, 64:65], 1.0), writes=[Rv1])

        def f4(base, off, n):
            return self.wsv(off, n, F32, base).rearrange("p (h c) -> p h c", h=4)
        tq = [f4(self.WS, 0, 384), f4(EX, 3072, 384)]
        tk = [f4(self.WS, 768, 384), f4(EX, 3840, 384)]
        tsq = [f4(self.WS, 1536, 384), f4(EX, 4608, 384)]
        qkbq = [self.wsv(2304, 384).rearrange("p (h c) -> p h c", h=4), self.wsv(5376, 384, BF16, EX).rearrange("p (h c) -> p h c", h=4)]
        qkbk = [self.wsv(2688, 384).rearrange("p (h c) -> p h c", h=4), self.wsv(5760, 384, BF16, EX).rearrange("p (h c) -> p h c", h=4)]
        trope = [self.wsv(2560, 256, F32, EX).rearrange("p (i h c) -> p i h c", i=4, h=4),
                 self.wsv(6144, 256, F32, EX).rearrange("p (i h c) -> p i h c", i=4, h=4)]
        Rtq, Rtk, Rtsq = [R("tq0"), R("tq1")], [R("tk0"), R("tk1")], [R("tsq0"), R("tsq1")]
        Rqkbq, Rqkbk, Rtrope = [R("qkbq0"), R("qkbq1")], [R("qkbk0"), R("qkbk1")], [R("trope0"), R("trope1")]
        s4q = [SVs[:, 8 + 4 * p:12 + 4 * p] for p in range(2)]
        s4k = [SVs[:, 16 + 4 * p:20 + 4 * p] for p in range(2)]
        s1 = [SVs[:, 24 + p:25 + p] for p in range(2)]
        Rs4q, Rs4k, Rs1 = [R("s4q0"), R("s4q1")], [R("s4k0"), R("s4k1")], [R("s10"), R("s11")]

        def B_mm(u):
            t, hh = u // 2, u % 2
            bq = u % 2
            bk = 2 + u % 2
            for j in range(2):
                K.op("pe", lambda e, j=j: e.matmul(self.ps[bq][:, 0:384], lhsT=qnkT[:, j, t * 128:(t + 1) * 128],
                                                  rhs=wuq[:, j, hh * 384:(hh + 1) * 384], start=(j == 0), stop=(j == 1)),
                     reads=[RqnkT, Rwu], writes=[self.Rps[bq]])
            K.op("pe", lambda e: e.matmul(self.ps[bk][:], lhsT=qnkT[:, 2, t * 128:(t + 1) * 128],
                                          rhs=wukv[:, hh * 512:(hh + 1) * 512], start=True, stop=True),
                 reads=[RqnkT, Rwu], writes=[self.Rps[bk]])

        def B_el(u):
            t, hh = u // 2, u % 2
            p = u % 2
            bq = u % 2
            bk = 2 + u % 2
            tp = t % 2
            if hh == 0:
                K.op("act", lambda e: e.activation(out=self.junk[:, 0:32], in_=kpe[:, t, :], func=AF.Square, accum_out=s1[tp]), reads=[Rkpe], writes=[Rs1[tp]])
            psq = self.ps[bq][:, 0:384].rearrange("p (h c) -> p h c", h=4)
            pskv = self.ps[bk][:].rearrange("p (h c) -> p h c", h=4)
            K.op("act", lambda e: e.activation(out=tsq[p], in_=psq, func=AF.Square), reads=[self.Rps[bq]], writes=[Rtsq[p]])
            K.op("dve", lambda e: e.tensor_reduce(out=s4q[p], in_=tsq[p], axis=AX.X, op=ALU.add), reads=[Rtsq[p]], writes=[Rs4q[p]])
            self.rstd_from_ss(s4q[p], 96, Rs4q[p])
            K.op("dve", lambda e: e.tensor_tensor(out=tq[p], in0=psq, in1=s4q[p].unsqueeze(2).to_broadcast([128, 4, 96]), op=ALU.mult),
                 reads=[self.Rps[bq], Rs4q[p]], writes=[Rtq[p]])
            K.op("dve", lambda e: e.tensor_tensor(out=tq[p], in0=tq[p], in1=gqh.unsqueeze(1).to_broadcast([128, 4, 96]), op=ALU.mult),
                 reads=[Rtq[p], Rg], writes=[Rtq[p]])
            self.rope(tq[p][:, :, 64:96], t, 4, trope[p], Rtq[p], Rtrope[p])
            K.op("act", lambda e: e.activation(out=qkbq[p], in_=tq[p], func=AF.Copy), reads=[Rtq[p]], writes=[Rqkbq[p]])
            K.op("act", lambda e: e.activation(out=v1[:, t, hh * 4:hh * 4 + 4, 0:64], in_=pskv[:, :, 64:128], func=AF.Copy),
                 reads=[self.Rps[bk]], writes=[Rv1])
            K.op("act", lambda e: e.activation(out=tsq[p][:, :, 0:64], in_=pskv[:, :, 0:64], func=AF.Square), reads=[self.Rps[bk]], writes=[Rtsq[p]])
            K.op("dve", lambda e: e.tensor_reduce(out=s4k[p], in_=tsq[p][:, :, 0:64], axis=AX.X, op=ALU.add), reads=[Rtsq[p]], writes=[Rs4k[p]])
            K.op("dve", lambda e: e.tensor_scalar(out=s4k[p], in0=s4k[p], scalar1=s1[tp], scalar2=None, op0=ALU.add), reads=[Rs4k[p], Rs1[tp]], writes=[Rs4k[p]])
            self.rstd_from_ss(s4k[p], 96, Rs4k[p])
            K.op("dve", lambda e: e.tensor_tensor(out=tk[p][:, :, 0:64], in0=pskv[:, :, 0:64], in1=s4k[p].unsqueeze(2).to_broadcast([128, 4, 64]), op=ALU.mult),
                 reads=[self.Rps[bk], Rs4k[p]], writes=[Rtk[p]])
            K.op("dve", lambda e: e.tensor_tensor(out=tk[p][:, :, 64:96], in0=kpe[:, t, :].unsqueeze(1).to_broadcast([128, 4, 32]),
                                                  in1=s4k[p].unsqueeze(2).to_broadcast([128, 4, 32]), op=ALU.mult),
                 reads=[Rkpe, Rs4k[p]], writes=[Rtk[p]])
            K.op("dve", lambda e: e.tensor_tensor(out=tk[p], in0=tk[p], in1=gkh.unsqueeze(1).to_broadcast([128, 4, 96]), op=ALU.mult),
                 reads=[Rtk[p], Rg], writes=[Rtk[p]])
            self.rope(tk[p][:, :, 64:96], t, 4, trope[p], Rtk[p], Rtrope[p])
            K.op("act", lambda e: e.activation(out=qkbk[p], in_=tk[p], func=AF.Copy), reads=[Rtk[p]], writes=[Rqkbk[p]])

        def B_tr(u):
            t, hh = u // 2, u % 2
            p = u % 2
            for (src, Rsrc, bnk, dstT, RdstT) in ((qkbq[p], Rqkbq[p], 4 + u % 2, qT, RqT), (qkbk[p], Rqkbk[p], 6 + u % 2, kT, RkT)):
                pb = self.ps[bnk][:].bitcast(BF16).rearrange("p (c k) -> p c k", c=8)
                for h in range(4):
                    K.op("pe", lambda e, h=h: e.transpose(out=pb[0:96, h, :], in_=src[:, h, :], identity=self.ident[:]),
                         reads=[Rsrc, self.Rconst], writes=[self.Rps[bnk]])
                K.op("act", lambda e: e.activation(out=dstT[0:96, hh * 4:hh * 4 + 4, t * 128:(t + 1) * 128], in_=pb[0:96, 0:4, :], func=AF.Copy),
                     reads=[self.Rps[bnk]], writes=[RdstT])

        NU = 2 * NT
        for v in range(NT + 1):
            if v < NT:
                B_mm(2 * v)
                B_mm(2 * v + 1)
            if v >= 1:
                B_tr(2 * v - 2)
                B_tr(2 * v - 1)
            if v < NT:
                K.play_interleaved([K.record(lambda: B_el(2 * v)), K.record(lambda: B_el(2 * v + 1))])
        K.barrier()
        rinv = self.ss[:, 16:20]
        its = [(h, qb, kt) for h in range(8) for qb in range(4) for kt in range(NT)]

        def emit_s(i):
            h, qb, kt = its[i]
            sb_ = i % 4
            K.op("pe", lambda e: e.matmul(self.ps[sb_][:], lhsT=kT[0:96, h, kt * 128:(kt + 1) * 128],
                                          rhs=qT[0:96, h, qb * 512:(qb + 1) * 512], start=True, stop=True),
                 reads=[RkT, RqT], writes=[self.Rps[sb_]])
            K.op("act", lambda e: e.activation(out=ET[sb_], in_=self.ps[sb_][:], func=AF.Exp),
                 reads=[self.Rps[sb_]], writes=[RET[sb_]])

        osT = [self.wsv(35200 + a * 1024, 512, F32) for a in range(1)]
        RosT = [R("osT0")]

        def emit_pv(i):
            h, qb, kt = its[i]
            eb = i % 4
            ob_ = 4 + (i // NT) % 2
            K.op("pe", lambda e: e.matmul(self.ps[ob_][0:65, :], lhsT=v1[:, kt, h, :], rhs=ET[eb], start=(kt == 0), stop=(kt == NT - 1)),
                 reads=[RET[eb], Rv1], writes=[self.Rps[ob_]])
            if kt == NT - 1:
                K.op("dve", lambda e: e.tensor_copy(out=osT[0][0:65, :], in_=self.ps[ob_][0:65, :]), reads=[self.Rps[ob_]], writes=[RosT[0]])
                pt = self.ps[6 + (i // NT) % 2][:, 0:260].rearrange("p (q c) -> p q c", q=4)
                Rpt = self.Rps[6 + (i // NT) % 2]
                for qt in range(4):
                    K.op("pe", lambda e, qt=qt: e.transpose(out=pt[:, qt, :], in_=osT[0][0:65, qt * 128:(qt + 1) * 128], identity=self.identf[0:65, 0:65]),
                         reads=[RosT[0], self.Rconst], writes=[Rpt])
                K.op("dve", lambda e: e.reciprocal(out=rinv, in_=pt[:, :, 64]), reads=[Rpt], writes=[Rss])
                K.op("dve", lambda e: e.tensor_tensor(out=osb[:, qb * 4:qb * 4 + 4, h, :], in0=pt[:, :, 0:64],
                                                      in1=rinv.unsqueeze(2).to_broadcast([128, 4, 64]), op=ALU.mult),
                     reads=[Rpt, Rss], writes=[Rosb])

        emit_s(0)
        emit_s(1)
        for i in range(len(its)):
            if i + 2 < len(its):
                emit_s(i + 2)
            emit_pv(i)
        K.barrier()
        to = self.wsv(0, 512, F32).rearrange("p (h c) -> p h c", h=8)
        ob = self.wsv(1024, 512)
        Rto, Rob = R("to"), R("ob")
        s8 = self.ss[:, 20:28]
        gout = SV[:, 0:512].rearrange("p (h c) -> p h c", h=8)
        for t in range(NT):
            K.op("act", lambda e, t=t: e.activation(out=to, in_=osb[:, t, :, :], func=AF.Square), reads=[Rosb], writes=[Rto])
            K.op("dve", lambda e: e.tensor_reduce(out=s8, in_=to, axis=AX.X, op=ALU.add), reads=[Rto], writes=[Rss])
            self.rstd_from_ss(s8, 64, Rss)
            K.op("dve", lambda e, t=t: e.tensor_tensor(out=to, in0=osb[:, t, :, :], in1=s8.unsqueeze(2).to_broadcast([128, 8, 64]), op=ALU.mult),
                 reads=[Rosb, Rss], writes=[Rto])
            K.op("dve", lambda e: e.tensor_tensor(out=ob.rearrange("p (h c) -> p h c", h=8), in0=to, in1=gout, op=ALU.mult),
                 reads=[Rto, self.RSV], writes=[Rob])
            bt = t % 2
            pb = self.ps[bt][:].bitcast(BF16).rearrange("p (c k) -> p c k", c=8)
            for c in range(4):
                K.op("pe", lambda e, c=c, pb=pb: e.transpose(out=pb[:, c, :], in_=ob[:, c * 128:(c + 1) * 128], identity=self.ident[:]),
                     reads=[Rob, self.Rconst], writes=[self.Rps[bt]])
            K.op("act", lambda e, t=t, pb=pb: e.activation(out=self.ymT[:, 8:12, t * 128:(t + 1) * 128], in_=pb[:, 0:4, :], func=AF.Copy),
                 reads=[self.Rps[bt]], writes=[self.RymT[8], self.RymT[9], self.RymT[10], self.RymT[11]])

    def ssd_stage(self, l):
        K = self.K
        d = self.d
        hT = self.hT
        EX = self.EX
        SV = self.SV
        RSV = self.RSV
        Winv = d["w_in"][l].rearrange("(k p) c -> p k c", p=128)
        dtb_row = SV[:, 0:32]
        a_row = SV[:, 32:64]
        d_row = SV[:, 64:80]
        scw = SV[:, 96:156].rearrange("p (c k) -> p c k", c=12)
        scb = SV[:, 160:172]
        ssn = SV[:, 176:177]
        K.dma("sp", out=dtb_row, in_=d["ssd_dt_bias"][l].partition_broadcast(128), writes=[RSV])
        K.dma("sp", out=a_row, in_=d["ssd_a_log"][l].partition_broadcast(128), writes=[RSV])
        K.dma("sp", out=d_row, in_=d["ssd_d"][l].partition_broadcast(128), writes=[RSV])
        for ci in range(12):
            K.dma("sp", out=scw[:, ci, :], in_=d["ssd_conv_w"][l][:, ci * 128:(ci + 1) * 128].rearrange("k p -> p k"), writes=[RSV],
                  allow_slow_non_contiguous=True)
        K.dma("sp", out=scb, in_=d["ssd_conv_b"][l].rearrange("(c p) -> p c", p=128), writes=[RSV], allow_slow_non_contiguous=True)
        K.dma("sp", out=self.grow[:], in_=d["ssd_norm"][l].partition_broadcast(128), writes=[self.Rgrow])

        def f3(off, n3=NT):
            return self.wsv(off, n3 * 32, F32, EX).rearrange("p (t c) -> p t c", t=n3)
        dt = f3(0)
        dta = f3(1024)
        Pc = f3(2048)
        Tend = f3(3072, 17)
        eoff = f3(4160)
        cd = f3(5184)
        Wdt = self.wsv(6208, 256, BF16, EX).rearrange("p (k c) -> p k c", k=8)
        Rdt = R("ssd_small")
        RWdt = R("Wdt")
        K.dma("pool", out=Wdt, in_=Winv[:, :, C_DT:C_DT + 32], writes=[RWdt])
        K.op("act", lambda e: e.activation(out=a_row, in_=a_row, func=AF.Exp), reads=[RSV], writes=[RSV])
        K.op("dve", lambda e: e.tensor_scalar(out=a_row, in0=a_row, scalar1=-1.0, scalar2=None, op0=ALU.mult), reads=[RSV], writes=[RSV])
        for t in range(NT):
            b = t % 2
            for k in range(8):
                K.op("pe", lambda e, k=k, b=b, t=t: e.matmul(self.ps[b][:, 0:32], lhsT=hT[:, k, t * 128:(t + 1) * 128], rhs=Wdt[:, k, :],
                                                          start=(k == 0), stop=(k == 7)),
                     reads=[self.RhT[t // 4], RWdt], writes=[self.Rps[b]])
            K.op("dve", lambda e, b=b, t=t: e.tensor_tensor(out=dt[:, t, :], in0=self.ps[b][:, 0:32], in1=dtb_row, op=ALU.add),
                 reads=[self.Rps[b], RSV], writes=[Rdt])
        K.op("act", lambda e: e.activation(out=dt, in_=dt, func=AF.Exp), reads=[Rdt], writes=[Rdt])
        K.op("act", lambda e: e.activation(out=dt, in_=dt, func=AF.Ln, bias=1.0, scale=1.0), reads=[Rdt], writes=[Rdt])
        K.op("dve", lambda e: e.tensor_tensor(out=dta, in0=dt, in1=a_row.unsqueeze(1).to_broadcast([128, NT, 32]), op=ALU.mult),
             reads=[Rdt, RSV], writes=[Rdt])
        K.op("dve", lambda e: e.memset(Tend[:, 0, :], 0.0), writes=[Rdt])
        for t in range(NT):
            ba = 2 + (t % 2) * 2
            bb = ba + 1
            K.op("pe", lambda e, ba=ba, t=t: e.matmul(self.ps[ba][:, 0:32], lhsT=self.tri[:], rhs=dta[:, t, :], start=True, stop=True),
                 reads=[Rdt, self.Rconst], writes=[self.Rps[ba]])
            K.op("pe", lambda e, bb=bb, t=t: e.matmul(self.ps[bb][:, 0:32], lhsT=self.onesf[:], rhs=dta[:, t, :], start=True, stop=True),
                 reads=[Rdt, self.Rconst], writes=[self.Rps[bb]])
            K.op("dve", lambda e, ba=ba, t=t: e.tensor_tensor(out=Pc[:, t, :], in0=self.ps[ba][:, 0:32], in1=Tend[:, t, :], op=ALU.add),
                 reads=[self.Rps[ba], Rdt], writes=[Rdt])
            K.op("dve", lambda e, bb=bb, t=t: e.tensor_tensor(out=Tend[:, t + 1, :], in0=self.ps[bb][:, 0:32], in1=Tend[:, t, :], op=ALU.add),
                 reads=[self.Rps[bb], Rdt], writes=[Rdt])
        K.op("dve", lambda e: e.tensor_tensor(out=eoff[:, :, 0:16], in0=Pc[:, :, 0:16], in1=Tend[:, 0:NT, 0:16], op=ALU.subtract), reads=[Rdt], writes=[Rdt])
        K.op("dve", lambda e: e.tensor_tensor(out=Pc[:, :, 16:32], in0=Pc[:, :, 16:32], in1=dta[:, :, 16:32], op=ALU.subtract), reads=[Rdt], writes=[Rdt])
        K.op("dve", lambda e: e.tensor_tensor(out=eoff[:, :, 16:32], in0=Tend[:, 1:NT + 1, 16:32], in1=Pc[:, :, 16:32], op=ALU.subtract), reads=[Rdt], writes=[Rdt])
        K.op("dve", lambda e: e.tensor_tensor(out=cd, in0=Tend[:, 1:NT + 1, :], in1=Tend[:, 0:NT, :], op=ALU.subtract), reads=[Rdt], writes=[Rdt])
        K.op("act", lambda e: e.activation(out=eoff, in_=eoff, func=AF.Exp), reads=[Rdt], writes=[Rdt])
        K.op("act", lambda e: e.activation(out=cd, in_=cd, func=AF.Exp), reads=[Rdt], writes=[Rdt])
        nb = dta
        K.op("dve", lambda e: e.tensor_scalar(out=nb[:, :, 0:16], in0=Pc[:, :, 0:16], scalar1=-1.0, scalar2=None, op0=ALU.mult), reads=[Rdt], writes=[Rdt])
        K.op("dve", lambda e: e.tensor_scalar(out=nb[:, :, 16:32], in0=Pc[:, :, 16:32], scalar1=-1.0, scalar2=None, op0=ALU.mult), reads=[Rdt], writes=[Rdt])
        K.barrier()

        xs_g = self.wsv(0, 8192).rearrange("p (t c) -> p t c", t=NT)
        BT = self.wsv(8192, 2048)
        CT = self.wsv(10240, 2048)
        Btok = self.wsv(12288, 2048).rearrange("p (t c) -> p t c", t=NT)
        Wz = self.wsv(14336, 4096).rearrange("p (k c) -> p k c", k=8)
        ybr = self.X[:, 8:16, :].rearrange("p a c -> p (a c)").bitcast(BF16)
        pres = [self.wsv(18432, 2052), self.wsv(0, 2052, BF16, ybr)]
        dgs = [self.wsv(22536 + a * 640, 640).rearrange("p (k c) -> p k c", k=5) for a in range(2)]
        xsTs = [self.wsv(26632, 2048), self.wsv(8200, 2048, BF16, ybr)]
        wch = [self.wsv(28680 + a * 1024, 1024).rearrange("p (k c) -> p k c", k=8) for a in range(2)]
        MTall = self.wsv(18432, 16384).rearrange("p (t h c) -> p t h c", t=NT, h=8)
        Dbuf = [self.wsv(6464, 1024, F32, EX).rearrange("p (h c) -> p h c", h=8),
                self.wsv(34816, 1024, F32).rearrange("p (h c) -> p h c", h=8)]
        CBm = self.wsv(10560, 128, F32, EX)
        CBm2 = self.wsv(8512, 128, F32, EX)
        ecol = self.wsv(8768, 128, F32, EX).rearrange("p (t h) -> p t h", t=NT)
        xd = [self.wsv(9536, 512, BF16, EX), self.wsv(13376, 512, BF16, EX)]
        xdw = [self.wsv(10048, 512, BF16, EX), self.wsv(9024, 512, BF16, EX)]
        t1s = [self.wsv(18432, 512, F32), self.wsv(19456, 512, F32)]
        szs = [self.wsv(20480, 512, F32), self.wsv(21504, 512, F32)]
        yns = [self.wsv(22528, 512), self.wsv(23040, 512)]
        prev = self.wsv(10816, 512, F32, EX)
        prevb = self.wsv(11840, 512, BF16, EX)
        ysum = self.wsv(12352, 512, F32, EX)
        ybwd = self.X[:, 8:16, :].rearrange("p a (b c) -> p (a b) c", b=2)
        Rxs, RBT, RCT, RBtok, RWz = R("xs_g"), R("BT"), R("CT"), R("Btok"), R("Wz")
        Rpres, Rdgs, RxsTs = [R("pre0"), R("pre1")], [R("dg0"), R("dg1")], [R("xsT0"), R("xsT1")]
        Rwch = [R("wch0"), R("wch1")]
        Rprev, Rprevb, Rysum = (R("prev"), R("prevb"), R("ysum"))
        Rxd = [R("xd0"), R("xd1")]
        Rxdw = [R("xdw0"), R("xdw1")]
        Rybwd = [R(f"ybwd{t}") for t in range(NT)]
        Rt1s = [R("t1s0"), R("t1s1")]
        Rszs = [R("szs0"), R("szs1")]
        Ryns = [R("yns0"), R("yns1")]
        RD = [R("Dbuf0"), R("Dbuf1")]
        RMT = [R(f"MT{t}") for t in range(NT)]
        RMT2 = [R(f"MTb{t}") for t in range(NT)]
        Recol = R("ecol")
        RCBm = [R("CBm0"), R("CBm1")]
        CBms = [CBm, CBm2]
        v8 = lambda ap: ap.rearrange("p (h c) -> p h c", h=8)

        for g in range(2):
            K.dma("pool", out=Wz, in_=Winv[:, :, C_Z + g * 512:C_Z + (g + 1) * 512], writes=[RWz])
            for pp in range(2):
                K.op("pool", lambda e, pp=pp: e.memset(pres[pp][:, 0:2], 0.0), writes=[Rpres[pp]])
                K.op("pool", lambda e, pp=pp: e.memset(pres[pp][:, 2050:2052], 0.0), writes=[Rpres[pp]])
            chunks = [g * 4 + i for i in range(4)] + [8 + g, 10 + g]

            def cdst(n_):
                sl = n_ % 2
                if n_ < 4:
                    return xsTs[sl], RxsTs[sl]
                elif n_ == 4:
                    return BT, RBT
                return CT, RCT

            def stA(n_):
                ci = chunks[n_]
                sl = n_ % 2
                pre, Rpre = pres[sl], Rpres[sl]
                K.dma("pool", out=wch[sl], in_=Winv[:, :, C_X + ci * 128:C_X + (ci + 1) * 128], writes=[Rwch[sl]])
                for tb in range(4):
                    b = tb % 2
                    for k in range(8):
                        K.op("pe", lambda e, k=k: e.matmul(self.ps[b][:], lhsT=wch[sl][:, k, :], rhs=hT[:, k, tb * 512:(tb + 1) * 512],
                                                          start=(k == 0), stop=(k == 7)),
                             reads=[Rwch[sl], self.RhT[tb]], writes=[self.Rps[b]])
                    K.op("act", lambda e: e.activation(out=pre[:, 2 + tb * 512:2 + (tb + 1) * 512], in_=self.ps[b][:], func=AF.Copy),
                         reads=[self.Rps[b]], writes=[Rpre])

            def stB(n_):
                ci = chunks[n_]
                sl = n_ % 2
                pre, Rpre, dg, Rdg = pres[sl], Rpres[sl], dgs[sl], Rdgs[sl]
                for kk in range(5):
                    K.op("dve", lambda e, kk=kk: e.tensor_scalar(out=dg[:, kk, :], in0=self.ident[:], scalar1=scw[:, ci, kk:kk + 1], scalar2=None, op0=ALU.mult),
                         reads=[self.Rconst, RSV], writes=[Rdg])
                dst, Rdst = cdst(n_)
                for tb in range(4):
                    bc = 4 + tb % 2
                    for kk in range(5):
                        K.op("pe", lambda e, kk=kk: e.matmul(self.ps[bc][:], lhsT=dg[:, kk, :], rhs=pre[:, tb * 512 + kk:tb * 512 + kk + 512],
                                                            start=(kk == 0), stop=(kk == 4)),
                             reads=[Rdg, Rpre], writes=[self.Rps[bc]])
                    K.op("act", lambda e: e.activation(out=dst[:, tb * 512:(tb + 1) * 512], in_=self.ps[bc][:], func=AF.Silu, bias=scb[:, ci:ci + 1]),
                         reads=[self.Rps[bc], RSV], writes=[Rdst])

            def stC(n_):
                if n_ > 4:
                    return
                dst, Rdst = cdst(n_)
                for rnd in range(2):
                    bt = 2 + rnd
                    pb = self.ps[bt][:].bitcast(BF16).rearrange("p (c k) -> p c k", c=8)
                    for i in range(8):
                        tt = rnd * 8 + i
                        K.op("pe", lambda e, i=i, tt=tt: e.transpose(out=pb[:, i, :], in_=dst[:, tt * 128:(tt + 1) * 128], identity=self.ident[:]),
                             reads=[Rdst, self.Rconst], writes=[self.Rps[bt]])
                    if n_ < 4:
                        K.op("act", lambda e: e.activation(out=xs_g[:, rnd * 8:(rnd + 1) * 8, n_ * 128:(n_ + 1) * 128], in_=pb, func=AF.Copy),
                             reads=[self.Rps[bt]], writes=[Rxs])
                    else:
                        K.op("act", lambda e: e.activation(out=Btok[:, rnd * 8:(rnd + 1) * 8, :], in_=pb, func=AF.Copy),
                             reads=[self.Rps[bt]], writes=[RBtok])

            for n_ in range(len(chunks) + 2):
                if n_ < len(chunks):
                    stA(n_)
                if 0 <= n_ - 1 < len(chunks):
                    stB(n_ - 1)
                if 0 <= n_ - 2 < len(chunks):
                    stC(n_ - 2)
            K.barrier()

            for dirn in (1, 0):
                cb = dirn * 16 + g * 8
                mask = self.triT if dirn == 1 else self.tri
                order = list(range(NT - 1, -1, -1)) if dirn == 1 else list(range(NT))
                col = 0 if dirn == 1 else 127
                def mt_emit(i, order=order, col=col):
                    t = order[i]
                    a = i % 2
                    K.op("dve", lambda e: e.tensor_tensor(out=MTall[:, t, 0:5, :], in0=Dbuf[a][:, 0:5, :], in1=CBms[a].unsqueeze(1).to_broadcast([128, 5, 128]), op=ALU.mult),
                         reads=[RD[a], RCBm[a]], writes=[RMT[t]])
                    K.op("pool", lambda e: e.tensor_tensor(out=MTall[:, t, 5:8, :], in0=Dbuf[a][:, 5:8, :], in1=CBms[a].unsqueeze(1).to_broadcast([128, 3, 128]), op=ALU.mult),
                         reads=[RD[a], RCBm[a]], writes=[RMT2[t]])
                    K.op("pool", lambda e: e.tensor_copy(out=ecol[:, t, :], in_=Dbuf[a][:, :, col]), reads=[RD[a]], writes=[Recol])

                for i, t in enumerate(order):
                    a = i % 2
                    tok = slice(t * 128, (t + 1) * 128)
                    K.op("pe", lambda e, tok=tok: e.matmul(self.ps[0][:, 0:128], lhsT=BT[:, tok], rhs=CT[:, tok], start=True, stop=True),
                         reads=[RBT, RCT], writes=[self.Rps[0]])
                    K.op("dve", lambda e, a=a: e.tensor_tensor(out=CBms[a], in0=self.ps[0][:, 0:128], in1=mask[:], op=ALU.mult),
                         reads=[self.Rps[0], self.Rconst], writes=[RCBm[a]])
                    for h in range(8):
                        bnk = 1 + a * 2 + h // 4
                        dstp = self.ps[bnk][:].rearrange("p (h c) -> p h c", h=4)[:, h % 4, :]
                        K.op("pe", lambda e, h=h, t=t, dstp=dstp: e.matmul(dstp, lhsT=Pc[:, t, cb + h:cb + h + 1].to_broadcast([128, 128]), rhs=self.identf[:],
                                                                         start=True, stop=True),
                             reads=[Rdt, self.Rconst], writes=[self.Rps[bnk]])
                    for h in range(8):
                        bnk = 1 + a * 2 + h // 4
                        src = self.ps[bnk][:].rearrange("p (h c) -> p h c", h=4)[:, h % 4, :]
                        if h in (5, 6, 7):
                            K.op("act", lambda e, h=h, src=src, t=t, a=a: e.activation(out=Dbuf[a][:, h, :], in_=src, func=AF.Abs,
                                                                                     scale=1.0, bias=nb[:, t, cb + h:cb + h + 1]),
                                 reads=[self.Rps[bnk], Rdt], writes=[RD[a]])
                        else:
                            K.op("dve", lambda e, h=h, src=src, t=t, a=a: e.tensor_scalar(out=Dbuf[a][:, h, :], in0=src, scalar1=Pc[:, t, cb + h:cb + h + 1],
                                                                                        scalar2=0.0, op0=ALU.subtract, op1=(ALU.max if dirn == 1 else ALU.min)),
                                 reads=[self.Rps[bnk], Rdt], writes=[RD[a]])
                    K.op("act", lambda e, a=a: e.activation(out=Dbuf[a][:, 0:5, :], in_=Dbuf[a][:, 0:5, :], func=AF.Exp, scale=(-1.0 if dirn == 1 else 1.0)),
                         reads=[RD[a]], writes=[RD[a]])
                    K.op("act", lambda e, a=a: e.activation(out=Dbuf[a][:, 5:8, :], in_=Dbuf[a][:, 5:8, :], func=AF.Exp, scale=-1.0), reads=[RD[a]], writes=[RD[a]])
                    if i >= 1:
                        mt_emit(i - 1)
                mt_emit(NT - 1)

                K.op("dve", lambda e: e.memset(prev, 0.0), writes=[Rprev])
                K.op("dve", lambda e: e.memset(prevb, 0.0), writes=[Rprevb])

                def front2(i, dirn=dirn, cb=cb, order=order):
                    t = order[i]
                    a = i % 2
                    K.op("pool", lambda e: e.tensor_tensor(out=v8(xd[a]), in0=v8(xs_g[:, t, :]),
                                                           in1=dt[:, t, cb:cb + 8].unsqueeze(2).to_broadcast([128, 8, 64]), op=ALU.mult),
                         reads=[Rxs, Rdt], writes=[Rxd[a]])
                    K.op("pool", lambda e: e.tensor_tensor(out=v8(xdw[a]), in0=v8(xd[a]), in1=ecol[:, t, :].unsqueeze(2).to_broadcast([128, 8, 64]), op=ALU.mult),
                         reads=[Rxd[a], Recol], writes=[Rxdw[a]])
                    K.op("pe", lambda e: e.matmul(self.ps[5 + a][:], lhsT=Btok[:, t, :], rhs=xdw[a], start=True, stop=True),
                         reads=[RBtok, Rxdw[a]], writes=[self.Rps[5 + a]])

                def back(i, dirn=dirn, cb=cb, order=order):
                    t = order[i]
                    a = i % 2
                    tok = slice(t * 128, (t + 1) * 128)
                    K.op("pe", lambda e: e.matmul(self.ps[0][:], lhsT=CT[:, tok], rhs=prevb, start=True, stop=True),
                         reads=[RCT, Rprevb], writes=[self.Rps[0]])
                    for h in range(8):
                        K.op("pe", lambda e, h=h: e.matmul(self.ps[7][:, h * 64:(h + 1) * 64], lhsT=MTall[:, t, h, :], rhs=xd[a][:, h * 64:(h + 1) * 64], start=True, stop=True),
                             reads=[RMT[t] if h < 5 else RMT2[t], Rxd[a]], writes=[self.Rps[7]])
                    K.op("dve", lambda e: e.tensor_tensor(out=v8(prev), in0=v8(prev), in1=cd[:, t, cb:cb + 8].unsqueeze(2).to_broadcast([128, 8, 64]), op=ALU.mult),
                         reads=[Rprev, Rdt], writes=[Rprev])
                    K.op("dve", lambda e: e.tensor_tensor(out=prev, in0=prev, in1=self.ps[5 + a][:], op=ALU.add), reads=[Rprev, self.Rps[5 + a]], writes=[Rprev])
                    K.op("act", lambda e: e.activation(out=prevb, in_=prev, func=AF.Copy), reads=[Rprev], writes=[Rprevb])
                    K.op("dve", lambda e: e.tensor_tensor(out=v8(ysum), in0=v8(self.ps[0][:]),
                                                          in1=eoff[:, t, cb:cb + 8].unsqueeze(2).to_broadcast([128, 8, 64]), op=ALU.mult),
                         reads=[self.Rps[0], Rdt], writes=[Rysum])
                    if dirn == 1:
                        K.op("dve", lambda e: e.tensor_tensor(out=ybwd[:, t, :], in0=ysum, in1=self.ps[7][:], op=ALU.add),
                             reads=[Rysum, self.Rps[7]], writes=[Rybwd[t]])
                    else:
                        K.op("dve", lambda e: e.tensor_tensor(out=ysum, in0=ysum, in1=self.ps[7][:], op=ALU.add),
                             reads=[Rysum, self.Rps[7]], writes=[Rysum])
                        K.op("dve", lambda e: e.tensor_tensor(out=ybwd[:, t, :], in0=ybwd[:, t, :], in1=ysum, op=ALU.add),
                             reads=[Rysum, Rybwd[t]], writes=[Rybwd[t]])

                front2(0)
                for i in range(NT):
                    if i + 1 < NT:
                        front2(i + 1)
                    back(i)
            K.barrier()
            for t in range(NT):
                a = t % 2
                tok = slice(t * 128, (t + 1) * 128)
                K.op("pool", lambda e, t=t, a=a: e.tensor_tensor(out=v8(t1s[a]), in0=v8(xs_g[:, t, :]),
                                                                in1=d_row[:, g * 8:(g + 1) * 8].unsqueeze(2).to_broadcast([128, 8, 64]), op=ALU.mult),
                     reads=[Rxs, RSV], writes=[Rt1s[a]])
                K.op("pool", lambda e, t=t, a=a: e.tensor_tensor(out=ybwd[:, t, :], in0=ybwd[:, t, :], in1=t1s[a], op=ALU.add),
                     reads=[Rt1s[a], Rybwd[t]], writes=[Rybwd[t]])
                bz = 6 + a
                for k in range(8):
                    K.op("pe", lambda e, k=k, tok=tok, bz=bz: e.matmul(self.ps[bz][:], lhsT=hT[:, k, tok], rhs=Wz[:, k, :], start=(k == 0), stop=(k == 7)),
                         reads=[self.RhT[t // 4], RWz], writes=[self.Rps[bz]])
                K.op("act", lambda e, a=a, bz=bz: e.activation(out=szs[a], in_=self.ps[bz][:], func=AF.Silu), reads=[self.Rps[bz]], writes=[Rszs[a]])
                K.op("dve", lambda e, t=t, a=a: e.tensor_tensor(out=ybwd[:, t, :], in0=ybwd[:, t, :], in1=szs[a], op=ALU.mult),
                     reads=[Rszs[a], Rybwd[t]], writes=[Rybwd[t]])
                K.op("act", lambda e, t=t, a=a: e.activation(out=t1s[a], in_=ybwd[:, t, :], func=AF.Square, accum_out=self.ss[:, t:t + 1]),
                     reads=[Rybwd[t]], writes=[Rt1s[a], self.Rss])
            self.rstd_from_ss(self.ss[:, 0:NT], 512, self.Rss)
            for t in range(NT):
                a = t % 2
                tok = slice(t * 128, (t + 1) * 128)
                K.op("dve", lambda e, t=t, a=a: e.scalar_tensor_tensor(out=yns[a], in0=ybwd[:, t, :], scalar=self.ss[:, t:t + 1], in1=self.grow[:, g * 512:(g + 1) * 512],
                                                                      op0=ALU.mult, op1=ALU.mult), reads=[Rybwd[t], self.Rss, self.Rgrow], writes=[Ryns[a]])
                bt = 4 + a
                pb = self.ps[bt][:].bitcast(BF16).rearrange("p (c k) -> p c k", c=8)
                for c in range(4):
                    K.op("pe", lambda e, c=c, a=a, pb=pb: e.transpose(out=pb[:, c, :], in_=yns[a][:, c * 128:(c + 1) * 128], identity=self.ident[:]),
                         reads=[Ryns[a], self.Rconst], writes=[self.Rps[bt]])
                K.op("act", lambda e, pb=pb, tok=tok: e.activation(out=self.ymT[:, g * 4:g * 4 + 4, tok], in_=pb[:, 0:4, :], func=AF.Copy),
                     reads=[self.Rps[bt]], writes=[self.RymT[g * 4 + c] for c in range(4)])
            K.barrier()


_INPUT_ORDER = ["x", "positions", "ffn1_norm", "ffn1_w_gate", "ffn1_w_up", "ffn1_w_down", "mix_norm", "w_in",
                "ssd_conv_w", "ssd_conv_b", "ssd_dt_bias", "ssd_a_log", "ssd_d", "ssd_norm",
                "mla_q_norm", "mla_w_uq", "mla_kv_norm", "mla_w_ukv", "mla_q_head_norm", "mla_k_head_norm",
                "mla_out_norm", "conv_w", "conv_out_norm", "w_out",
                "ffn2_norm", "ffn2_w_gate", "ffn2_w_up", "ffn2_w_down"]


def make_in_maps(inputs, cores):
    maps = []
    for b in cores:
        m = {}
        for k in _INPUT_ORDER:
            a = np.asarray(inputs[k])
            if k == "x":
                m[k] = np.ascontiguousarray(a[b])
            elif k == "positions":
                m[k] = np.ascontiguousarray(a[b].reshape(S, 1).astype(np.int32))
            elif k in ("ssd_dt_bias", "ssd_a_log"):
                m[k] = np.ascontiguousarray(a.reshape(DEPTH, 32))
            else:
                m[k] = np.ascontiguousarray(a)
        maps.append(m)
    return maps


def kernel(**inputs):
    prog = Prog()
    nc = prog.build()
    in_maps = make_in_maps(inputs, list(range(8)))
    res = run_bass_kernel_spmd(nc, in_maps, core_ids=list(range(8)))
    return np.stack([np.asarray(r["out"]).reshape(S, D) for r in res.results], axis=0).astype(np.float32)
```

```python
import math
import numpy as np
from contextlib import ExitStack
import concourse.bass as bass
import concourse.mybir as mybir
from concourse.bass_utils import run_bass_kernel_spmd

F32 = mybir.dt.float32
BF16 = mybir.dt.bfloat16
I32 = mybir.dt.int32
AF = mybir.ActivationFunctionType
ALU = mybir.AluOpType
AX = mybir.AxisListType

D = 1024
S = 2048
NT = 16
DFF = 2816
NFF = 22
DEPTH = 4
D_IN = 4544
EPS = 1e-6
FFN_GROUPS = [(0, 6), (6, 12), (12, 17), (17, 22)]

C_Z = 0
C_X = 1024
C_B = 2048
C_C = 2304
C_DT = 2560
C_QL = 2592
C_KVL = 2848
C_KPE = 2976
C_CH = 3008
C_CB = 3520
C_CC = 4032


class R:
    __slots__ = ("name", "w", "r")

    def __init__(self, name):
        self.name = name
        self.w = None
        self.r = {}


class KB:
    def __init__(self, nc, es):
        self.nc = nc
        self.es = es
        self.eng = dict(pe=nc.tensor, act=nc.scalar, dve=nc.vector, pool=nc.gpsimd, sp=nc.sync)
        self.psem = {e: es.enter_context(nc.semaphore("p_" + e)) for e in self.eng}
        self.cnt = {e: 0 for e in self.eng}
        self.seen = {e: {} for e in self.eng}
        self.dsem = {}
        self.nwait = 0

    def _semh(self, key):
        if key[0] == "e":
            return self.psem[key[1]]
        return self.dsem[key][0]

    def wait(self, eng, dep):
        key, val = dep
        if key[0] == "e" and key[1] == eng:
            if eng in ("pe", "sp"):
                return
            if self.cnt[eng] - val >= 3:
                return
        if key[0] == "d":
            val = max(val, 16 * self.dsem[key][1])
        if self.seen[eng].get(key, 0) >= val:
            return
        self.eng[eng].wait_ge(self._semh(key), val)
        self.seen[eng][key] = val
        self.nwait += 1

    def _deps(self, eng, reads, writes):
        for r in reads:
            if r.w is not None:
                self.wait(eng, r.w)
        for w in writes:
            if w.w is not None:
                self.wait(eng, w.w)
            for k, v in w.r.items():
                self.wait(eng, (k, v))

    def _mark(self, me, reads, writes):
        k, v = me
        for r in reads:
            if r.r.get(k, 0) < v:
                r.r[k] = v
        for w in writes:
            w.w = me
            w.r = {}

    def record(self, f):
        old = getattr(self, "rec", None)
        self.rec = []
        f()
        out = self.rec
        self.rec = old
        return out

    def play_interleaved(self, lists):
        n = max(len(x) for x in lists)
        for i in range(n):
            for x in lists:
                if i < len(x):
                    eng, fn, reads, writes = x[i]
                    self.op(eng, fn, reads, writes)

    def op(self, eng, fn, reads=(), writes=()):
        if getattr(self, "rec", None) is not None:
            self.rec.append((eng, fn, list(reads), list(writes)))
            return None
        self._deps(eng, reads, writes)
        ins = fn(self.eng[eng])
        self.cnt[eng] += 1
        ins.then_inc(self.psem[eng], 1)
        self._mark((("e", eng), self.cnt[eng]), reads, writes)
        return ins

    def dma(self, q, out, in_, reads=(), writes=(), semres=None, **kw):
        self._deps(q, reads, writes)
        sr = semres if semres is not None else writes[0]
        key = ("d", sr.name)
        if key not in self.dsem:
            self.dsem[key] = [self.es.enter_context(self.nc.semaphore("d_" + sr.name)), 0]
        ent = self.dsem[key]
        ent[1] += 1
        self.eng[q].dma_start(out=out, in_=in_, **kw).then_inc(ent[0], 16)
        self._mark((key, 16 * ent[1]), reads, writes)

    def barrier(self):
        for e in self.eng:
            for e2 in self.eng:
                if e2 != e and self.cnt[e2] > 0:
                    self.wait(e, (("e", e2), self.cnt[e2]))
            for key, ent in self.dsem.items():
                if ent[1] > 0:
                    self.wait(e, (key, 16 * ent[1]))

    def final_wait(self, eng="sp"):
        for key, ent in self.dsem.items():
            if ent[1] > 0:
                self.wait(eng, (key, 16 * ent[1]))
        for e2 in self.eng:
            if e2 != eng and self.cnt[e2] > 0:
                self.wait(eng, (("e", e2), self.cnt[e2]))


class Prog:
    def __init__(self, n_layers=DEPTH, stages=("ffn1", "conv", "mla", "ssd", "ffn2")):
        self.n_layers = n_layers
        self.stages = stages

    def build(self):
        nc = bass.Bass("TRN2", target_bir_lowering=False)
        self.nc = nc
        L = DEPTH

        def din(name, shape, dt=F32):
            return nc.dram_tensor(name, list(shape), dt, kind="ExternalInput").ap()

        self.d = d = {}
        d["x"] = din("x", [S, D])
        d["positions"] = din("positions", [S, 1], I32)
        d["ffn1_norm"] = din("ffn1_norm", [L, D])
        d["ffn1_w_gate"] = din("ffn1_w_gate", [L, D, DFF])
        d["ffn1_w_up"] = din("ffn1_w_up", [L, D, DFF])
        d["ffn1_w_down"] = din("ffn1_w_down", [L, DFF, D])
        d["mix_norm"] = din("mix_norm", [L, D])
        d["w_in"] = din("w_in", [L, D, D_IN])
        d["ssd_conv_w"] = din("ssd_conv_w", [L, 5, 1536])
        d["ssd_conv_b"] = din("ssd_conv_b", [L, 1536])
        d["ssd_dt_bias"] = din("ssd_dt_bias", [L, 32])
        d["ssd_a_log"] = din("ssd_a_log", [L, 32])
        d["ssd_d"] = din("ssd_d", [L, 16])
        d["ssd_norm"] = din("ssd_norm", [L, 1024])
        d["mla_q_norm"] = din("mla_q_norm", [L, 256])
        d["mla_w_uq"] = din("mla_w_uq", [L, 256, 768])
        d["mla_kv_norm"] = din("mla_kv_norm", [L, 128])
        d["mla_w_ukv"] = din("mla_w_ukv", [L, 128, 1024])
        d["mla_q_head_norm"] = din("mla_q_head_norm", [L, 96])
        d["mla_k_head_norm"] = din("mla_k_head_norm", [L, 96])
        d["mla_out_norm"] = din("mla_out_norm", [L, 512])
        d["conv_w"] = din("conv_w", [L, 3, 512])
        d["conv_out_norm"] = din("conv_out_norm", [L, 512])
        d["w_out"] = din("w_out", [L, 2048, D])
        d["ffn2_norm"] = din("ffn2_norm", [L, D])
        d["ffn2_w_gate"] = din("ffn2_w_gate", [L, D, DFF])
        d["ffn2_w_up"] = din("ffn2_w_up", [L, D, DFF])
        d["ffn2_w_down"] = din("ffn2_w_down", [L, DFF, D])
        self.out = nc.dram_tensor("out", [S, D], F32, kind="ExternalOutput").ap()

        with ExitStack() as es:
            self.es = es
            K = self.K = KB(nc, es)

            def sb(name, shape, dt):
                return es.enter_context(nc.sbuf_tensor(name, list(shape), dt))

            self.X = sb("X", [128, NT, D], F32)
            self.RX = [R(f"X{t}") for t in range(NT)]
            self.RXs = R("Xsem")
            self.Rxsps = R("xspsem")
            self.hT = sb("hT", [128, 8, S], BF16)
            self.RhT = [R(f"hT{b}") for b in range(4)]
            self.WS = sb("WS", [128, 36864], BF16)
            self.EX = sb("EX", [128, 14336], BF16)
            self.ident = sb("ident", [128, 128], BF16)
            self.identf = sb("identf", [128, 128], F32)
            self.grow = sb("grow", [128, D], F32)
            self.Rgrow = R("grow")
            self.ss = sb("ss", [128, 2 * NT], F32)
            self.SV = sb("SV", [128, 512], F32)
            self.st2 = sb("st2", [128, 64], F32)
            self.RSV = R("SV")
            self.BD = sb("BD", [128, 128], BF16)
            self.xsp = nc.dram_tensor("xspill", [S, D], F32, kind="Internal").ap()
            self.Rxsp = [R(f"xsp{t}") for t in range(NT)]
            self.ymT = self.X[:].rearrange("p t c -> p (t c)").bitcast(BF16).rearrange("p (j s) -> p j s", j=16)
            self.RymT = [R(f"ymT{j}") for j in range(16)]
            self.Rss = R("ss")
            self.junk = self.EX[:, 9216:10240]
            self.Rjunk = R("junk")
            self.ps = [es.enter_context(nc.psum_tensor(f"ps{i}", [128, 512], F32)) for i in range(8)]
            self.Rps = [R(f"ps{i}") for i in range(8)]
            self.Rconst = R("const")

            K.op("pool", lambda e: e.memset(self.ident[:], 0.0), writes=[self.Rconst])
            K.op("pool", lambda e: e.affine_select(out=self.ident[:], in_=self.ident[:], pattern=[[-1, 128]],
                                                   compare_op=ALU.not_equal, fill=1.0, base=0, channel_multiplier=1),
                 writes=[self.Rconst])
            K.op("pool", lambda e: e.memset(self.identf[:], 0.0), writes=[self.Rconst])
            K.op("pool", lambda e: e.affine_select(out=self.identf[:], in_=self.identf[:], pattern=[[-1, 128]],
                                                   compare_op=ALU.not_equal, fill=1.0, base=0, channel_multiplier=1),
                 writes=[self.Rconst])

            K.op("pool", lambda e: e.memset(self.BD[:], 0.0), writes=[self.Rconst])
            K.op("pool", lambda e: e.memset(self.BD[0:64, 0:64], 1.0), writes=[self.Rconst])
            K.op("pool", lambda e: e.memset(self.BD[64:128, 64:128], 1.0), writes=[self.Rconst])
            self.cs = sb("cs", [128, 2, NT, 16], F32)
            self.tri = sb("tri", [128, 128], F32)
            self.triT = sb("triT", [128, 128], F32)
            self.onesf = sb("onesf", [128, 128], F32)
            for tt_, st_, cm_ in ((self.tri, 1, -1), (self.triT, -1, 1)):
                K.op("pool", lambda e, tt_=tt_: e.memset(tt_[:], 1.0), writes=[self.Rconst])
                K.op("pool", lambda e, tt_=tt_, st_=st_, cm_=cm_: e.affine_select(out=tt_[:], in_=tt_[:], pattern=[[st_, 128]], compare_op=ALU.is_ge, fill=0.0,
                                                                                 base=0, channel_multiplier=cm_), writes=[self.Rconst])
            K.op("pool", lambda e: e.memset(self.onesf[:], 1.0), writes=[self.Rconst])
            self.Rcs = R("cs")
            self.rope_setup()
            xv = d["x"].rearrange("(t p) c -> p t c", p=128)
            for t in range(NT):
                K.dma("sp", out=self.X[:, t, :], in_=xv[:, t, :], writes=[self.RX[t]], semres=self.RXs)

            for l in range(self.n_layers):
                if "ffn1" in self.stages:
                    self.norm_stage(d["ffn1_norm"][l])
                    self.ffn_stage(d["ffn1_w_gate"][l], d["ffn1_w_up"][l], d["ffn1_w_down"][l])
                mix = [s for s in self.stages if s in ("conv", "mla", "ssd")]
                if mix:
                    self.norm_stage(d["mix_norm"][l])
                    self.mixer_begin(l)
                    K.barrier()
                    if "ssd" in mix:
                        self.ssd_stage(l)
                        K.barrier()
                    if "conv" in mix:
                        self.conv_stage(l)
                        K.barrier()
                    if "mla" in mix:
                        self.mla_stage(l)
                        K.barrier()
                    self.mixer_end(l, mix)
                    K.barrier()
                if "ffn2" in self.stages:
                    self.norm_stage(d["ffn2_norm"][l])
                    self.ffn_stage(d["ffn2_w_gate"][l], d["ffn2_w_up"][l], d["ffn2_w_down"][l])

            ov = self.out.rearrange("(t p) c -> p t c", p=128)
            Rout = R("out")
            for t in range(NT):
                K.dma("sp", out=ov[:, t, :], in_=self.X[:, t, :], reads=[self.RX[t]], writes=[Rout])
            K.final_wait("sp")
        return nc

    def wsv(self, off, n, dt=BF16, base=None):
        base = self.WS if base is None else base
        if dt == F32:
            return base[:, off:off + 2 * n].bitcast(F32)
        return base[:, off:off + n]

    def norm_stage(self, gain):
        K = self.K
        X, hT = self.X, self.hT
        K.dma("sp", out=self.grow[:], in_=gain.partition_broadcast(128), writes=[self.Rgrow])
        for t in range(NT):
            K.op("act", lambda e, t=t: e.activation(out=self.junk, in_=X[:, t, :], func=AF.Square,
                                                    accum_out=self.ss[:, t:t + 1]),
                 reads=[self.RX[t]], writes=[self.Rss])
        K.op("act", lambda e: e.activation(out=self.ss[:, NT:2 * NT], in_=self.ss[:, 0:NT], func=AF.Sqrt,
                                           scale=1.0 / D, bias=EPS),
             reads=[self.Rss], writes=[self.Rss])
        K.op("dve", lambda e: e.reciprocal(out=self.ss[:, NT:2 * NT], in_=self.ss[:, NT:2 * NT]),
             reads=[self.Rss], writes=[self.Rss])
        xs = [self.wsv(7168, D, BF16, self.EX), self.wsv(8192, D, BF16, self.EX)]
        if not hasattr(self, "Rxsn"):
            self.Rxsn = [R("xs0"), R("xs1")]
        Rxs = self.Rxsn
        for t in range(NT):
            j = t % 2
            K.op("dve", lambda e, t=t, j=j: e.scalar_tensor_tensor(out=xs[j], in0=X[:, t, :],
                                                                   scalar=self.ss[:, NT + t:NT + t + 1],
                                                                   in1=self.grow[:], op0=ALU.mult, op1=ALU.mult),
                 reads=[self.RX[t], self.Rss, self.Rgrow], writes=[Rxs[j]])
            bank = t % 2
            pb = self.ps[bank][:].bitcast(BF16).rearrange("p (c k) -> p c k", c=8)
            for c in range(8):
                K.op("pe", lambda e, c=c, j=j, pb=pb: e.transpose(out=pb[:, c, :], in_=xs[j][:, c * 128:(c + 1) * 128],
                                                                  identity=self.ident[:]),
                     reads=[Rxs[j], self.Rconst], writes=[self.Rps[bank]])
            K.op("act", lambda e, t=t, pb=pb: e.activation(out=hT[:, :, t * 128:(t + 1) * 128], in_=pb, func=AF.Copy),
                 reads=[self.Rps[bank]], writes=[self.RhT[t // 4]])

    def ffn_stage(self, Wg, Wu, Wd):
        K = self.K
        X, hT = self.X, self.hT
        SL = 18432
        WG = [self.wsv(s * SL, 6144).rearrange("p (k c) -> p k c", k=8) for s in range(2)]
        WU = [self.wsv(s * SL + 6144, 6144).rearrange("p (k c) -> p k c", k=8) for s in range(2)]
        WD = [self.wsv(s * SL + 12288, 6144).rearrange("p (f c) -> p f c", f=6) for s in range(2)]
        if not hasattr(self, "RWffn"):
            self.RWffn = [R("ffw0"), R("ffw1")]
            self.Ractffn = [R("act0"), R("act1")]
            self.Rsilffn = [R("sil0"), R("sil1")]
        RW = self.RWffn
        act = [self.wsv(a * 3072, 3072, BF16, self.EX).rearrange("p (f c) -> p f c", f=6) for a in range(2)]
        Ract = self.Ractffn
        sil = [self.wsv(6144 + a * 512, 512, BF16, self.EX) for a in range(2)]
        Rsil = self.Rsilffn
        Wgv = Wg.rearrange("(k p) c -> p k c", p=128)
        Wuv = Wu.rearrange("(k p) c -> p k c", p=128)

        def load(q):
            f0, f1 = FFN_GROUPS[q]
            nf = f1 - f0
            s = q % 2
            K.dma("pool", out=WG[s][:, :, 0:nf * 128], in_=Wgv[:, :, f0 * 128:f1 * 128], writes=[RW[s]])
            K.dma("pool", out=WU[s][:, :, 0:nf * 128], in_=Wuv[:, :, f0 * 128:f1 * 128], writes=[RW[s]])
            K.dma("pool", out=WD[s][:, 0:nf, :], in_=Wd[f0 * 128:f1 * 128, :].rearrange("(f p) c -> p f c", p=128),
                  writes=[RW[s]])

        load(0)
        it = 0
        ab = 0
        for q in range(4):
            if q + 1 < 4:
                load(q + 1)
            f0, f1 = FFN_GROUPS[q]
            nf = f1 - f0
            s = q % 2
            for tb in range(4):
                for f in range(nf):
                    bg = (it % 2) * 2
                    bu = bg + 1
                    si = it % 2
                    it += 1
                    for k in range(8):
                        K.op("pe", lambda e, k=k, f=f, bg=bg: e.matmul(self.ps[bg][:], lhsT=WG[s][:, k, f * 128:(f + 1) * 128],
                                                                       rhs=hT[:, k, tb * 512:(tb + 1) * 512], start=(k == 0), stop=(k == 7)),
                             reads=[RW[s], self.RhT[tb]], writes=[self.Rps[bg]])
                    for k in range(8):
                        K.op("pe", lambda e, k=k, f=f, bu=bu: e.matmul(self.ps[bu][:], lhsT=WU[s][:, k, f * 128:(f + 1) * 128],
                                                                       rhs=hT[:, k, tb * 512:(tb + 1) * 512], start=(k == 0), stop=(k == 7)),
                             reads=[RW[s], self.RhT[tb]], writes=[self.Rps[bu]])
                    K.op("act", lambda e, bg=bg, si=si: e.activation(out=sil[si], in_=self.ps[bg][:], func=AF.Silu),
                         reads=[self.Rps[bg]], writes=[Rsil[si]])
                    K.op("dve", lambda e, bu=bu, si=si, f=f: e.tensor_tensor(out=act[ab][:, f, :], in0=self.ps[bu][:], in1=sil[si], op=ALU.mult),
                         reads=[self.Rps[bu], Rsil[si]], writes=[Ract[ab]])
                for tt in range(4):
                    t = tb * 4 + tt
                    for half in range(2):
                        bo = 4 + (t % 2) * 2 + half
                        for f in range(nf):
                            K.op("pe", lambda e, f=f, bo=bo, tt=tt, half=half: e.matmul(
                                self.ps[bo][:], lhsT=act[ab][:, f, tt * 128:(tt + 1) * 128],
                                rhs=WD[s][:, f, half * 512:(half + 1) * 512], start=(f == 0), stop=(f == nf - 1)),
                                 reads=[Ract[ab], RW[s]], writes=[self.Rps[bo]])
                        K.op("dve", lambda e, bo=bo, t=t, half=half: e.scalar_tensor_tensor(
                            out=X[:, t, half * 512:(half + 1) * 512], in0=self.ps[bo][:], scalar=0.5,
                            in1=X[:, t, half * 512:(half + 1) * 512], op0=ALU.mult, op1=ALU.add),
                             reads=[self.Rps[bo], self.RX[t]], writes=[self.RX[t]])
                ab ^= 1

    def mixer_begin(self, l):
        K = self.K
        xv = self.xsp.rearrange("(t p) c -> p t c", p=128)
        for t in range(NT):
            K.dma("sp", out=xv[:, t, :], in_=self.X[:, t, :], reads=[self.RX[t]], writes=[self.Rxsp[t]], semres=self.Rxsps)

    def mixer_end(self, l, mix):
        K = self.K
        chunks = []
        if "ssd" in mix:
            chunks += list(range(0, 8))
        if "mla" in mix:
            chunks += list(range(8, 12))
        if "conv" in mix:
            chunks += list(range(12, 16))
        wo = self.wsv(2048, 16384).rearrange("p (j c) -> p j c", j=16)
        if getattr(self, "wo_ready_layer", None) == l:
            Rwo = self.Rwo_pref
        else:
            Rwo = R("wo")
            wov = self.d["w_out"][l].rearrange("(j p) c -> p j c", p=128)
            for j0 in range(0, 16, 4):
                K.dma("pool", out=wo[:, j0:j0 + 4, :], in_=wov[:, j0:j0 + 4, :], writes=[Rwo])
        stg = [self.wsv(18432 + i * 2048, 1024, F32) for i in range(9)] + [self.wsv(i * 2048, 1024, F32, self.EX) for i in range(7)]
        Rstg = [R(f"stg{t}") for t in range(NT)]
        Rstgs = R("stgsem")
        xv = self.xsp.rearrange("(t p) c -> p t c", p=128)
        for t in range(NT):
            K.dma("sp", out=stg[t], in_=xv[:, t, :], reads=[self.Rxsp[t], Rwo], writes=[Rstg[t]], semres=Rstgs)
        for t in range(NT):
            for half in range(2):
                bo = (t % 2) * 2 + half
                for i, j in enumerate(chunks):
                    K.op("pe", lambda e, j=j, bo=bo, half=half, i=i: e.matmul(
                        self.ps[bo][:], lhsT=self.ymT[:, j, t * 128:(t + 1) * 128],
                        rhs=wo[:, j, half * 512:(half + 1) * 512], start=(i == 0), stop=(i == len(chunks) - 1)),
                         reads=[self.RymT[j], Rwo], writes=[self.Rps[bo]])
                K.op("dve", lambda e, bo=bo, t=t, half=half: e.tensor_tensor(
                    out=stg[t][:, half * 512:(half + 1) * 512], in0=self.ps[bo][:],
                    in1=stg[t][:, half * 512:(half + 1) * 512], op=ALU.add),
                     reads=[self.Rps[bo], Rstg[t]], writes=[Rstg[t]])
        K.barrier()
        engs = ["act", "dve", "act", "pool"]
        for t in range(NT):
            en = engs[t % 4]
            if en == "act":
                K.op("act", lambda e, t=t: e.activation(out=self.X[:, t, :], in_=stg[t], func=AF.Copy), reads=[Rstg[t]], writes=[self.RX[t]])
            else:
                K.op(en, lambda e, t=t: e.tensor_copy(out=self.X[:, t, :], in_=stg[t]), reads=[Rstg[t]], writes=[self.RX[t]])

    def conv_stage(self, l):
        K = self.K
        d = self.d
        hT = self.hT
        SV = self.SV
        cw = SV[:, 0:12].rearrange("p (j k) -> p j k", j=4)
        gcol = SV[:, 12:16]
        for j in range(4):
            K.dma("sp", out=cw[:, j, :], in_=d["conv_w"][l][:, j * 128:(j + 1) * 128].rearrange("k p -> p k"), writes=[self.RSV],
                  allow_slow_non_contiguous=True)
        K.dma("sp", out=gcol, in_=d["conv_out_norm"][l].rearrange("(j p) -> p j", p=128), writes=[self.RSV],
              allow_slow_non_contiguous=True)
        Winv = d["w_in"][l].rearrange("(k p) c -> p k c", p=128)
        EX = self.EX
        wcall = [self.wsv(i * 4096, 4096).rearrange("p (k c) -> p k c", k=8) for i in range(3)]
        Rwc = [R("wcall"), R("wcall_")]
        ms = [self.wsv(12288, 2052), self.wsv(0, 2052, BF16, EX)]
        dgc = [self.wsv(32776 + a * 384, 384).rearrange("p (k c) -> p k c", k=3) for a in range(2)]
        Rdgc = [R("dgc0"), R("dgc1")]
        cbufs = [self.wsv(16392, 2048, F32), self.wsv(4104, 2048, F32, EX)]
        ys = [self.wsv(20488, 2048, F32), self.wsv(8200, 2048, F32, EX)]
        ysqs = [self.wsv(24584, 2048), self.wsv(26632, 2048)]
        rs = [self.wsv(28680 + a * 1024, 512, F32) for a in range(2)]
        tmp = [self.wsv(30728 + a * 1024, 512, F32) for a in range(2)]
        for i in range(3):
            K.dma("pool", out=wcall[i], in_=Winv[:, :, (C_CH, C_CB, C_CC)[i]:(C_CH, C_CB, C_CC)[i] + 512], writes=[Rwc[0]])
        Rms, Rcbs, Rys, Rysqs = [R("cm0"), R("cm1")], [R("ccb0"), R("ccb1")], [R("cy0"), R("cy1")], [R("cysq0"), R("cysq1")]
        Rrs = [R("crs0"), R("crs1")]
        Rtmp = [R("ctmp0"), R("ctmp1")]
        for pp in range(2):
            K.op("pool", lambda e, pp=pp: e.memset(ms[pp][:, 0:1], 0.0), writes=[Rms[pp]])
            K.op("pool", lambda e, pp=pp: e.memset(ms[pp][:, 2049:2052], 0.0), writes=[Rms[pp]])
        cols = (C_CH, C_CB, C_CC)

        def stA(j):
            sl = j % 2
            m, cbuf = ms[sl], cbufs[sl]
            for tb in range(4):
                b0 = 3 * (tb % 2)
                a = tb % 2
                for i in range(3):
                    for k in range(8):
                        K.op("pe", lambda e, i=i, k=k: e.matmul(self.ps[b0 + i][:], lhsT=wcall[i][:, k, j * 128:(j + 1) * 128],
                                                               rhs=hT[:, k, tb * 512:(tb + 1) * 512], start=(k == 0), stop=(k == 7)),
                             reads=[Rwc[0], self.RhT[tb]], writes=[self.Rps[b0 + i]])
                K.op("act", lambda e: e.activation(out=tmp[a], in_=self.ps[b0 + 2][:], func=AF.Copy),
                     reads=[self.Rps[b0 + 2]], writes=[Rtmp[a]])
                K.op("dve", lambda e: e.tensor_tensor(out=m[:, 1 + tb * 512:1 + (tb + 1) * 512], in0=self.ps[b0][:], in1=tmp[a], op=ALU.mult),
                     reads=[self.Rps[b0], Rtmp[a]], writes=[Rms[sl]])
                K.op("act", lambda e: e.activation(out=cbuf[:, tb * 512:(tb + 1) * 512], in_=self.ps[b0 + 1][:], func=AF.Copy),
                     reads=[self.Rps[b0 + 1]], writes=[Rcbs[sl]])

        def stB(j):
            sl = j % 2
            m, cbuf, y, ysq = ms[sl], cbufs[sl], ys[sl], ysqs[sl]
            for kk in range(3):
                K.op("dve", lambda e, kk=kk: e.tensor_scalar(out=dgc[sl][:, kk, :], in0=self.ident[:], scalar1=cw[:, j, kk:kk + 1], scalar2=None, op0=ALU.mult),
                     reads=[self.Rconst, self.RSV], writes=[Rdgc[sl]])
            for tb in range(4):
                for kk in range(3):
                    K.op("pe", lambda e, kk=kk: e.matmul(self.ps[6][:], lhsT=dgc[sl][:, kk, :], rhs=m[:, tb * 512 + kk:tb * 512 + kk + 512],
                                                        start=(kk == 0), stop=(kk == 2)),
                         reads=[Rdgc[sl], Rms[sl]], writes=[self.Rps[6]])
                K.op("dve", lambda e: e.tensor_tensor(out=y[:, tb * 512:(tb + 1) * 512], in0=self.ps[6][:], in1=cbuf[:, tb * 512:(tb + 1) * 512], op=ALU.mult),
                     reads=[self.Rps[6], Rcbs[sl]], writes=[Rys[sl]])
            K.op("act", lambda e: e.activation(out=ysq, in_=y, func=AF.Square), reads=[Rys[sl]], writes=[Rysqs[sl]])

        def stC(j):
            sl = j % 2
            y, ysq = ys[sl], ysqs[sl]
            for tb in range(4):
                a = tb % 2
                bb = 7
                K.op("pe", lambda e: e.matmul(self.ps[bb][:], lhsT=self.BD[:], rhs=ysq[:, tb * 512:(tb + 1) * 512], start=True, stop=True),
                     reads=[Rysqs[sl], self.Rconst], writes=[self.Rps[bb]])
                K.op("act", lambda e: e.activation(out=rs[a], in_=self.ps[bb][:], func=AF.Sqrt, scale=1.0 / 64, bias=EPS),
                     reads=[self.Rps[bb]], writes=[Rrs[a]])
                K.op("dve", lambda e: e.reciprocal(out=rs[a], in_=rs[a]), reads=[Rrs[a]], writes=[Rrs[a]])
                K.op("dve", lambda e: e.scalar_tensor_tensor(out=self.ymT[:, 12 + j, tb * 512:(tb + 1) * 512],
                                                             in0=y[:, tb * 512:(tb + 1) * 512], scalar=gcol[:, j:j + 1],
                                                             in1=rs[a], op0=ALU.mult, op1=ALU.mult),
                     reads=[Rys[sl], Rrs[a], self.RSV], writes=[self.RymT[12 + j]])

        for j in range(4 + 2):
            if j < 4:
                stA(j)
            if 0 <= j - 1 < 4:
                stB(j - 1)
            if 0 <= j - 2 < 4:
                stC(j - 2)

    def rope_setup(self):
        K = self.K
        EX = self.EX
        posi = EX[:, 0:32].bitcast(I32)
        posf = self.wsv(32, 16, F32, EX)
        ang = self.wsv(64, 256, F32, EX).rearrange("p (t i) -> p t i", t=NT)
        nf = self.wsv(576, 256, F32, EX).rearrange("p (t i) -> p t i", t=NT)
        ni = EX[:, 1088:1600].bitcast(I32).rearrange("p (t i) -> p t i", t=NT)
        msk = self.wsv(1600, 256, F32, EX).rearrange("p (t i) -> p t i", t=NT)
        yy = self.wsv(2112, 256, F32, EX).rearrange("p (t i) -> p t i", t=NT)
        Rr = R("ropetmp")
        K.dma("sp", out=posi, in_=self.d["positions"].rearrange("(t p) o -> p (t o)", p=128), writes=[Rr],
              allow_slow_non_contiguous=True)
        K.op("dve", lambda e: e.tensor_copy(out=posf, in_=posi), reads=[Rr], writes=[Rr])
        for i in range(16):
            inv = float(10000.0 ** (-i / 16.0))
            K.op("dve", lambda e, i=i, inv=inv: e.tensor_scalar(out=ang[:, :, i], in0=posf, scalar1=inv, scalar2=None, op0=ALU.mult),
                 reads=[Rr], writes=[Rr])
        TWO_PI = 2.0 * math.pi
        C1 = 6.28125
        C2 = TWO_PI - C1
        K.op("dve", lambda e: e.tensor_scalar(out=nf, in0=ang, scalar1=1.0 / TWO_PI, scalar2=None, op0=ALU.mult), reads=[Rr], writes=[Rr])
        K.op("dve", lambda e: e.tensor_copy(out=ni, in_=nf), reads=[Rr], writes=[Rr])
        K.op("dve", lambda e: e.tensor_copy(out=nf, in_=ni), reads=[Rr], writes=[Rr])
        K.op("dve", lambda e: e.scalar_tensor_tensor(out=ang, in0=nf, scalar=-C1, in1=ang, op0=ALU.mult, op1=ALU.add), reads=[Rr], writes=[Rr])
        K.op("dve", lambda e: e.scalar_tensor_tensor(out=ang, in0=nf, scalar=-C2, in1=ang, op0=ALU.mult, op1=ALU.add), reads=[Rr], writes=[Rr])
        for which, shift in ((1, 0.0), (0, math.pi / 2)):
            K.op("dve", lambda e, shift=shift: e.tensor_scalar(out=yy, in0=ang, scalar1=shift, scalar2=None, op0=ALU.add), reads=[Rr], writes=[Rr])
            for _ in range(2):
                K.op("dve", lambda e: e.tensor_scalar(out=msk, in0=yy, scalar1=math.pi, scalar2=-TWO_PI, op0=ALU.is_gt, op1=ALU.mult), reads=[Rr], writes=[Rr])
                K.op("dve", lambda e: e.tensor_tensor(out=yy, in0=yy, in1=msk, op=ALU.add), reads=[Rr], writes=[Rr])
                K.op("dve", lambda e: e.tensor_scalar(out=msk, in0=yy, scalar1=-math.pi, scalar2=TWO_PI, op0=ALU.is_lt, op1=ALU.mult), reads=[Rr], writes=[Rr])
                K.op("dve", lambda e: e.tensor_tensor(out=yy, in0=yy, in1=msk, op=ALU.add), reads=[Rr], writes=[Rr])
            K.op("dve", lambda e: e.tensor_scalar(out=yy, in0=yy, scalar1=math.pi, scalar2=-math.pi, op0=ALU.min, op1=ALU.max), reads=[Rr], writes=[Rr])
            K.op("act", lambda e, which=which: e.activation(out=self.cs[:, which, :, :], in_=yy, func=AF.Sin), reads=[Rr], writes=[self.Rcs])
        K.barrier()

    def rstd_from_ss(self, ss_ap, n, Rs):
        K = self.K
        K.op("act", lambda e: e.activation(out=ss_ap, in_=ss_ap, func=AF.Sqrt, scale=1.0 / n, bias=EPS), reads=[Rs], writes=[Rs])
        K.op("dve", lambda e: e.reciprocal(out=ss_ap, in_=ss_ap), reads=[Rs], writes=[Rs])

    def rope(self, x, t, nh, tmp, Rx, Rt):
        K = self.K
        cosb = self.cs[:, 0, t, :].unsqueeze(1).to_broadcast([128, nh, 16])
        sinb = self.cs[:, 1, t, :].unsqueeze(1).to_broadcast([128, nh, 16])
        x1 = x[:, :, 0:16]
        x2 = x[:, :, 16:32]
        for i, (a, b) in enumerate(((x1, cosb), (x2, sinb), (x1, sinb), (x2, cosb))):
            K.op("dve", lambda e, i=i, a=a, b=b: e.tensor_tensor(out=tmp[:, i, :, :], in0=a, in1=b, op=ALU.mult),
                 reads=[Rx, self.Rcs], writes=[Rt])
        K.op("dve", lambda e: e.tensor_tensor(out=x1, in0=tmp[:, 0, :, :], in1=tmp[:, 1, :, :], op=ALU.subtract), reads=[Rt], writes=[Rx])
        K.op("dve", lambda e: e.tensor_tensor(out=x2, in0=tmp[:, 2, :, :], in1=tmp[:, 3, :, :], op=ALU.add), reads=[Rt], writes=[Rx])

    def mla_stage(self, l):
        K = self.K
        d = self.d
        hT = self.hT
        EX = self.EX
        grow = self.grow
        SV = self.SV
        Rg = self.Rgrow
        K.dma("sp", out=grow[:, 0:256], in_=d["mla_q_norm"][l].partition_broadcast(128), writes=[Rg])
        K.dma("sp", out=grow[:, 256:384], in_=d["mla_kv_norm"][l].partition_broadcast(128), writes=[Rg])
        K.dma("sp", out=grow[:, 384:480], in_=d["mla_q_head_norm"][l].partition_broadcast(128), writes=[Rg])
        K.dma("sp", out=grow[:, 480:576], in_=d["mla_k_head_norm"][l].partition_broadcast(128), writes=[Rg])
        K.dma("sp", out=SV[:, 0:512], in_=d["mla_out_norm"][l].partition_broadcast(128), writes=[self.RSV])
        K.op("dve", lambda e: e.tensor_scalar(out=grow[:, 384:480], in0=grow[:, 384:480], scalar1=float(96 ** -0.5), scalar2=None, op0=ALU.mult),
             reads=[Rg], writes=[Rg])
        gq = grow[:, 0:256]
        gkv = grow[:, 256:384]
        gqh = grow[:, 384:480]
        gkh = grow[:, 480:576]
        Winv = d["w_in"][l].rearrange("(k p) c -> p k c", p=128)
        wm = self.wsv(0, 3328).rearrange("p (k c) -> p k c", k=8)
        qnkT = self.wsv(3328, 6144).rearrange("p (j s) -> p j s", j=3)
        kpe = self.wsv(9472, 512, F32).rearrange("p (t i) -> p t i", t=NT)
        Rwm, RqnkT, Rkpe = R("wm"), R("qnkT"), R("kpe")
        K.dma("pool", out=wm, in_=Winv[:, :, C_QL:C_QL + 416], writes=[Rwm])
        qn = [self.wsv(a * 384, 384, BF16, EX) for a in range(2)]
        Rqn = [R("qn0"), R("qn1")]
        ssa = self.ss
        Rss = self.Rss
        SVs = self.st2
        ssa = [SVs[:, 2 * a:2 + 2 * a] for a in range(2)]
        Rssa = [R("ssa0"), R("ssa1")]

        def A_mm(t):
            b = t % 2
            for k in range(8):
                K.op("pe", lambda e, k=k: e.matmul(self.ps[b][:, 0:416], lhsT=hT[:, k, t * 128:(t + 1) * 128], rhs=wm[:, k, :],
                                                  start=(k == 0), stop=(k == 7)),
                     reads=[self.RhT[t // 4], Rwm], writes=[self.Rps[b]])

        def A_el(t):
            a = t % 2
            b = t % 2
            K.op("act", lambda e: e.activation(out=self.junk[:, 0:256], in_=self.ps[b][:, 0:256], func=AF.Square, accum_out=ssa[a][:, 0:1]),
                 reads=[self.Rps[b]], writes=[Rssa[a]])
            K.op("act", lambda e: e.activation(out=self.junk[:, 256:384], in_=self.ps[b][:, 256:384], func=AF.Square, accum_out=ssa[a][:, 1:2]),
                 reads=[self.Rps[b]], writes=[Rssa[a]])
            self.rstd_from_ss(ssa[a][:, 0:1], 256, Rssa[a])
            self.rstd_from_ss(ssa[a][:, 1:2], 128, Rssa[a])
            K.op("dve", lambda e: e.scalar_tensor_tensor(out=qn[a][:, 0:256], in0=self.ps[b][:, 0:256], scalar=ssa[a][:, 0:1], in1=gq,
                                                         op0=ALU.mult, op1=ALU.mult), reads=[self.Rps[b], Rssa[a], Rg], writes=[Rqn[a]])
            K.op("dve", lambda e: e.scalar_tensor_tensor(out=qn[a][:, 256:384], in0=self.ps[b][:, 256:384], scalar=ssa[a][:, 1:2], in1=gkv,
                                                         op0=ALU.mult, op1=ALU.mult), reads=[self.Rps[b], Rssa[a], Rg], writes=[Rqn[a]])
            K.op("act", lambda e: e.activation(out=kpe[:, t, :], in_=self.ps[b][:, 384:416], func=AF.Copy), reads=[self.Rps[b]], writes=[Rkpe])

        def A_tr(t):
            a = t % 2
            bt = 2 + t % 2
            pb = self.ps[bt][:].bitcast(BF16).rearrange("p (c k) -> p c k", c=8)
            for c in range(3):
                K.op("pe", lambda e, c=c: e.transpose(out=pb[:, c, :], in_=qn[a][:, c * 128:(c + 1) * 128], identity=self.ident[:]),
                     reads=[Rqn[a], self.Rconst], writes=[self.Rps[bt]])
            K.op("act", lambda e: e.activation(out=qnkT[:, :, t * 128:(t + 1) * 128], in_=pb[:, 0:3, :], func=AF.Copy),
                 reads=[self.Rps[bt]], writes=[RqnkT])

        for t in range(NT + 2):
            if t < NT:
                A_mm(t)
            if 0 <= t - 1 < NT:
                A_el(t - 1)
            if 0 <= t - 2 < NT:
                A_tr(t - 2)
        K.barrier()
        qT = self.hT
        kT = self.wsv(10496, 16384).rearrange("p (h s) -> p h s", h=8)
        v1 = self.wsv(26880, 8320).rearrange("p (t h c) -> p t h c", t=NT, h=8)
        RqT, RkT, Rv1 = R("qT"), R("kT"), R("v1")
        wuq = self.wsv(0, 1536, BF16, EX).rearrange("p (j c) -> p j c", j=2)
        wukv = self.wsv(1536, 1024, BF16, EX)
        osb = self.wsv(2560, 8192, BF16, EX).rearrange("p (t h c) -> p t h c", t=NT, h=8)
        ET = [self.wsv(10752 + a * 512, 512, BF16, EX) for a in range(4)]
        Rwu, Rosb = R("wu"), R("osb")
        RET = [R(f"ET{a}") for a in range(4)]
        K.dma("pool", out=wuq, in_=d["mla_w_uq"][l].rearrange("(j p) c -> p j c", p=128), writes=[Rwu])
        K.dma("pool", out=wukv, in_=d["mla_w_ukv"][l], writes=[Rwu])
        K.op("pool", lambda e: e.memset(v1[:, :, :, 64:65], 1.0), writes=[Rv1])

        def f4(base, off, n):
            return self.wsv(off, n, F32, base).rearrange("p (h c) -> p h c", h=4)
        tq = [f4(self.WS, 0, 384), f4(EX, 3072, 384)]
        tk = [f4(self.WS, 768, 384), f4(EX, 3840, 384)]
        tsq = [f4(self.WS, 1536, 384), f4(EX, 4608, 384)]
        qkbq = [self.wsv(2304, 384).rearrange("p (h c) -> p h c", h=4), self.wsv(5376, 384, BF16, EX).rearrange("p (h c) -> p h c", h=4)]
        qkbk = [self.wsv(2688, 384).rearrange("p (h c) -> p h c", h=4), self.wsv(5760, 384, BF16, EX).rearrange("p (h c) -> p h c", h=4)]
        trope = [self.wsv(2560, 256, F32, EX).rearrange("p (i h c) -> p i h c", i=4, h=4),
                 self.wsv(6144, 256, F32, EX).rearrange("p (i h c) -> p i h c", i=4, h=4)]
        Rtq, Rtk, Rtsq = [R("tq0"), R("tq1")], [R("tk0"), R("tk1")], [R("tsq0"), R("tsq1")]
        Rqkbq, Rqkbk, Rtrope = [R("qkbq0"), R("qkbq1")], [R("qkbk0"), R("qkbk1")], [R("trope0"), R("trope1")]
        tsqk = [f4(EX, 6656, 256), f4(EX, 7168, 256)]
        Rtsqk = [R("tsqk0"), R("tsqk1")]
        tropek = [self.wsv(7680, 256, F32, EX).rearrange("p (i h c) -> p i h c", i=4, h=4),
                  self.wsv(8192, 256, F32, EX).rearrange("p (i h c) -> p i h c", i=4, h=4)]
        Rtropek = [R("tropek0"), R("tropek1")]
        s4q = [SVs[:, 8 + 4 * p:12 + 4 * p] for p in range(2)]
        s4k = [SVs[:, 16 + 4 * p:20 + 4 * p] for p in range(2)]
        s1 = [SVs[:, 24 + p:25 + p] for p in range(2)]
        Rs4q, Rs4k, Rs1 = [R("s4q0"), R("s4q1")], [R("s4k0"), R("s4k1")], [R("s10"), R("s11")]

        def B_mm(u):
            t, hh = u // 2, u % 2
            bq = u % 2
            bk = 2 + u % 2
            for j in range(2):
                K.op("pe", lambda e, j=j: e.matmul(self.ps[bq][:, 0:384], lhsT=qnkT[:, j, t * 128:(t + 1) * 128],
                                                  rhs=wuq[:, j, hh * 384:(hh + 1) * 384], start=(j == 0), stop=(j == 1)),
                     reads=[RqnkT, Rwu], writes=[self.Rps[bq]])
            K.op("pe", lambda e: e.matmul(self.ps[bk][:], lhsT=qnkT[:, 2, t * 128:(t + 1) * 128],
                                          rhs=wukv[:, hh * 512:(hh + 1) * 512], start=True, stop=True),
                 reads=[RqnkT, Rwu], writes=[self.Rps[bk]])

        def B_el(u):
            t, hh = u // 2, u % 2
            p = u % 2
            bq = u % 2
            bk = 2 + u % 2
            tp = t % 2
            if hh == 0:
                K.op("act", lambda e: e.activation(out=self.junk[:, 0:32], in_=kpe[:, t, :], func=AF.Square, accum_out=s1[tp]), reads=[Rkpe], writes=[Rs1[tp]])
            psq = self.ps[bq][:, 0:384].rearrange("p (h c) -> p h c", h=4)
            pskv = self.ps[bk][:].rearrange("p (h c) -> p h c", h=4)
            K.op("act", lambda e: e.activation(out=tsq[p], in_=psq, func=AF.Square), reads=[self.Rps[bq]], writes=[Rtsq[p]])
            K.op("dve", lambda e: e.tensor_reduce(out=s4q[p], in_=tsq[p], axis=AX.X, op=ALU.add), reads=[Rtsq[p]], writes=[Rs4q[p]])
            self.rstd_from_ss(s4q[p], 96, Rs4q[p])
            K.op("dve", lambda e: e.tensor_tensor(out=tq[p], in0=psq, in1=s4q[p].unsqueeze(2).to_broadcast([128, 4, 96]), op=ALU.mult),
                 reads=[self.Rps[bq], Rs4q[p]], writes=[Rtq[p]])
            K.op("dve", lambda e: e.tensor_tensor(out=tq[p], in0=tq[p], in1=gqh.unsqueeze(1).to_broadcast([128, 4, 96]), op=ALU.mult),
                 reads=[Rtq[p], Rg], writes=[Rtq[p]])
            self.rope(tq[p][:, :, 64:96], t, 4, trope[p], Rtq[p], Rtrope[p])
            K.op("act", lambda e: e.activation(out=qkbq[p], in_=tq[p], func=AF.Copy), reads=[Rtq[p]], writes=[Rqkbq[p]])

        def B_elk(u):
            t, hh = u // 2, u % 2
            p = u % 2
            bk = 2 + u % 2
            tp = t % 2
            pskv = self.ps[bk][:].rearrange("p (h c) -> p h c", h=4)
            K.op("act", lambda e: e.activation(out=v1[:, t, hh * 4:hh * 4 + 4, 0:64], in_=pskv[:, :, 64:128], func=AF.Copy),
                 reads=[self.Rps[bk]], writes=[Rv1])
            K.op("act", lambda e: e.activation(out=tsqk[p], in_=pskv[:, :, 0:64], func=AF.Square), reads=[self.Rps[bk]], writes=[Rtsqk[p]])
            K.op("dve", lambda e: e.tensor_reduce(out=s4k[p], in_=tsqk[p], axis=AX.X, op=ALU.add), reads=[Rtsqk[p]], writes=[Rs4k[p]])
            K.op("dve", lambda e: e.tensor_scalar(out=s4k[p], in0=s4k[p], scalar1=s1[tp], scalar2=None, op0=ALU.add), reads=[Rs4k[p], Rs1[tp]], writes=[Rs4k[p]])
            self.rstd_from_ss(s4k[p], 96, Rs4k[p])
            K.op("dve", lambda e: e.tensor_tensor(out=tk[p][:, :, 0:64], in0=pskv[:, :, 0:64], in1=s4k[p].unsqueeze(2).to_broadcast([128, 4, 64]), op=ALU.mult),
                 reads=[self.Rps[bk], Rs4k[p]], writes=[Rtk[p]])
            K.op("dve", lambda e: e.tensor_tensor(out=tk[p][:, :, 64:96], in0=kpe[:, t, :].unsqueeze(1).to_broadcast([128, 4, 32]),
                                                  in1=s4k[p].unsqueeze(2).to_broadcast([128, 4, 32]), op=ALU.mult),
                 reads=[Rkpe, Rs4k[p]], writes=[Rtk[p]])
            K.op("dve", lambda e: e.tensor_tensor(out=tk[p], in0=tk[p], in1=gkh.unsqueeze(1).to_broadcast([128, 4, 96]), op=ALU.mult),
                 reads=[Rtk[p], Rg], writes=[Rtk[p]])
            self.rope(tk[p][:, :, 64:96], t, 4, tropek[p], Rtk[p], Rtropek[p])
            K.op("act", lambda e: e.activation(out=qkbk[p], in_=tk[p], func=AF.Copy), reads=[Rtk[p]], writes=[Rqkbk[p]])

        def B_tr(u):
            t, hh = u // 2, u % 2
            p = u % 2
            for (src, Rsrc, bnk, dstT, RdstT) in ((qkbq[p], Rqkbq[p], 4 + u % 2, qT, RqT), (qkbk[p], Rqkbk[p], 6 + u % 2, kT, RkT)):
                pb = self.ps[bnk][:].bitcast(BF16).rearrange("p (c k) -> p c k", c=8)
                for h in range(4):
                    K.op("pe", lambda e, h=h: e.transpose(out=pb[0:96, h, :], in_=src[:, h, :], identity=self.ident[:]),
                         reads=[Rsrc, self.Rconst], writes=[self.Rps[bnk]])
                K.op("act", lambda e: e.activation(out=dstT[0:96, hh * 4:hh * 4 + 4, t * 128:(t + 1) * 128], in_=pb[0:96, 0:4, :], func=AF.Copy),
                     reads=[self.Rps[bnk]], writes=[RdstT])

        NU = 2 * NT
        for v in range(NT + 1):
            if v < NT:
                B_mm(2 * v)
                B_mm(2 * v + 1)
            if v >= 1:
                B_tr(2 * v - 2)
                B_tr(2 * v - 1)
            if v < NT:
                K.play_interleaved([K.record(lambda: B_el(2 * v)), K.record(lambda: B_elk(2 * v)),
                                    K.record(lambda: B_el(2 * v + 1)), K.record(lambda: B_elk(2 * v + 1))])
        K.barrier()
        rinv = self.ss[:, 16:20]
        its = [(h, qb, kt) for h in range(8) for qb in range(4) for kt in range(NT)]

        def emit_s(i):
            h, qb, kt = its[i]
            sb_ = i % 4
            K.op("pe", lambda e: e.matmul(self.ps[sb_][:], lhsT=kT[0:96, h, kt * 128:(kt + 1) * 128],
                                          rhs=qT[0:96, h, qb * 512:(qb + 1) * 512], start=True, stop=True),
                 reads=[RkT, RqT], writes=[self.Rps[sb_]])
            K.op("act", lambda e: e.activation(out=ET[sb_], in_=self.ps[sb_][:], func=AF.Exp),
                 reads=[self.Rps[sb_]], writes=[RET[sb_]])

        osT = [self.wsv(35200 + a * 1024, 512, F32) for a in range(1)]
        RosT = [R("osT0")]

        def emit_pv(i):
            h, qb, kt = its[i]
            eb = i % 4
            ob_ = 4 + (i // NT) % 2
            K.op("pe", lambda e: e.matmul(self.ps[ob_][0:65, :], lhsT=v1[:, kt, h, :], rhs=ET[eb], start=(kt == 0), stop=(kt == NT - 1)),
                 reads=[RET[eb], Rv1], writes=[self.Rps[ob_]])
            if kt == NT - 1:
                K.op("dve", lambda e: e.tensor_copy(out=osT[0][0:65, :], in_=self.ps[ob_][0:65, :]), reads=[self.Rps[ob_]], writes=[RosT[0]])
                pt = self.ps[6 + (i // NT) % 2][:, 0:260].rearrange("p (q c) -> p q c", q=4)
                Rpt = self.Rps[6 + (i // NT) % 2]
                for qt in range(4):
                    K.op("pe", lambda e, qt=qt: e.transpose(out=pt[:, qt, :], in_=osT[0][0:65, qt * 128:(qt + 1) * 128], identity=self.identf[0:65, 0:65]),
                         reads=[RosT[0], self.Rconst], writes=[Rpt])
                K.op("dve", lambda e: e.reciprocal(out=rinv, in_=pt[:, :, 64]), reads=[Rpt], writes=[Rss])
                K.op("dve", lambda e: e.tensor_tensor(out=osb[:, qb * 4:qb * 4 + 4, h, :], in0=pt[:, :, 0:64],
                                                      in1=rinv.unsqueeze(2).to_broadcast([128, 4, 64]), op=ALU.mult),
                     reads=[Rpt, Rss], writes=[Rosb])

        emit_s(0)
        emit_s(1)
        for i in range(len(its)):
            if i + 2 < len(its):
                emit_s(i + 2)
            emit_pv(i)
        K.barrier()
        wo_p = self.wsv(2048, 16384).rearrange("p (j c) -> p j c", j=16)
        self.Rwo_pref = R("wo")
        wov_p = d["w_out"][l].rearrange("(j p) c -> p j c", p=128)
        for j0 in range(0, 16, 4):
            K.dma("pool", out=wo_p[:, j0:j0 + 4, :], in_=wov_p[:, j0:j0 + 4, :], writes=[self.Rwo_pref])
        self.wo_ready_layer = l
        to = self.wsv(0, 512, F32).rearrange("p (h c) -> p h c", h=8)
        ob = self.wsv(1024, 512)
        Rto, Rob = R("to"), R("ob")
        s8 = self.ss[:, 20:28]
        gout = SV[:, 0:512].rearrange("p (h c) -> p h c", h=8)
        for t in range(NT):
            K.op("act", lambda e, t=t: e.activation(out=to, in_=osb[:, t, :, :], func=AF.Square), reads=[Rosb], writes=[Rto])
            K.op("dve", lambda e: e.tensor_reduce(out=s8, in_=to, axis=AX.X, op=ALU.add), reads=[Rto], writes=[Rss])
            self.rstd_from_ss(s8, 64, Rss)
            K.op("dve", lambda e, t=t: e.tensor_tensor(out=to, in0=osb[:, t, :, :], in1=s8.unsqueeze(2).to_broadcast([128, 8, 64]), op=ALU.mult),
                 reads=[Rosb, Rss], writes=[Rto])
            K.op("dve", lambda e: e.tensor_tensor(out=ob.rearrange("p (h c) -> p h c", h=8), in0=to, in1=gout, op=ALU.mult),
                 reads=[Rto, self.RSV], writes=[Rob])
            bt = t % 2
            pb = self.ps[bt][:].bitcast(BF16).rearrange("p (c k) -> p c k", c=8)
            for c in range(4):
                K.op("pe", lambda e, c=c, pb=pb: e.transpose(out=pb[:, c, :], in_=ob[:, c * 128:(c + 1) * 128], identity=self.ident[:]),
                     reads=[Rob, self.Rconst], writes=[self.Rps[bt]])
            K.op("act", lambda e, t=t, pb=pb: e.activation(out=self.ymT[:, 8:12, t * 128:(t + 1) * 128], in_=pb[:, 0:4, :], func=AF.Copy),
                 reads=[self.Rps[bt]], writes=[self.RymT[8], self.RymT[9], self.RymT[10], self.RymT[11]])

    def ssd_stage(self, l):
        K = self.K
        d = self.d
        hT = self.hT
        EX = self.EX
        SV = self.SV
        RSV = self.RSV
        Winv = d["w_in"][l].rearrange("(k p) c -> p k c", p=128)
        dtb_row = SV[:, 0:32]
        a_row = SV[:, 32:64]
        d_row = SV[:, 64:80]
        scw = SV[:, 96:156].rearrange("p (c k) -> p c k", c=12)
        scb = SV[:, 160:172]
        ssn = SV[:, 176:177]
        K.dma("sp", out=dtb_row, in_=d["ssd_dt_bias"][l].partition_broadcast(128), writes=[RSV])
        K.dma("sp", out=a_row, in_=d["ssd_a_log"][l].partition_broadcast(128), writes=[RSV])
        K.dma("sp", out=d_row, in_=d["ssd_d"][l].partition_broadcast(128), writes=[RSV])
        for ci in range(12):
            K.dma("sp", out=scw[:, ci, :], in_=d["ssd_conv_w"][l][:, ci * 128:(ci + 1) * 128].rearrange("k p -> p k"), writes=[RSV],
                  allow_slow_non_contiguous=True)
        K.dma("sp", out=scb, in_=d["ssd_conv_b"][l].rearrange("(c p) -> p c", p=128), writes=[RSV], allow_slow_non_contiguous=True)
        K.dma("sp", out=self.grow[:], in_=d["ssd_norm"][l].partition_broadcast(128), writes=[self.Rgrow])

        def f3(off, n3=NT):
            return self.wsv(off, n3 * 32, F32, EX).rearrange("p (t c) -> p t c", t=n3)
        dt = f3(0)
        dta = f3(1024)
        Pc = f3(2048)
        Tend = f3(3072, 17)
        eoff = f3(4160)
        cd = f3(5184)
        Wdt = self.wsv(6208, 256, BF16, EX).rearrange("p (k c) -> p k c", k=8)
        Rdt = R("ssd_small")
        RWdt = R("Wdt")
        K.dma("pool", out=Wdt, in_=Winv[:, :, C_DT:C_DT + 32], writes=[RWdt])
        K.op("act", lambda e: e.activation(out=a_row, in_=a_row, func=AF.Exp), reads=[RSV], writes=[RSV])
        K.op("dve", lambda e: e.tensor_scalar(out=a_row, in0=a_row, scalar1=-1.0, scalar2=None, op0=ALU.mult), reads=[RSV], writes=[RSV])
        for t in range(NT):
            b = t % 2
            for k in range(8):
                K.op("pe", lambda e, k=k, b=b, t=t: e.matmul(self.ps[b][:, 0:32], lhsT=hT[:, k, t * 128:(t + 1) * 128], rhs=Wdt[:, k, :],
                                                          start=(k == 0), stop=(k == 7)),
                     reads=[self.RhT[t // 4], RWdt], writes=[self.Rps[b]])
            K.op("dve", lambda e, b=b, t=t: e.tensor_tensor(out=dt[:, t, :], in0=self.ps[b][:, 0:32], in1=dtb_row, op=ALU.add),
                 reads=[self.Rps[b], RSV], writes=[Rdt])
        K.op("act", lambda e: e.activation(out=dt, in_=dt, func=AF.Exp), reads=[Rdt], writes=[Rdt])
        K.op("act", lambda e: e.activation(out=dt, in_=dt, func=AF.Ln, bias=1.0, scale=1.0), reads=[Rdt], writes=[Rdt])
        K.op("dve", lambda e: e.tensor_tensor(out=dta, in0=dt, in1=a_row.unsqueeze(1).to_broadcast([128, NT, 32]), op=ALU.mult),
             reads=[Rdt, RSV], writes=[Rdt])
        K.op("dve", lambda e: e.memset(Tend[:, 0, :], 0.0), writes=[Rdt])
        for t in range(NT):
            ba = 2 + (t % 2) * 2
            bb = ba + 1
            K.op("pe", lambda e, ba=ba, t=t: e.matmul(self.ps[ba][:, 0:32], lhsT=self.tri[:], rhs=dta[:, t, :], start=True, stop=True),
                 reads=[Rdt, self.Rconst], writes=[self.Rps[ba]])
            K.op("pe", lambda e, bb=bb, t=t: e.matmul(self.ps[bb][:, 0:32], lhsT=self.onesf[:], rhs=dta[:, t, :], start=True, stop=True),
                 reads=[Rdt, self.Rconst], writes=[self.Rps[bb]])
            K.op("dve", lambda e, ba=ba, t=t: e.tensor_tensor(out=Pc[:, t, :], in0=self.ps[ba][:, 0:32], in1=Tend[:, t, :], op=ALU.add),
                 reads=[self.Rps[ba], Rdt], writes=[Rdt])
            K.op("dve", lambda e, bb=bb, t=t: e.tensor_tensor(out=Tend[:, t + 1, :], in0=self.ps[bb][:, 0:32], in1=Tend[:, t, :], op=ALU.add),
                 reads=[self.Rps[bb], Rdt], writes=[Rdt])
        K.op("dve", lambda e: e.tensor_tensor(out=eoff[:, :, 0:16], in0=Pc[:, :, 0:16], in1=Tend[:, 0:NT, 0:16], op=ALU.subtract), reads=[Rdt], writes=[Rdt])
        K.op("dve", lambda e: e.tensor_tensor(out=Pc[:, :, 16:32], in0=Pc[:, :, 16:32], in1=dta[:, :, 16:32], op=ALU.subtract), reads=[Rdt], writes=[Rdt])
        K.op("dve", lambda e: e.tensor_tensor(out=eoff[:, :, 16:32], in0=Tend[:, 1:NT + 1, 16:32], in1=Pc[:, :, 16:32], op=ALU.subtract), reads=[Rdt], writes=[Rdt])
        K.op("dve", lambda e: e.tensor_tensor(out=cd, in0=Tend[:, 1:NT + 1, :], in1=Tend[:, 0:NT, :], op=ALU.subtract), reads=[Rdt], writes=[Rdt])
        K.op("act", lambda e: e.activation(out=eoff, in_=eoff, func=AF.Exp), reads=[Rdt], writes=[Rdt])
        K.op("act", lambda e: e.activation(out=cd, in_=cd, func=AF.Exp), reads=[Rdt], writes=[Rdt])
        nb = dta
        K.op("dve", lambda e: e.tensor_scalar(out=nb[:, :, 0:16], in0=Pc[:, :, 0:16], scalar1=-1.0, scalar2=None, op0=ALU.mult), reads=[Rdt], writes=[Rdt])
        K.op("dve", lambda e: e.tensor_scalar(out=nb[:, :, 16:32], in0=Pc[:, :, 16:32], scalar1=-1.0, scalar2=None, op0=ALU.mult), reads=[Rdt], writes=[Rdt])
        K.barrier()

        xs_g = self.wsv(0, 8192).rearrange("p (t c) -> p t c", t=NT)
        BT = self.wsv(8192, 2048)
        CT = self.wsv(10240, 2048)
        Btok = self.wsv(12288, 2048).rearrange("p (t c) -> p t c", t=NT)
        Wz = self.wsv(14336, 4096).rearrange("p (k c) -> p k c", k=8)
        ybr = self.X[:, 8:16, :].rearrange("p a c -> p (a c)").bitcast(BF16)
        pres = [self.wsv(18432, 2052), self.wsv(0, 2052, BF16, ybr)]
        dgs = [self.wsv(22536 + a * 640, 640).rearrange("p (k c) -> p k c", k=5) for a in range(2)]
        xsTs = [self.wsv(26632, 2048), self.wsv(8200, 2048, BF16, ybr)]
        wch = [self.wsv(28680 + a * 1024, 1024).rearrange("p (k c) -> p k c", k=8) for a in range(2)]
        MTall = self.wsv(18432, 16384).rearrange("p (t h c) -> p t h c", t=NT, h=8)
        Dbuf = [self.wsv(6464, 1024, F32, EX).rearrange("p (h c) -> p h c", h=8),
                self.wsv(34816, 1024, F32).rearrange("p (h c) -> p h c", h=8)]
        CBm = self.wsv(10560, 128, F32, EX)
        CBm2 = self.wsv(8512, 128, F32, EX)
        ecol = self.wsv(8768, 128, F32, EX).rearrange("p (t h) -> p t h", t=NT)
        xd = [self.wsv(9536, 512, BF16, EX), self.wsv(13376, 512, BF16, EX)]
        xdw = [self.wsv(10048, 512, BF16, EX), self.wsv(9024, 512, BF16, EX)]
        t1s = [self.wsv(18432, 512, F32), self.wsv(19456, 512, F32)]
        szs = [self.wsv(20480, 512, F32), self.wsv(21504, 512, F32)]
        yns = [self.wsv(22528, 512), self.wsv(23040, 512)]
        prev = self.wsv(10816, 512, F32, EX)
        prevb = self.wsv(11840, 512, BF16, EX)
        ysum = self.wsv(12352, 512, F32, EX)
        ybwd = self.X[:, 8:16, :].rearrange("p a (b c) -> p (a b) c", b=2)
        Rxs, RBT, RCT, RBtok, RWz = R("xs_g"), R("BT"), R("CT"), R("Btok"), R("Wz")
        Rpres, Rdgs, RxsTs = [R("pre0"), R("pre1")], [R("dg0"), R("dg1")], [R("xsT0"), R("xsT1")]
        Rwch = [R("wch0"), R("wch1")]
        Rprev, Rprevb, Rysum = (R("prev"), R("prevb"), R("ysum"))
        Rxd = [R("xd0"), R("xd1")]
        Rxdw = [R("xdw0"), R("xdw1")]
        Rybwd = [R(f"ybwd{t}") for t in range(NT)]
        Rt1s = [R("t1s0"), R("t1s1")]
        Rszs = [R("szs0"), R("szs1")]
        Ryns = [R("yns0"), R("yns1")]
        RD = [R("Dbuf0"), R("Dbuf1")]
        RMT = [R(f"MT{t}") for t in range(NT)]
        RMT2 = [R(f"MTb{t}") for t in range(NT)]
        Recol = R("ecol")
        RCBm = [R("CBm0"), R("CBm1")]
        CBms = [CBm, CBm2]
        v8 = lambda ap: ap.rearrange("p (h c) -> p h c", h=8)

        for g in range(2):
            K.dma("pool", out=Wz, in_=Winv[:, :, C_Z + g * 512:C_Z + (g + 1) * 512], writes=[RWz])
            for pp in range(2):
                K.op("pool", lambda e, pp=pp: e.memset(pres[pp][:, 0:2], 0.0), writes=[Rpres[pp]])
                K.op("pool", lambda e, pp=pp: e.memset(pres[pp][:, 2050:2052], 0.0), writes=[Rpres[pp]])
            chunks = [g * 4 + i for i in range(4)] + [8 + g, 10 + g]

            def cdst(n_):
                sl = n_ % 2
                if n_ < 4:
                    return xsTs[sl], RxsTs[sl]
                elif n_ == 4:
                    return BT, RBT
                return CT, RCT

            def stA(n_):
                ci = chunks[n_]
                sl = n_ % 2
                pre, Rpre = pres[sl], Rpres[sl]
                K.dma("pool", out=wch[sl], in_=Winv[:, :, C_X + ci * 128:C_X + (ci + 1) * 128], writes=[Rwch[sl]])
                for tb in range(4):
                    b = tb % 2
                    for k in range(8):
                        K.op("pe", lambda e, k=k: e.matmul(self.ps[b][:], lhsT=wch[sl][:, k, :], rhs=hT[:, k, tb * 512:(tb + 1) * 512],
                                                          start=(k == 0), stop=(k == 7)),
                             reads=[Rwch[sl], self.RhT[tb]], writes=[self.Rps[b]])
                    K.op("act", lambda e: e.activation(out=pre[:, 2 + tb * 512:2 + (tb + 1) * 512], in_=self.ps[b][:], func=AF.Copy),
                         reads=[self.Rps[b]], writes=[Rpre])

            def stB(n_):
                ci = chunks[n_]
                sl = n_ % 2
                pre, Rpre, dg, Rdg = pres[sl], Rpres[sl], dgs[sl], Rdgs[sl]
                for kk in range(5):
                    K.op("dve", lambda e, kk=kk: e.tensor_scalar(out=dg[:, kk, :], in0=self.ident[:], scalar1=scw[:, ci, kk:kk + 1], scalar2=None, op0=ALU.mult),
                         reads=[self.Rconst, RSV], writes=[Rdg])
                dst, Rdst = cdst(n_)
                for tb in range(4):
                    bc = 4 + tb % 2
                    for kk in range(5):
                        K.op("pe", lambda e, kk=kk: e.matmul(self.ps[bc][:], lhsT=dg[:, kk, :], rhs=pre[:, tb * 512 + kk:tb * 512 + kk + 512],
                                                            start=(kk == 0), stop=(kk == 4)),
                             reads=[Rdg, Rpre], writes=[self.Rps[bc]])
                    K.op("act", lambda e: e.activation(out=dst[:, tb * 512:(tb + 1) * 512], in_=self.ps[bc][:], func=AF.Silu, bias=scb[:, ci:ci + 1]),
                         reads=[self.Rps[bc], RSV], writes=[Rdst])

            def stC(n_):
                if n_ > 4:
                    return
                dst, Rdst = cdst(n_)
                for rnd in range(2):
                    bt = 2 + rnd
                    pb = self.ps[bt][:].bitcast(BF16).rearrange("p (c k) -> p c k", c=8)
                    for i in range(8):
                        tt = rnd * 8 + i
                        K.op("pe", lambda e, i=i, tt=tt: e.transpose(out=pb[:, i, :], in_=dst[:, tt * 128:(tt + 1) * 128], identity=self.ident[:]),
                             reads=[Rdst, self.Rconst], writes=[self.Rps[bt]])
                    if n_ < 4:
                        K.op("act", lambda e: e.activation(out=xs_g[:, rnd * 8:(rnd + 1) * 8, n_ * 128:(n_ + 1) * 128], in_=pb, func=AF.Copy),
                             reads=[self.Rps[bt]], writes=[Rxs])
                    else:
                        K.op("act", lambda e: e.activation(out=Btok[:, rnd * 8:(rnd + 1) * 8, :], in_=pb, func=AF.Copy),
                             reads=[self.Rps[bt]], writes=[RBtok])

            for n_ in range(len(chunks) + 2):
                if n_ < len(chunks):
                    stA(n_)
                if 0 <= n_ - 1 < len(chunks):
                    stB(n_ - 1)
                if 0 <= n_ - 2 < len(chunks):
                    stC(n_ - 2)
            K.barrier()

            for dirn in (1, 0):
                cb = dirn * 16 + g * 8
                mask = self.triT if dirn == 1 else self.tri
                order = list(range(NT - 1, -1, -1)) if dirn == 1 else list(range(NT))
                col = 0 if dirn == 1 else 127
                def mt_emit(i, order=order, col=col):
                    t = order[i]
                    a = i % 2
                    K.op("dve", lambda e: e.tensor_tensor(out=MTall[:, t, 0:5, :], in0=Dbuf[a][:, 0:5, :], in1=CBms[a].unsqueeze(1).to_broadcast([128, 5, 128]), op=ALU.mult),
                         reads=[RD[a], RCBm[a]], writes=[RMT[t]])
                    K.op("pool", lambda e: e.tensor_tensor(out=MTall[:, t, 5:8, :], in0=Dbuf[a][:, 5:8, :], in1=CBms[a].unsqueeze(1).to_broadcast([128, 3, 128]), op=ALU.mult),
                         reads=[RD[a], RCBm[a]], writes=[RMT2[t]])
                    K.op("pool", lambda e: e.tensor_copy(out=ecol[:, t, :], in_=Dbuf[a][:, :, col]), reads=[RD[a]], writes=[Recol])

                for i, t in enumerate(order):
                    a = i % 2
                    tok = slice(t * 128, (t + 1) * 128)
                    K.op("pe", lambda e, tok=tok: e.matmul(self.ps[0][:, 0:128], lhsT=BT[:, tok], rhs=CT[:, tok], start=True, stop=True),
                         reads=[RBT, RCT], writes=[self.Rps[0]])
                    K.op("dve", lambda e, a=a: e.tensor_tensor(out=CBms[a], in0=self.ps[0][:, 0:128], in1=mask[:], op=ALU.mult),
                         reads=[self.Rps[0], self.Rconst], writes=[RCBm[a]])
                    for h in range(8):
                        bnk = 1 + a * 2 + h // 4
                        dstp = self.ps[bnk][:].rearrange("p (h c) -> p h c", h=4)[:, h % 4, :]
                        K.op("pe", lambda e, h=h, t=t, dstp=dstp: e.matmul(dstp, lhsT=Pc[:, t, cb + h:cb + h + 1].to_broadcast([128, 128]), rhs=self.identf[:],
                                                                         start=True, stop=True),
                             reads=[Rdt, self.Rconst], writes=[self.Rps[bnk]])
                    for h in range(8):
                        bnk = 1 + a * 2 + h // 4
                        src = self.ps[bnk][:].rearrange("p (h c) -> p h c", h=4)[:, h % 4, :]
                        if h in (5, 6, 7):
                            K.op("act", lambda e, h=h, src=src, t=t, a=a: e.activation(out=Dbuf[a][:, h, :], in_=src, func=AF.Abs,
                                                                                     scale=1.0, bias=nb[:, t, cb + h:cb + h + 1]),
                                 reads=[self.Rps[bnk], Rdt], writes=[RD[a]])
                        else:
                            K.op("dve", lambda e, h=h, src=src, t=t, a=a: e.tensor_scalar(out=Dbuf[a][:, h, :], in0=src, scalar1=Pc[:, t, cb + h:cb + h + 1],
                                                                                        scalar2=0.0, op0=ALU.subtract, op1=(ALU.max if dirn == 1 else ALU.min)),
                                 reads=[self.Rps[bnk], Rdt], writes=[RD[a]])
                    K.op("act", lambda e, a=a: e.activation(out=Dbuf[a][:, 0:5, :], in_=Dbuf[a][:, 0:5, :], func=AF.Exp, scale=(-1.0 if dirn == 1 else 1.0)),
                         reads=[RD[a]], writes=[RD[a]])
                    K.op("act", lambda e, a=a: e.activation(out=Dbuf[a][:, 5:8, :], in_=Dbuf[a][:, 5:8, :], func=AF.Exp, scale=-1.0), reads=[RD[a]], writes=[RD[a]])
                    if i >= 1:
                        mt_emit(i - 1)
                mt_emit(NT - 1)

                K.op("dve", lambda e: e.memset(prev, 0.0), writes=[Rprev])
                K.op("dve", lambda e: e.memset(prevb, 0.0), writes=[Rprevb])

                def front2(i, dirn=dirn, cb=cb, order=order):
                    t = order[i]
                    a = i % 2
                    K.op("pool", lambda e: e.tensor_tensor(out=v8(xd[a]), in0=v8(xs_g[:, t, :]),
                                                           in1=dt[:, t, cb:cb + 8].unsqueeze(2).to_broadcast([128, 8, 64]), op=ALU.mult),
                         reads=[Rxs, Rdt], writes=[Rxd[a]])
                    K.op("pool", lambda e: e.tensor_tensor(out=v8(xdw[a]), in0=v8(xd[a]), in1=ecol[:, t, :].unsqueeze(2).to_broadcast([128, 8, 64]), op=ALU.mult),
                         reads=[Rxd[a], Recol], writes=[Rxdw[a]])
                    K.op("pe", lambda e: e.matmul(self.ps[5 + a][:], lhsT=Btok[:, t, :], rhs=xdw[a], start=True, stop=True),
                         reads=[RBtok, Rxdw[a]], writes=[self.Rps[5 + a]])

                def back(i, dirn=dirn, cb=cb, order=order):
                    t = order[i]
                    a = i % 2
                    tok = slice(t * 128, (t + 1) * 128)
                    K.op("pe", lambda e: e.matmul(self.ps[0][:], lhsT=CT[:, tok], rhs=prevb, start=True, stop=True),
                         reads=[RCT, Rprevb], writes=[self.Rps[0]])
                    for h in range(8):
                        K.op("pe", lambda e, h=h: e.matmul(self.ps[7][:, h * 64:(h + 1) * 64], lhsT=MTall[:, t, h, :], rhs=xd[a][:, h * 64:(h + 1) * 64], start=True, stop=True),
                             reads=[RMT[t] if h < 5 else RMT2[t], Rxd[a]], writes=[self.Rps[7]])
                    K.op("dve", lambda e: e.tensor_tensor(out=v8(prev), in0=v8(prev), in1=cd[:, t, cb:cb + 8].unsqueeze(2).to_broadcast([128, 8, 64]), op=ALU.mult),
                         reads=[Rprev, Rdt], writes=[Rprev])
                    K.op("dve", lambda e: e.tensor_tensor(out=prev, in0=prev, in1=self.ps[5 + a][:], op=ALU.add), reads=[Rprev, self.Rps[5 + a]], writes=[Rprev])
                    K.op("act", lambda e: e.activation(out=prevb, in_=prev, func=AF.Copy), reads=[Rprev], writes=[Rprevb])
                    K.op("dve", lambda e: e.tensor_tensor(out=v8(ysum), in0=v8(self.ps[0][:]),
                                                          in1=eoff[:, t, cb:cb + 8].unsqueeze(2).to_broadcast([128, 8, 64]), op=ALU.mult),
                         reads=[self.Rps[0], Rdt], writes=[Rysum])
                    if dirn == 1:
                        K.op("dve", lambda e: e.tensor_tensor(out=ybwd[:, t, :], in0=ysum, in1=self.ps[7][:], op=ALU.add),
                             reads=[Rysum, self.Rps[7]], writes=[Rybwd[t]])
                    else:
                        K.op("dve", lambda e: e.tensor_tensor(out=ysum, in0=ysum, in1=self.ps[7][:], op=ALU.add),
                             reads=[Rysum, self.Rps[7]], writes=[Rysum])
                        K.op("dve", lambda e: e.tensor_tensor(out=ybwd[:, t, :], in0=ybwd[:, t, :], in1=ysum, op=ALU.add),
                             reads=[Rysum, Rybwd[t]], writes=[Rybwd[t]])

                front2(0)
                for i in range(NT):
                    if i + 1 < NT:
                        front2(i + 1)
                    back(i)
            K.barrier()
            for t in range(NT):
                a = t % 2
                tok = slice(t * 128, (t + 1) * 128)
                K.op("pool", lambda e, t=t, a=a: e.tensor_tensor(out=v8(t1s[a]), in0=v8(xs_g[:, t, :]),
                                                                in1=d_row[:, g * 8:(g + 1) * 8].unsqueeze(2).to_broadcast([128, 8, 64]), op=ALU.mult),
                     reads=[Rxs, RSV], writes=[Rt1s[a]])
                K.op("pool", lambda e, t=t, a=a: e.tensor_tensor(out=ybwd[:, t, :], in0=ybwd[:, t, :], in1=t1s[a], op=ALU.add),
                     reads=[Rt1s[a], Rybwd[t]], writes=[Rybwd[t]])
                bz = 6 + a
                for k in range(8):
                    K.op("pe", lambda e, k=k, tok=tok, bz=bz: e.matmul(self.ps[bz][:], lhsT=hT[:, k, tok], rhs=Wz[:, k, :], start=(k == 0), stop=(k == 7)),
                         reads=[self.RhT[t // 4], RWz], writes=[self.Rps[bz]])
                K.op("act", lambda e, a=a, bz=bz: e.activation(out=szs[a], in_=self.ps[bz][:], func=AF.Silu), reads=[self.Rps[bz]], writes=[Rszs[a]])
                K.op("dve", lambda e, t=t, a=a: e.tensor_tensor(out=ybwd[:, t, :], in0=ybwd[:, t, :], in1=szs[a], op=ALU.mult),
                     reads=[Rszs[a], Rybwd[t]], writes=[Rybwd[t]])
                K.op("act", lambda e, t=t, a=a: e.activation(out=t1s[a], in_=ybwd[:, t, :], func=AF.Square, accum_out=self.ss[:, t:t + 1]),
                     reads=[Rybwd[t]], writes=[Rt1s[a], self.Rss])
            self.rstd_from_ss(self.ss[:, 0:NT], 512, self.Rss)
            for t in range(NT):
                a = t % 2
                tok = slice(t * 128, (t + 1) * 128)
                K.op("dve", lambda e, t=t, a=a: e.scalar_tensor_tensor(out=yns[a], in0=ybwd[:, t, :], scalar=self.ss[:, t:t + 1], in1=self.grow[:, g * 512:(g + 1) * 512],
                                                                      op0=ALU.mult, op1=ALU.mult), reads=[Rybwd[t], self.Rss, self.Rgrow], writes=[Ryns[a]])
                bt = 4 + a
                pb = self.ps[bt][:].bitcast(BF16).rearrange("p (c k) -> p c k", c=8)
                for c in range(4):
                    K.op("pe", lambda e, c=c, a=a, pb=pb: e.transpose(out=pb[:, c, :], in_=yns[a][:, c * 128:(c + 1) * 128], identity=self.ident[:]),
                         reads=[Ryns[a], self.Rconst], writes=[self.Rps[bt]])
                K.op("act", lambda e, pb=pb, tok=tok: e.activation(out=self.ymT[:, g * 4:g * 4 + 4, tok], in_=pb[:, 0:4, :], func=AF.Copy),
                     reads=[self.Rps[bt]], writes=[self.RymT[g * 4 + c] for c in range(4)])
            K.barrier()


_INPUT_ORDER = ["x", "positions", "ffn1_norm", "ffn1_w_gate", "ffn1_w_up", "ffn1_w_down", "mix_norm", "w_in",
                "ssd_conv_w", "ssd_conv_b", "ssd_dt_bias", "ssd_a_log", "ssd_d", "ssd_norm",
                "mla_q_norm", "mla_w_uq", "mla_kv_norm", "mla_w_ukv", "mla_q_head_norm", "mla_k_head_norm",
                "mla_out_norm", "conv_w", "conv_out_norm", "w_out",
                "ffn2_norm", "ffn2_w_gate", "ffn2_w_up", "ffn2_w_down"]


def make_in_maps(inputs, cores):
    maps = []
    for b in cores:
        m = {}
        for k in _INPUT_ORDER:
            a = np.asarray(inputs[k])
            if k == "x":
                m[k] = np.ascontiguousarray(a[b])
            elif k == "positions":
                m[k] = np.ascontiguousarray(a[b].reshape(S, 1).astype(np.int32))
            elif k in ("ssd_dt_bias", "ssd_a_log"):
                m[k] = np.ascontiguousarray(a.reshape(DEPTH, 32))
            else:
                m[k] = np.ascontiguousarray(a)
        maps.append(m)
    return maps


def kernel(**inputs):
    prog = Prog()
    nc = prog.build()
    in_maps = make_in_maps(inputs, list(range(8)))
    res = run_bass_kernel_spmd(nc, in_maps, core_ids=list(range(8)))
    return np.stack([np.asarray(r["out"]).reshape(S, D) for r in res.results], axis=0).astype(np.float32)
```
